# Optimizing a Trainium2 kernel written in Bass

```python
import math
import jax, jax.numpy as jnp
from jax import lax
import numpy as np

D_MODEL = 1024
BATCH = 4
SEQ = 8192
DEPTH = 2

D_RNN = D_MODEL
RNN_HEADS = 8
RNN_BLOCK = D_RNN // RNN_HEADS
RNN_CONV = 4
RG_C = 8.0
ATT_HEADS = 8
HEAD_DIM = 128
ATT_WIDTH = ATT_HEADS * HEAD_DIM
Q_BLOCK = 128
D_FF = 3 * D_MODEL
FFN_CONV = 3
EPS = 1e-6
IN_SPLITS = (D_RNN, D_RNN, ATT_WIDTH, ATT_WIDTH, ATT_WIDTH, D_MODEL, D_MODEL)
D_IN = sum(IN_SPLITS)

kernel_name = "hybrid_rglru_stickbreak_adaln_convffn"


def rmsnorm(x, g):
    x32 = x.astype(jnp.float32)
    y = x32 * lax.rsqrt(jnp.mean(x32 * x32, axis=-1, keepdims=True) + EPS)
    return (y * g.astype(jnp.float32)).astype(x.dtype)


def causal_dwconv(x, w, b):
    k_w = w.shape[0]
    s = x.shape[1]
    xp = jnp.pad(x, ((0, 0), (k_w - 1, 0), (0, 0)))
    out = b + w[0] * xp[:, 0:s]
    for k in range(1, k_w):
        out = out + w[k] * xp[:, k:k + s]
    return out


def rg_lru(x, wa, ba, wx, bx, lam):
    b_, s_, c_ = x.shape
    xh = x.reshape(b_, s_, RNN_HEADS, RNN_BLOCK)
    r = jax.nn.sigmoid(jnp.einsum('bshi,hij->bshj', xh, wa).reshape(b_, s_, c_) + ba)
    i = jax.nn.sigmoid(jnp.einsum('bshi,hij->bshj', xh, wx).reshape(b_, s_, c_) + bx)
    log_a = -RG_C * r.astype(jnp.float32) * jax.nn.softplus(-lam.astype(jnp.float32))
    a = jnp.exp(log_a)
    mult = jnp.sqrt(-jnp.expm1(2.0 * log_a))
    u = mult * (i * x).astype(jnp.float32)

    def step(h, inp):
        a_t, u_t = inp
        h = a_t * h + u_t
        return h, h

    h0 = jnp.zeros((b_, c_), jnp.float32)
    _, hs = lax.scan(step, h0, (jnp.swapaxes(a, 0, 1), jnp.swapaxes(u, 0, 1)))
    return jnp.swapaxes(hs, 0, 1).astype(x.dtype)


def head_rmsnorm(x, g):
    return rmsnorm(x, g)


def stick_breaking_attention(q, k, v):
    b_, s_, h_, d_ = q.shape
    nb = s_ // Q_BLOCK
    scale = 1.0 / math.sqrt(d_)
    kh = jnp.transpose(k, (0, 2, 1, 3))
    vh = jnp.transpose(v, (0, 2, 1, 3))
    qb = jnp.transpose(q.reshape(b_, nb, Q_BLOCK, h_, d_), (1, 0, 3, 2, 4))
    key_pos = jnp.arange(s_)

    def one_block(args):
        q_blk, blk = args
        q_pos = blk * Q_BLOCK + jnp.arange(Q_BLOCK)
        mask = key_pos[None, :] < q_pos[:, None]
        z = jnp.einsum('bhqd,bhkd->bhqk', q_blk, kh).astype(jnp.float32) * scale
        log_beta = jax.nn.log_sigmoid(z)
        log_1mb = jnp.where(mask, log_beta - z, 0.0)
        suffix = lax.cumsum(log_1mb, axis=3, reverse=True) - log_1mb
        attn = jnp.where(mask, jnp.exp(log_beta + suffix), 0.0)
        return jnp.einsum('bhqk,bhkd->bhqd', attn.astype(v.dtype), vh)

    out = lax.map(one_block, (qb, jnp.arange(nb)))
    return jnp.transpose(out, (1, 0, 3, 2, 4)).reshape(b_, s_, h_ * d_)


def setup_inputs(seed: int = 0) -> dict:
    key = jax.random.key(seed)
    ks = jax.random.split(key, 24)
    n = jax.random.normal
    L, D = DEPTH, D_MODEL
    u = jax.random.uniform(ks[10], (L, D_RNN), minval=0.9, maxval=0.999)
    a0 = u ** (1.0 / RG_C)
    lam = jnp.log(a0) - jnp.log1p(-a0)
    return {
        "x": n(ks[0], (BATCH, SEQ, D), jnp.float32),
        "c": n(ks[1], (BATCH, D), jnp.float32),
        "ada_w": n(ks[2], (L, D, 6 * D), jnp.float32) * (0.5 * D ** -0.5),
        "ada_b": n(ks[3], (L, 6 * D), jnp.float32) * 0.01,
        "norm1_g": 1.0 + 0.05 * n(ks[4], (L, D), jnp.float32),
        "w_in": n(ks[5], (L, D, D_IN), jnp.float32) * D ** -0.5,
        "conv_w": n(ks[6], (L, RNN_CONV, D_RNN), jnp.float32) * RNN_CONV ** -0.5,
        "conv_b": n(ks[7], (L, D_RNN), jnp.float32) * 0.01,
        "rg_wa": n(ks[8], (L, RNN_HEADS, RNN_BLOCK, RNN_BLOCK), jnp.float32) * RNN_BLOCK ** -0.5,
        "rg_ba": n(ks[9], (L, D_RNN), jnp.float32) * 0.01,
        "rg_wx": n(ks[11], (L, RNN_HEADS, RNN_BLOCK, RNN_BLOCK), jnp.float32) * RNN_BLOCK ** -0.5,
        "rg_bx": n(ks[12], (L, D_RNN), jnp.float32) * 0.01,
        "rg_lambda": lam.astype(jnp.float32),
        "q_norm_g": 1.0 + 0.05 * n(ks[13], (L, HEAD_DIM), jnp.float32),
        "k_norm_g": 1.0 + 0.05 * n(ks[14], (L, HEAD_DIM), jnp.float32),
        "w_out": n(ks[15], (L, D, D), jnp.float32) * D ** -0.5,
        "norm2_g": 1.0 + 0.05 * n(ks[16], (L, D), jnp.float32),
        "ffn_up": n(ks[17], (L, D, 2 * D_FF), jnp.float32) * D ** -0.5,
        "ffn_conv_w": n(ks[18], (L, FFN_CONV, 2 * D_FF), jnp.float32) * FFN_CONV ** -0.5,
        "ffn_conv_b": n(ks[19], (L, 2 * D_FF), jnp.float32) * 0.01,
        "ffn_down": n(ks[20], (L, D_FF, D), jnp.float32) * D_FF ** -0.5,
    }


def reference(x, c, ada_w, ada_b, norm1_g, w_in, conv_w, conv_b, rg_wa, rg_ba, rg_wx, rg_bx,
              rg_lambda, q_norm_g, k_norm_g, w_out, norm2_g, ffn_up, ffn_conv_w, ffn_conv_b,
              ffn_down):
    b_, s_, d_ = x.shape
    cuts = np.cumsum(IN_SPLITS)[:-1].tolist()
    for l in range(DEPTH):
        mod = c @ ada_w[l] + ada_b[l]
        sh1, sc1, gt1, sh2, sc2, gt2 = [m[:, None, :] for m in jnp.split(mod, 6, axis=-1)]

        h = rmsnorm(x, norm1_g[l]) * (1.0 + sc1) + sh1
        p = h @ w_in[l]
        xr, yr, q, k, v, ga, gb = jnp.split(p, cuts, axis=-1)

        xr = causal_dwconv(xr, conv_w[l], conv_b[l])
        y_a = rg_lru(xr, rg_wa[l], rg_ba[l], rg_wx[l], rg_bx[l], rg_lambda[l]) * jax.nn.gelu(yr)

        q = head_rmsnorm(q.reshape(b_, s_, ATT_HEADS, HEAD_DIM), q_norm_g[l])
        k = head_rmsnorm(k.reshape(b_, s_, ATT_HEADS, HEAD_DIM), k_norm_g[l])
        v = v.reshape(b_, s_, ATT_HEADS, HEAD_DIM)
        y_b = stick_breaking_attention(q, k, v)

        mix = jax.nn.sigmoid(ga) * y_a + jax.nn.sigmoid(gb) * y_b
        x = x + gt1 * (mix @ w_out[l])

        h2 = rmsnorm(x, norm2_g[l]) * (1.0 + sc2) + sh2
        up = causal_dwconv(h2 @ ffn_up[l], ffn_conv_w[l], ffn_conv_b[l])
        g_ff, v_ff = jnp.split(up, 2, axis=-1)
        x = x + gt2 * ((jax.nn.gelu(g_ff) * v_ff) @ ffn_down[l])
    return x
```

```python
import contextlib
import re
import numpy as np
import concourse.bass as bass
import concourse.mybir as mybir
from concourse.bass_utils import run_bass_kernel_spmd

F32 = mybir.dt.float32
BF16 = mybir.dt.bfloat16
AF = mybir.ActivationFunctionType
ALU = mybir.AluOpType

D = 1024
NH = 8
EPS = 1e-6
DFF = 3072


class _Eng:
    def __init__(self, name, sem, is_pe=False):
        self.name = name
        self.sem = sem
        self.count = 0
        self.waited = {}
        self.prog = []
        self.is_pe = is_pe
        self.is_slot = False
        self.pending = []


class _Slot:
    def __init__(self, name, sem):
        self.name = name
        self.sem = sem
        self.count = 0
        self.is_slot = True


class Buf:
    __slots__ = ("last_w", "readers", "name")

    def __init__(self, name=""):
        self.last_w = None
        self.readers = {}
        self.name = name


class TT:
    def __init__(self, t, name):
        self.t = t
        self.buf = Buf(name)

    def __getitem__(self, key):
        return self.t[key]


class K:
    def __init__(self, nc, stack):
        self.nc = nc
        self.stack = stack
        self.engs = {}
        for name in ("pe", "act", "dve", "pool", "sp"):
            sem = stack.enter_context(nc.semaphore("sem_" + name))
            self.engs[name] = _Eng(name, sem, is_pe=(name == "pe"))
        self.slots = []
        self._slot_cache = {}
        self.dbufs = {}
        self.n_inst = 0

    def slot(self, name):
        key = re.sub(r"^([mf])\d", r"\1", name)
        if key in self._slot_cache:
            return self._slot_cache[key]
        sem = self.stack.enter_context(self.nc.semaphore("slot_" + key))
        s = _Slot(key, sem)
        self.slots.append(s)
        self._slot_cache[key] = s
        return s

    def sb(self, name, shape, dtype, stack=None):
        st = stack if stack is not None else self.stack
        t = st.enter_context(self.nc.sbuf_tensor(name, list(shape), dtype))
        return TT(t, name)

    def ps(self, name, shape, dtype, stack=None):
        st = stack if stack is not None else self.stack
        t = st.enter_context(self.nc.psum_tensor(name, list(shape), dtype))
        return TT(t, name)

    def dbuf(self, key):
        b = self.dbufs.get(key)
        if b is None:
            b = Buf(str(key))
            self.dbufs[key] = b
        return b

    @staticmethod
    def _b(x):
        return x.buf if isinstance(x, TT) else x

    def _waits(self, E, reads, writes):
        deps = {}

        def add(obj, val):
            if deps.get(obj, 0) < val:
                deps[obj] = val
        for b in reads:
            b = self._b(b)
            if b.last_w is not None:
                add(*b.last_w)
        for b in writes:
            b = self._b(b)
            if b.last_w is not None:
                add(*b.last_w)
            for o, v in b.readers.items():
                add(o, v)
        waits = []
        for obj, val in deps.items():
            if obj.is_slot:
                val = obj.count
            elif obj is E:
                if E.is_pe:
                    continue
                if E.count - val >= 2:
                    continue
            if E.waited.get(obj, 0) >= val:
                continue
            E.waited[obj] = val
            waits.append((obj.sem, val))
        return waits

    def barrier(self):
        for E in self.engs.values():
            for o in list(self.engs.values()) + self.slots:
                if o is E or o.count == 0:
                    continue
                if E.waited.get(o, 0) >= o.count:
                    continue
                E.waited[o] = o.count
                E.pending.append((o.sem, o.count))

    def op(self, eng, fn, r=(), w=()):
        E = self.engs[eng]
        waits = self._waits(E, r, w)
        waits = E.pending + waits
        E.pending = []
        E.count += 1
        idx = E.count
        E.prog.append((waits, fn, (E.sem, 1)))
        for b in r:
            b = self._b(b)
            if b.readers.get(E, 0) < idx:
                b.readers[E] = idx
        for b in w:
            b = self._b(b)
            b.last_w = (E, idx)
            b.readers = {}
        self.n_inst += 1

    def dma(self, q, slot, out, in_, r=(), w=()):
        E = self.engs[q]
        waits = self._waits(E, r, w)
        waits = E.pending + waits
        E.pending = []
        slot.count += 16
        val = slot.count
        E.prog.append((waits, lambda e, out=out, in_=in_: e.dma_start(out=out, in_=in_), (slot.sem, 16)))
        for b in r:
            b = self._b(b)
            b.readers[slot] = val
        for b in w:
            b = self._b(b)
            b.last_w = (slot, val)
            b.readers = {}
        self.n_inst += 1

    def finish(self):
        E = self.engs["sp"]
        final_waits = [(s.sem, s.count) for s in self.slots if s.count > 0]
        for n in ("pe", "act", "dve", "pool"):
            e2 = self.engs[n]
            if e2.count > 0:
                final_waits.append((e2.sem, e2.count))
        progs = {n: e.prog for n, e in self.engs.items()}
        nc = self.nc
        with nc.Block() as block:
            def run(e, prog, extra=()):
                for waits, fn, inc in prog:
                    for sem, val in waits:
                        e.wait_ge(sem, val)
                    ins = fn(e)
                    ins.then_inc(inc[0], inc[1])
                for sem, val in extra:
                    e.wait_ge(sem, val)

            @block.sync
            def _(e):
                run(e, progs["sp"], final_waits)

            @block.tensor
            def _(e):
                run(e, progs["pe"])

            @block.scalar
            def _(e):
                run(e, progs["act"])

            @block.vector
            def _(e):
                run(e, progs["dve"])

            @block.gpsimd
            def _(e):
                run(e, progs["pool"])


def _act(k, out, in_, func, r, w, **kw):
    k.op("act", lambda e: e.activation(out=out, in_=in_, func=func, **kw), r=r, w=w)


def _mm(k, out, lhsT, rhs, start, stop, r, w):
    k.op("pe", lambda e: e.matmul(out, lhsT=lhsT, rhs=rhs, start=start, stop=stop), r=r, w=w)


def _tt(k, eng, out, in0, in1, op, r, w):
    k.op(eng, lambda e: e.tensor_tensor(out=out, in0=in0, in1=in1, op=op), r=r, w=w)


def _ts(k, eng, out, in0, s1, s2, op0, op1, r, w):
    if op1 is None:
        k.op(eng, lambda e: e.tensor_scalar(out=out, in0=in0, scalar1=s1, scalar2=None, op0=op0), r=r, w=w)
    else:
        k.op(eng, lambda e: e.tensor_scalar(out=out, in0=in0, scalar1=s1, scalar2=s2, op0=op0, op1=op1), r=r, w=w)


def _stt(k, out, in0, scalar, in1, op0, op1, r, w):
    k.op("dve", lambda e: e.scalar_tensor_tensor(out=out, in0=in0, scalar=scalar, in1=in1, op0=op0, op1=op1),
         r=r, w=w)


def _copy(k, eng, out, in_, r, w):
    k.op(eng, lambda e: e.tensor_copy(out=out, in_=in_), r=r, w=w)


M_CV, M_BSH, M_BSC, M_G1, M_CW, M_CB, M_BA, M_BX, M_LAM, M_QG, M_KG, M_NP = 0, 8, 16, 24, 32, 64, 72, 80, 88, 96, 97, 98


def emit_mixer(k, nc, S, io, pfx="m", P=None, ncc=4, mix_bf16=False):
    NT = S // 512
    NB = S // 128
    x_d, mix_d = io["x"], io["mixT"]
    if P is None:
        P = [k.ps(f"{pfx}ps{i}", [128, 512], F32) for i in range(8)]
    top = contextlib.ExitStack()
    bank_ctr = [0]

    def bank():
        b = P[bank_ctr[0] % 8]
        bank_ctr[0] += 1
        return b

    hT_d = io["dbg_hT"] if "dbg_hT" in io else nc.dram_tensor(f"{pfx}_hT_d", [128, 8, S], BF16).ap()

    ident = k.sb(f"{pfx}ident", [128, 128], BF16, stack=top)
    tri = k.sb(f"{pfx}tri", [128, 128], BF16, stack=top)
    trin = k.sb(f"{pfx}trin", [128, 128], BF16, stack=top)
    ones = k.sb(f"{pfx}ones", [128, 128], BF16, stack=top)
    masks = k.sb(f"{pfx}masks", [128, 4 * 512], F32, stack=top)
    smp = k.sb(f"{pfx}smp", [128, M_NP], F32, stack=top)
    wa = k.sb(f"{pfx}wa", [128, ncc, 128], BF16, stack=top)
    wx = k.sb(f"{pfx}wx", [128, ncc, 128], BF16, stack=top)
    s_c = k.slot(f"{pfx}const")
    s_c2 = k.slot(f"{pfx}const2")
    k.dma("pool", s_c, ident[:], io["ident"], w=[ident])
    k.dma("pool", s_c, tri[:], io["tri"], w=[tri])
    k.dma("pool", s_c, trin[:], io["trin"], w=[trin])
    k.dma("pool", s_c, ones[:], io["ones"], w=[ones])
    k.dma("pool", s_c, wa[:], io["rgwa"], w=[wa])
    k.dma("pool", s_c, wx[:], io["rgwx"], w=[wx])
    k.dma("sp", s_c2, masks[:], io["masks"], w=[masks])
    k.dma("sp", s_c2, smp[:], io["smallp"], w=[smp])

    modt = k.sb(f"{pfx}modt", [128, 16], F32, stack=top)
    A1 = k.sb(f"{pfx}A1", [128, 8], F32, stack=top)
    negc = k.sb(f"{pfx}negc", [128, 8], F32, stack=top)
    negc2 = k.sb(f"{pfx}negc2", [128, 8], F32, stack=top)
    t4 = k.sb(f"{pfx}t4", [128, 8], F32, stack=top)
    gqs = k.sb(f"{pfx}gqs", [128, 1], F32, stack=top)

    with contextlib.ExitStack() as ps:
        adaw_t = [k.sb(f"{pfx}adaw{i}", [128, 8, 512], F32, stack=ps) for i in range(2)]
        s_aw = [k.slot(f"{pfx}adaw{i}") for i in range(2)]
        pm = bank()
        for piece in range(4):
            at = adaw_t[piece % 2]
            k.dma("sp", s_aw[piece % 2], at[:], io["adaw"][piece], w=[at])
            for jj in range(4):
                j = piece * 4 + jj
                for kc in range(8):
                    _mm(k, pm[:, j:j + 1], at[:, kc, jj * 128:(jj + 1) * 128], smp[:, M_CV + kc:M_CV + kc + 1],
                        kc == 0, kc == 7, r=[at, smp], w=[pm])
        _tt(k, "dve", modt[:], pm[:, 0:16], smp[:, M_BSH:M_BSH + 16], ALU.add, r=[pm, smp], w=[modt])
        _ts(k, "dve", A1[:], modt[:, 8:16], 1.0, None, ALU.add, None, r=[modt], w=[A1])
        _tt(k, "dve", A1[:], A1[:], smp[:, M_G1:M_G1 + 8], ALU.mult, r=[A1, smp], w=[A1])
        _act(k, t4[:], smp[:, M_LAM:M_LAM + 8], AF.Exp, r=[smp], w=[t4], scale=-1.0)
        _act(k, t4[:], t4[:], AF.Ln, r=[t4], w=[t4], bias=1.0)
        _ts(k, "dve", negc[:], t4[:], -8.0, None, ALU.mult, None, r=[t4], w=[negc])
        _ts(k, "dve", negc2[:], t4[:], -16.0, None, ALU.mult, None, r=[t4], w=[negc2])
        _ts(k, "dve", gqs[:], smp[:, M_QG:M_QG + 1], 1.0 / float(np.sqrt(128.0)), None, ALU.mult, None,
            r=[smp], w=[gqs])
        k.barrier()

    with contextlib.ExitStack() as ps:
        xb = [k.sb(f"{pfx}xb{i}", [128, 1024], F32, stack=ps) for i in range(3)]
        s_x = [k.slot(f"{pfx}x{i}") for i in range(3)]
        xh = [k.sb(f"{pfx}xh{i}", [128, 1024], BF16, stack=ps) for i in range(2)]
        junk = k.sb(f"{pfx}junk", [128, 1024], BF16, stack=ps)
        ssq = [k.sb(f"{pfx}ssq{i}", [128, 1], F32, stack=ps) for i in range(2)]
        hTo = [k.sb(f"{pfx}hTo{i}", [128, 8, 512], BF16, stack=ps) for i in range(2)]
        s_ho = [k.slot(f"{pfx}ho{i}") for i in range(2)]
        for tt in range(NT):
            banks = [P[(tt % 2) * 4 + i] for i in range(4)]
            bviews = [b[:].bitcast(BF16) for b in banks]
            for sub in range(4):
                blk = tt * 4 + sub
                xt = xb[blk % 3]
                k.dma("sp", s_x[blk % 3], xt[:], x_d[blk * 128:(blk + 1) * 128, :], w=[xt])
                sq = ssq[blk % 2]
                _act(k, junk[:], xt[:], AF.Square, r=[xt], w=[junk, sq], accum_out=sq[:])
                _act(k, sq[:], sq[:], AF.Sqrt, r=[sq], w=[sq], scale=1.0 / D, bias=EPS)
                k.op("dve", lambda e, sq=sq: e.reciprocal(out=sq[:], in_=sq[:]), r=[sq], w=[sq])
                xht = xh[blk % 2]
                _ts(k, "pool", xht[:], xt[:], sq[:, 0:1], None, ALU.mult, None, r=[xt, sq], w=[xht])
                for fc in range(8):
                    bi = fc // 2
                    c0 = (fc % 2) * 512 + sub * 128
                    k.op("pe", lambda e, o=bviews[bi][:, c0:c0 + 128], i_=xht[:, fc * 128:(fc + 1) * 128]:
                         e.transpose(o, i_, ident[:]), r=[xht, ident], w=[banks[bi]])
            ho = hTo[tt % 2]
            for fc in range(8):
                bi = fc // 2
                c0 = (fc % 2) * 512
                _act(k, ho[:, fc, :], bviews[bi][:, c0:c0 + 512], AF.Identity, r=[banks[bi], A1, modt], w=[ho],
                     scale=A1[:, fc:fc + 1], bias=modt[:, fc:fc + 1])
            k.dma("sp", s_ho[tt % 2], hT_d[:, :, tt * 512:(tt + 1) * 512], ho[:], r=[ho],
                  w=[k.dbuf((pfx + "hT", tt))])
        k.barrier()

    with contextlib.ExitStack() as ps:
        W = k.sb(f"{pfx}W", [128, 8, 896], BF16, stack=ps)
        s_w = k.slot(f"{pfx}w")
        hT = [k.sb(f"{pfx}hT{i}", [128, 8, 512], BF16, stack=ps) for i in range(2)]
        s_h = [k.slot(f"{pfx}h{i}") for i in range(2)]
        qT = k.sb(f"{pfx}qT", [128, S], BF16, stack=ps)
        kT = k.sb(f"{pfx}kT", [128, S], BF16, stack=ps)
        vA = k.sb(f"{pfx}vA", [128, NB * 128], BF16, stack=ps)
        mixA = k.sb(f"{pfx}mixA", [128, S], BF16, stack=ps)
        sgb = k.sb(f"{pfx}sgb", [128, S], BF16, stack=ps)
        xraw = k.sb(f"{pfx}xraw", [128, 515], F32, stack=ps)
        state = k.sb(f"{pfx}state", [128, 1], F32, stack=ps)

        def f32t(n):
            return k.sb(f"{pfx}{n}", [128, 512], F32, stack=ps)

        def bf16t(n):
            return k.sb(f"{pfx}{n}", [128, 512], BF16, stack=ps)
        xc, r_t, ig, a_t, a2, u_t, hs, gl, ya, sga, rq, rk = [f32t(n) for n in
                                                              ("xc", "r", "ig", "a", "a2", "u", "hs", "gl", "ya",
                                                               "sga", "rq", "rk")]
        xcb, sqq, sqk = [bf16t(n) for n in ("xcb", "sqq", "sqk")]
        EB = [f32t(f"e{i}") for i in range(4)]
        LB = [bf16t(f"L{i}") for i in range(4)]
        XB = [f32t(f"X{i}") for i in range(3)]
        ATB = [bf16t(f"at{i}") for i in range(4)]
        tmpo = [f32t(f"tmpo{i}") for i in range(2)]
        mixo = [(bf16t if mix_bf16 else f32t)(f"mixo{i}") for i in range(2)]
        s_mo = [k.slot(f"{pfx}mo{i}") for i in range(2)]
        mo_ctr = [0]

        for cc in range(ncc):
            k.dma("pool", s_w, W[:], io["win"][cc], w=[W])
            k.op("dve", lambda e: e.memset(state[:], 0.0), w=[state])
            k.op("dve", lambda e: e.memset(xraw[:, 0:3], 0.0), w=[xraw])
            for tt in range(NT):
                tsl = slice(tt * 512, (tt + 1) * 512)
                h = hT[tt % 2]
                k.dma("sp", s_h[tt % 2], h[:], hT_d[:, :, tsl], r=[k.dbuf((pfx + "hT", tt))], w=[h])

                def proj(pb, s_idx):
                    for kc in range(8):
                        _mm(k, pb[:], W[:, kc, s_idx * 128:(s_idx + 1) * 128], h[:, kc, :], kc == 0, kc == 7,
                            r=[W, h], w=[pb])
                pb = bank()
                proj(pb, 0)
                _act(k, xraw[:, 3:515], pb[:], AF.Copy, r=[pb], w=[xraw])
                pb = bank()
                proj(pb, 1)
                _act(k, gl[:], pb[:], AF.Gelu_apprx_tanh, r=[pb], w=[gl])
                pbq = bank()
                proj(pbq, 2)
                _act(k, sqq[:], pbq[:], AF.Square, r=[pbq], w=[sqq])
                pbk = bank()
                proj(pbk, 3)
                _act(k, sqk[:], pbk[:], AF.Square, r=[pbk], w=[sqk])
                pb = bank()
                proj(pb, 5)
                _act(k, sga[:], pb[:], AF.Sigmoid, r=[pb], w=[sga])
                pb = bank()
                proj(pb, 6)
                _act(k, sgb[:, tsl], pb[:], AF.Sigmoid, r=[pb], w=[sgb])
                pbv = bank()
                for sub in range(4):
                    for kc in range(8):
                        _mm(k, pbv[:, sub * 128:(sub + 1) * 128], h[:, kc, sub * 128:(sub + 1) * 128],
                            W[:, kc, 4 * 128:5 * 128], kc == 0, kc == 7, r=[W, h], w=[pbv])
                _copy(k, "dve", vA[:, tt * 512:(tt + 1) * 512], pbv[:], r=[pbv], w=[vA])
                cw = M_CW + cc * 4
                _ts(k, "dve", xc[:], xraw[:, 0:512], smp[:, cw:cw + 1], smp[:, M_CB + cc:M_CB + cc + 1],
                    ALU.mult, ALU.add, r=[xraw, smp], w=[xc])
                for tap in range(1, 4):
                    _stt(k, xc[:], xraw[:, tap:tap + 512], smp[:, cw + tap:cw + tap + 1], xc[:], ALU.mult, ALU.add,
                         r=[xraw, xc, smp], w=[xc])
                _copy(k, "pool", xcb[:], xc[:], r=[xc], w=[xcb])
                _copy(k, "pool", xraw[:, 0:3], xraw[:, 512:515], r=[xraw], w=[xraw])
                pbr = bank()
                _mm(k, pbr[:], wa[:, cc, :], xcb[:], True, True, r=[wa, xcb], w=[pbr])
                pbi = bank()
                _mm(k, pbi[:], wx[:, cc, :], xcb[:], True, True, r=[wx, xcb], w=[pbi])
                _act(k, r_t[:], pbr[:], AF.Sigmoid, r=[pbr, smp], w=[r_t], bias=smp[:, M_BA + cc:M_BA + cc + 1])
                _act(k, ig[:], pbi[:], AF.Sigmoid, r=[pbi, smp], w=[ig], bias=smp[:, M_BX + cc:M_BX + cc + 1])
                _act(k, a_t[:], r_t[:], AF.Exp, r=[r_t, negc], w=[a_t], scale=negc[:, cc:cc + 1])
                _act(k, a2[:], r_t[:], AF.Exp, r=[r_t, negc2], w=[a2], scale=negc2[:, cc:cc + 1])
                _act(k, a2[:], a2[:], AF.Sqrt, r=[a2], w=[a2], scale=-1.0, bias=1.0)
                _tt(k, "pool", u_t[:], ig[:], xc[:], ALU.mult, r=[ig, xc], w=[u_t])
                _tt(k, "pool", u_t[:], u_t[:], a2[:], ALU.mult, r=[u_t, a2], w=[u_t])
                k.op("dve", lambda e: e.tensor_tensor_scan(out=hs[:], data0=a_t[:], data1=u_t[:],
                                                           initial=state[:, 0:1], op0=ALU.mult, op1=ALU.add),
                     r=[a_t, u_t, state], w=[hs])
                _copy(k, "dve", state[:], hs[:, 511:512], r=[hs], w=[state])
                _tt(k, "pool", ya[:], hs[:], gl[:], ALU.mult, r=[hs, gl], w=[ya])
                _tt(k, "pool", mixA[:, tsl], ya[:], sga[:], ALU.mult, r=[ya, sga], w=[mixA])
                for (pbx, sqx, rx, gsc, dst) in ((pbq, sqq, rq, gqs[:, 0:1], qT),
                                                 (pbk, sqk, rk, smp[:, M_KG:M_KG + 1], kT)):
                    pbs = bank()
                    _mm(k, pbs[:], ones[:], sqx[:], True, True, r=[ones, sqx], w=[pbs])
                    _act(k, rx[:], pbs[:], AF.Sqrt, r=[pbs], w=[rx], scale=1.0 / 128.0, bias=EPS)
                    k.op("dve", lambda e, rx=rx: e.reciprocal(out=rx[:], in_=rx[:]), r=[rx], w=[rx])
                    _stt(k, dst[:, tsl], pbx[:], gsc, rx[:], ALU.mult, ALU.mult, r=[pbx, rx, gqs, smp], w=[dst])

            steps = []
            for qt in range(NT):
                topkb = 4 * qt + 3
                for kb in range(topkb, -1, -1):
                    steps.append(dict(i=len(steps), qt=qt, kb=kb, first=(kb == topkb), last=(kb == 0),
                                      diag=(kb - 4 * qt) if kb >= 4 * qt else None))
            ZP = [P[0], P[1]]
            ACC = [P[2], P[3]]
            OB = [P[4], P[5], P[6]]

            def stA(s):
                zp = ZP[s["i"] % 2]
                kb, qt = s["kb"], s["qt"]
                _mm(k, zp[:], kT[:, kb * 128:(kb + 1) * 128], qT[:, qt * 512:(qt + 1) * 512], True, True,
                    r=[kT, qT], w=[zp])

            def stB(s):
                zp = ZP[s["i"] % 2]
                e_ = EB[s["i"] % 4]
                L = LB[s["i"] % 4]
                _act(k, e_[:], zp[:], AF.Exp, r=[zp], w=[e_])
                if s["diag"] is not None:
                    dg = s["diag"]
                    _tt(k, "dve", e_[:], e_[:], masks[:, dg * 512:(dg + 1) * 512], ALU.mult, r=[e_, masks], w=[e_])
                _act(k, L[:], e_[:], AF.Ln, r=[e_], w=[L], bias=1.0)

            def stC(s):
                acc = ACC[s["qt"] % 2]
                L = LB[s["i"] % 4]
                _mm(k, acc[:], tri[:], L[:], s["first"], s["last"], r=[tri, L], w=[acc])

            def stD(s):
                acc = ACC[s["qt"] % 2]
                X = XB[s["i"] % 3]
                _act(k, X[:], acc[:], AF.Exp, r=[acc], w=[X], scale=-1.0)

            def stE(s):
                if s["last"]:
                    return
                acc = ACC[s["qt"] % 2]
                L = LB[s["i"] % 4]
                _mm(k, acc[:], trin[:], L[:], False, False, r=[trin, L], w=[acc])

            def stF(s):
                at = ATB[s["i"] % 4]
                _tt(k, "dve", at[:], EB[s["i"] % 4][:], XB[s["i"] % 3][:], ALU.mult,
                    r=[EB[s["i"] % 4], XB[s["i"] % 3]], w=[at])

            def stG(s):
                o = OB[s["qt"] % 3]
                at = ATB[s["i"] % 4]
                kb, qt = s["kb"], s["qt"]
                _mm(k, o[:], vA[:, kb * 128:(kb + 1) * 128], at[:], s["first"], s["last"], r=[vA, at], w=[o])
                if s["last"]:
                    qsl = slice(qt * 512, (qt + 1) * 512)
                    j = mo_ctr[0] % 2
                    mo_ctr[0] += 1
                    _tt(k, "dve", tmpo[j][:], o[:], sgb[:, qsl], ALU.mult, r=[o, sgb], w=[tmpo[j]])
                    _tt(k, "pool", mixo[j][:], tmpo[j][:], mixA[:, qsl], ALU.add, r=[tmpo[j], mixA], w=[mixo[j]])
                    k.dma("sp", s_mo[j], mix_d[cc * 128:(cc + 1) * 128, qsl], mixo[j][:], r=[mixo[j]],
                          w=[k.dbuf((pfx + "mix", cc, qt))])

            if "dbg_q" in io and cc == ncc - 1:
                s_dbg = k.slot(f"{pfx}dbg")
                k.dma("sp", s_dbg, io["dbg_small"][:, 0:16], modt[:], r=[modt])
                k.dma("sp", s_dbg, io["dbg_small"][:, 16:24], A1[:], r=[A1])
                for nm, t in (("dbg_q", qT), ("dbg_k", kT), ("dbg_v", vA), ("dbg_mixA", mixA), ("dbg_sgb", sgb)):
                    k.dma("sp", s_dbg, io[nm], t[:], r=[t])
            n = len(steps)
            for it in range(n + 3):
                if it < n:
                    stA(steps[it])
                    stB(steps[it])
                if 0 <= it - 2 < n:
                    stE(steps[it - 2])
                if 0 <= it - 1 < n:
                    stC(steps[it - 1])
                    stD(steps[it - 1])
                if 0 <= it - 2 < n:
                    stF(steps[it - 2])
                if 0 <= it - 3 < n:
                    stG(steps[it - 3])
        k.barrier()
    top.close()


def build_mixer(S, dbg=False):
    nc = bass.Bass("TRN2", target_bir_lowering=False)
    io = {}

    def inp(name, shape, dt=F32):
        io[name] = nc.dram_tensor(name, list(shape), dt, kind="ExternalInput").ap()
    inp("x", [S, D])
    inp("ident", [128, 128])
    inp("tri", [128, 128])
    inp("trin", [128, 128])
    inp("ones", [128, 128])
    inp("masks", [128, 2048])
    inp("smallp", [128, M_NP])
    inp("rgwa", [128, 4, 128])
    inp("rgwx", [128, 4, 128])
    inp("adaw", [4, 128, 8, 512])
    inp("win", [4, 128, 8, 896])
    io["mixT"] = nc.dram_tensor("mixT", [512, S], F32, kind="ExternalOutput").ap()
    if dbg:
        io["dbg_hT"] = nc.dram_tensor("dbg_hT", [128, 8, S], BF16, kind="ExternalOutput").ap()
        io["dbg_small"] = nc.dram_tensor("dbg_small", [128, 24], F32, kind="ExternalOutput").ap()
        for nm in ("dbg_q", "dbg_k", "dbg_v", "dbg_mixA", "dbg_sgb"):
            io[nm] = nc.dram_tensor(nm, [128, S], BF16, kind="ExternalOutput").ap()
    with contextlib.ExitStack() as st:
        k = K(nc, st)
        emit_mixer(k, nc, S, io)
        k.finish()
    return nc


def _pk(v):
    v = np.asarray(v, np.float32)
    return np.ascontiguousarray(v.reshape(-1, 128).T)


def _consts():
    p = np.arange(128)[:, None]
    c = np.arange(128)[None, :]
    tri = (p >= c).astype(np.float32)
    trin = (p < c).astype(np.float32)
    cc = np.arange(512)[None, :]
    masks = np.concatenate([((128 * i + p) < cc).astype(np.float32) for i in range(4)], axis=1)
    return dict(ident=np.eye(128, dtype=np.float32), tri=tri, trin=trin, ones=np.ones((128, 128), np.float32),
                masks=np.ascontiguousarray(masks))


def mixer_weights(l, b, hh, inp):
    if hh is None:
        ch, h0, ncc = slice(0, 1024), 0, 8
    else:
        ch, h0, ncc = slice(hh * 512, (hh + 1) * 512), hh * 4, 4
    d = {}
    ada_w, ada_b = inp["ada_w"][l], inp["ada_b"][l]
    smallp = np.zeros((128, M_NP), np.float32)
    smallp[:, M_CV:M_CV + 8] = _pk(inp["c"][b])
    smallp[:, M_BSH:M_BSH + 8] = _pk(ada_b[0:1024])
    smallp[:, M_BSC:M_BSC + 8] = _pk(ada_b[1024:2048])
    smallp[:, M_G1:M_G1 + 8] = _pk(inp["norm1_g"][l])
    cw = inp["conv_w"][l][:, ch]
    for cc in range(ncc):
        smallp[:, M_CW + cc * 4:M_CW + cc * 4 + 4] = cw[:, cc * 128:(cc + 1) * 128].T
    smallp[:, M_CB:M_CB + ncc] = _pk(inp["conv_b"][l][ch])
    smallp[:, M_BA:M_BA + ncc] = _pk(inp["rg_ba"][l][ch])
    smallp[:, M_BX:M_BX + ncc] = _pk(inp["rg_bx"][l][ch])
    smallp[:, M_LAM:M_LAM + ncc] = _pk(inp["rg_lambda"][l][ch])
    smallp[:, M_QG] = inp["q_norm_g"][l]
    smallp[:, M_KG] = inp["k_norm_g"][l]
    d["smallp"] = smallp
    d["rgwa"] = np.ascontiguousarray(inp["rg_wa"][l][h0:h0 + ncc].transpose(1, 0, 2))
    d["rgwx"] = np.ascontiguousarray(inp["rg_wx"][l][h0:h0 + ncc].transpose(1, 0, 2))
    aw = ada_w[:, 0:2048].reshape(8, 128, 4, 512).transpose(2, 1, 0, 3)
    d["adaw"] = np.ascontiguousarray(aw)
    w_in = inp["w_in"][l]
    w7 = w_in.reshape(8, 128, 7, 8, 128)[:, :, :, h0:h0 + ncc, :]
    d["win"] = np.ascontiguousarray(w7.transpose(3, 1, 0, 2, 4).reshape(ncc, 128, 8, 896))
    return d


def mixer_inputs(l, b, hh, xb, inp):
    d = dict(_consts())
    d["x"] = np.ascontiguousarray(xb, dtype=np.float32)
    d.update(mixer_weights(l, b, hh, inp))
    return d


_PROGS = {}


def _prog(kind, n):
    key = (kind, n)
    if key not in _PROGS:
        _PROGS[key] = build_mixer(n) if kind == "mixer" else build_ffn(n)
    return _PROGS[key]


FUSED = True


def kernel(**inputs):
    inp = {k_: np.asarray(v, dtype=np.float32) for k_, v in inputs.items()}
    x = inp["x"]
    B, S, _ = x.shape
    depth = inp["w_in"].shape[0]
    cores = list(range(8))
    if FUSED:
        key = ("fused", S, depth)
        if key not in _PROGS:
            _PROGS[key] = build_fused(S, depth)
        per_b = [fused_inputs(b, inp, depth) for b in range(B)]
        res = run_bass_kernel_spmd(_PROGS[key], [per_b[c // 2] for c in cores], core_ids=cores)
        half = S // 2
        return np.stack([np.concatenate([np.asarray(res.results[2 * b]["out"])[:half],
                                         np.asarray(res.results[2 * b + 1]["out"])[half:]], axis=0)
                         for b in range(B)], axis=0).astype(np.float32)
    for l in range(depth):
        nc = _prog("mixer", S)
        in_maps = [mixer_inputs(l, c // 2, c % 2, x[c // 2], inp) for c in cores]
        res = run_bass_kernel_spmd(nc, in_maps, core_ids=cores)
        mixT = [np.concatenate([np.asarray(res.results[2 * b]["mixT"]), np.asarray(res.results[2 * b + 1]["mixT"])],
                               axis=0) for b in range(B)]
        nc = _prog("ffn", S // 2)
        in_maps = [ffn_inputs(l, c // 2, c % 2, x[c // 2], mixT[c // 2], inp) for c in cores]
        res = run_bass_kernel_spmd(nc, in_maps, core_ids=cores)
        x = np.stack([np.concatenate([np.asarray(res.results[2 * b]["xout"]), np.asarray(res.results[2 * b + 1]["xout"])],
                                     axis=0) for b in range(B)], axis=0).astype(np.float32)
    return x


F_CV, F_BSH, F_BSC, F_G2, F_CW, F_CB, F_FLAG, F_NP = 0, 8, 16, 24, 32, 176, 224, 225


def emit_ffn(k, nc, S2, io, pfx="f", P=None, pre=True, mix_bf16=False):
    NT = S2 // 512
    x_d, mix_d, out_d = io["xin"], io["mixin"], io["xout"]
    if P is None:
        P = [k.ps(f"{pfx}ps{i}", [128, 512], F32) for i in range(8)]
    top = contextlib.ExitStack()
    bank_ctr = [0]

    def bank():
        b = P[4 + bank_ctr[0] % 4]
        bank_ctr[0] += 1
        return b

    ident = k.sb(f"{pfx}ident", [128, 128], BF16, stack=top)
    smp = k.sb(f"{pfx}smp", [128, F_NP], F32, stack=top)
    wo = k.sb(f"{pfx}wo", [128, 8, 1024], BF16, stack=top)
    wd = k.sb(f"{pfx}wd", [128, 24, 1024], BF16, stack=top)
    gtB = k.sb(f"{pfx}gtB", [128, 2048], F32, stack=top)
    modt = k.sb(f"{pfx}modt", [128, 16], F32, stack=top)
    A2 = k.sb(f"{pfx}A2", [128, 8], F32, stack=top)
    halo = k.sb(f"{pfx}halo", [128, 48, 2], F32, stack=top)
    s_c = k.slot(f"{pfx}const")
    s_c2 = k.slot(f"{pfx}const2")
    k.dma("sp", s_c2, smp[:], io["smallp"], w=[smp])
    k.dma("pool", s_c, ident[:], io["ident"], w=[ident])
    k.dma("pool", s_c, wo[:], io["wout"], w=[wo])
    for q in range(4):
        k.dma("pool", s_c, wd[:, q * 6:(q + 1) * 6, :], io["fdown"][:, q * 6:(q + 1) * 6, :], w=[wd])

    with contextlib.ExitStack() as ps:
        adaw_t = [k.sb(f"{pfx}adaw{i}", [128, 8, 512], F32, stack=ps) for i in range(2)]
        s_aw = [k.slot(f"{pfx}adaw{i}") for i in range(2)]
        cB = k.sb(f"{pfx}cB", [128, 8, 128], F32, stack=ps)
        onesf = k.sb(f"{pfx}onesf", [128, 128], F32, stack=ps)
        gtb_t = k.sb(f"{pfx}gtb_t", [128, 2048], F32, stack=ps)
        k.dma("sp", s_c2, gtb_t[:], io["gtb"].partition_broadcast(128), w=[gtb_t])
        k.op("dve", lambda e: e.memset(onesf[:], 1.0), w=[onesf])
        for kc in range(8):
            _ts(k, "dve", cB[:, kc, :], onesf[:], smp[:, F_CV + kc:F_CV + kc + 1], None, ALU.mult, None,
                r=[onesf, smp], w=[cB])
        pm = bank()
        for piece in range(4):
            at = adaw_t[piece % 2]
            k.dma("sp", s_aw[piece % 2], at[:], io["adaw"][piece], w=[at])
            for jj in range(4):
                j = piece * 4 + jj
                for kc in range(8):
                    _mm(k, pm[:, j:j + 1], at[:, kc, jj * 128:(jj + 1) * 128], smp[:, F_CV + kc:F_CV + kc + 1],
                        kc == 0, kc == 7, r=[at, smp], w=[pm])
        _tt(k, "dve", modt[:], pm[:, 0:16], smp[:, F_BSH:F_BSH + 16], ALU.add, r=[pm, smp], w=[modt])
        _ts(k, "dve", A2[:], modt[:, 8:16], 1.0, None, ALU.add, None, r=[modt], w=[A2])
        _tt(k, "dve", A2[:], A2[:], smp[:, F_G2:F_G2 + 8], ALU.mult, r=[A2, smp], w=[A2])
        for piece in range(4, 8):
            at = adaw_t[piece % 2]
            k.dma("sp", s_aw[piece % 2], at[:], io["adaw"][piece], w=[at])
            pb = bank()
            for kc in range(8):
                _mm(k, pb[:], cB[:, kc, :], at[:, kc, :], kc == 0, kc == 7, r=[cB, at], w=[pb])
            c0 = (piece - 4) * 512
            _tt(k, "dve", gtB[:, c0:c0 + 512], pb[:], gtb_t[:, c0:c0 + 512], ALU.add, r=[pb, gtb_t], w=[gtB])
        k.barrier()

    with contextlib.ExitStack() as ps:
        mx = [k.sb(f"{pfx}mx{i}", [128, 8, 512], BF16, stack=ps) for i in range(1)]
        s_mx = [k.slot(f"{pfx}mx{i}") for i in range(1)]
        xt = [k.sb(f"{pfx}xt{i}", [128, 1024], F32, stack=ps) for i in range(2)]
        s_xt = [k.slot(f"{pfx}xt{i}") for i in range(2)]
        x1t = [k.sb(f"{pfx}x1t{i}", [128, 1024], F32, stack=ps) for i in range(4)]
        xh = [k.sb(f"{pfx}xh{i}", [128, 1024], BF16, stack=ps) for i in range(2)]
        junk = k.sb(f"{pfx}junk", [128, 1024], BF16, stack=ps)
        ssq = [k.sb(f"{pfx}ssq{i}", [128, 1], F32, stack=ps) for i in range(2)]
        h2T = k.sb(f"{pfx}h2T", [128, 8, 512], BF16, stack=ps)
        Wj = [k.sb(f"{pfx}Wj{i}", [128, 8, 256], BF16, stack=ps) for i in range(3)]
        s_wj = [k.slot(f"{pfx}wj{i}") for i in range(3)]
        raw = [k.sb(f"{pfx}raw{i}", [128, 514], F32, stack=ps) for i in range(4)]
        tcv = [k.sb(f"{pfx}tcv{i}", [128, 512], F32, stack=ps) for i in range(4)]
        gg = [k.sb(f"{pfx}gg{i}", [128, 512], F32, stack=ps) for i in range(2)]
        actT = k.sb(f"{pfx}actT", [128, 24, 512], BF16, stack=ps)
        tmp = [k.sb(f"{pfx}tmp{i}", [128, 512], F32, stack=ps) for i in range(2)]
        xo = [k.sb(f"{pfx}xo{i}", [128, 1024], F32, stack=ps) for i in range(2)]
        s_xo = [k.slot(f"{pfx}xo{i}") for i in range(2)]
        ctr = dict(x=0, w=0, raw=0, t=0, xo=0, g=0, tmp=0, tile=0)
        mixv = mix_d.rearrange("(kc p) t -> p kc t", p=128)

        def do_tile(row0, ntok, pre, orow0):
            nsub = ntok // 128
            ti = ctr["tile"]
            ctr["tile"] += 1
            m = mx[0]
            k.dma("sp" if mix_bf16 else "pool", s_mx[0], m[:, :, 0:ntok], mixv[:, :, row0:row0 + ntok], w=[m])
            tb = [P[i] for i in range(4)]
            tbv = [b[:].bitcast(BF16) for b in tb]
            for sub in range(nsub):
                xi = ctr["x"] % 2
                ctr["x"] += 1
                xx = xt[xi]
                k.dma("sp", s_xt[xi], xx[:], x_d[row0 + sub * 128:row0 + (sub + 1) * 128, :], w=[xx])
                x1 = x1t[sub]
                for half in range(2):
                    hs_ = slice(half * 512, (half + 1) * 512)
                    po = bank()
                    for kc in range(8):
                        _mm(k, po[:], m[:, kc, sub * 128:(sub + 1) * 128], wo[:, kc, hs_], kc == 0, kc == 7,
                            r=[m, wo], w=[po])
                    tp = tmp[ctr["tmp"] % 2]
                    ctr["tmp"] += 1
                    _tt(k, "dve", tp[:], po[:], gtB[:, hs_], ALU.mult, r=[po, gtB], w=[tp])
                    _tt(k, "pool", x1[:, hs_], tp[:], xx[:, hs_], ALU.add, r=[tp, xx], w=[x1])
                sq = ssq[sub % 2]
                _act(k, junk[:], x1[:], AF.Square, r=[x1], w=[junk, sq], accum_out=sq[:])
                _act(k, sq[:], sq[:], AF.Sqrt, r=[sq], w=[sq], scale=1.0 / D, bias=EPS)
                k.op("dve", lambda e, sq=sq: e.reciprocal(out=sq[:], in_=sq[:]), r=[sq], w=[sq])
                xht = xh[sub % 2]
                _ts(k, "pool", xht[:], x1[:], sq[:, 0:1], None, ALU.mult, None, r=[x1, sq], w=[xht])
                for fc in range(8):
                    bi = fc // 2
                    c0 = (fc % 2) * 512 + sub * 128
                    k.op("pe", lambda e, o=tbv[bi][:, c0:c0 + 128], i_=xht[:, fc * 128:(fc + 1) * 128]:
                         e.transpose(o, i_, ident[:]), r=[xht, ident], w=[tb[bi]])
            for fc in range(8):
                bi = fc // 2
                c0 = (fc % 2) * 512
                _act(k, h2T[:, fc, 0:ntok], tbv[bi][:, c0:c0 + ntok], AF.Identity, r=[tb[bi], A2, modt], w=[h2T],
                     scale=A2[:, fc:fc + 1], bias=modt[:, fc:fc + 1])
            for j in range(24):
                wi = ctr["w"] % 3
                ctr["w"] += 1
                wj = Wj[wi]
                k.dma("pool", s_wj[wi], wj[:], io["fup"][j], w=[wj])
                res = []
                for br in range(2):
                    cidx = br * 24 + j
                    pb = bank()
                    for kc in range(8):
                        _mm(k, pb[:, 0:ntok], wj[:, kc, br * 128:(br + 1) * 128], h2T[:, kc, 0:ntok], kc == 0, kc == 7,
                            r=[wj, h2T], w=[pb])
                    if pre:
                        _ts(k, "dve", halo[:, cidx, :], pb[:, ntok - 2:ntok], smp[:, F_FLAG:F_FLAG + 1], None,
                            ALU.mult, None, r=[pb, smp], w=[halo])
                        continue
                    rw = raw[ctr["raw"] % 4]
                    ctr["raw"] += 1
                    _act(k, rw[:, 2:2 + ntok], pb[:, 0:ntok], AF.Copy, r=[pb], w=[rw])
                    _copy(k, "pool", rw[:, 0:2], halo[:, cidx, :], r=[halo], w=[rw])
                    tc = tcv[ctr["t"] % 4]
                    ctr["t"] += 1
                    cw = F_CW + cidx * 3
                    _ts(k, "dve", tc[:], rw[:, 0:512], smp[:, cw:cw + 1], smp[:, F_CB + cidx:F_CB + cidx + 1],
                        ALU.mult, ALU.add, r=[rw, smp], w=[tc])
                    for tap in (1, 2):
                        _stt(k, tc[:], rw[:, tap:tap + 512], smp[:, cw + tap:cw + tap + 1], tc[:], ALU.mult, ALU.add,
                             r=[rw, tc, smp], w=[tc])
                    _copy(k, "pool", halo[:, cidx, :], rw[:, 512:514], r=[rw], w=[halo])
                    res.append(tc)
                if pre:
                    continue
                g_ = gg[ctr["g"] % 2]
                ctr["g"] += 1
                _act(k, g_[:], res[0][:], AF.Gelu_apprx_tanh, r=[res[0]], w=[g_])
                _tt(k, "pool", actT[:, j, :], g_[:], res[1][:], ALU.mult, r=[g_, res[1]], w=[actT])
            if pre:
                return
            for sub in range(nsub):
                oi = ctr["xo"] % 2
                ctr["xo"] += 1
                xo_ = xo[oi]
                for half in range(2):
                    hs_ = slice(half * 512, (half + 1) * 512)
                    pd = bank()
                    for j in range(24):
                        _mm(k, pd[:], actT[:, j, sub * 128:(sub + 1) * 128], wd[:, j, hs_], j == 0, j == 23,
                            r=[actT, wd], w=[pd])
                    tp = tmp[ctr["tmp"] % 2]
                    ctr["tmp"] += 1
                    _tt(k, "dve", tp[:], pd[:], gtB[:, 1024 + half * 512:1024 + (half + 1) * 512], ALU.mult,
                        r=[pd, gtB], w=[tp])
                    _tt(k, "pool", xo_[:, hs_], tp[:], x1t[sub][:, hs_], ALU.add, r=[tp, x1t[sub]], w=[xo_])
                r0 = orow0 + sub * 128
                k.dma("sp", s_xo[oi], out_d[r0:r0 + 128, :], xo_[:], r=[xo_], w=[k.dbuf((pfx + "out", r0))])

        if pre:
            do_tile(0, 128, True, 0)
        else:
            k.op("dve", lambda e: e.memset(halo[:], 0.0), w=[halo])
        for tt in range(NT):
            do_tile((128 if pre else 0) + tt * 512, 512, False, tt * 512)
        k.barrier()
    top.close()


def build_ffn(S2):
    nc = bass.Bass("TRN2", target_bir_lowering=False)
    io = {}

    def inp(name, shape, dt=F32):
        io[name] = nc.dram_tensor(name, list(shape), dt, kind="ExternalInput").ap()
    inp("xin", [128 + S2, D])
    inp("mixin", [D, 128 + S2])
    inp("ident", [128, 128])
    inp("smallp", [128, F_NP])
    inp("gtb", [2048])
    inp("adaw", [8, 128, 8, 512])
    inp("wout", [128, 8, 1024])
    inp("fup", [24, 128, 8, 256])
    inp("fdown", [128, 24, 1024])
    io["xout"] = nc.dram_tensor("xout", [S2, D], F32, kind="ExternalOutput").ap()
    with contextlib.ExitStack() as st:
        k = K(nc, st)
        emit_ffn(k, nc, S2, io)
        k.finish()
    return nc


def ffn_weights(l, b, flag, inp):
    d = {}
    ada_w, ada_b = inp["ada_w"][l], inp["ada_b"][l]
    smallp = np.zeros((128, F_NP), np.float32)
    smallp[:, F_CV:F_CV + 8] = _pk(inp["c"][b])
    smallp[:, F_BSH:F_BSH + 8] = _pk(ada_b[3072:4096])
    smallp[:, F_BSC:F_BSC + 8] = _pk(ada_b[4096:5120])
    smallp[:, F_G2:F_G2 + 8] = _pk(inp["norm2_g"][l])
    cw = inp["ffn_conv_w"][l]
    smallp[:, F_CW:F_CW + 144] = cw.reshape(3, 48, 128).transpose(2, 1, 0).reshape(128, 144)
    smallp[:, F_CB:F_CB + 48] = _pk(inp["ffn_conv_b"][l])
    smallp[:, F_FLAG] = flag
    d["smallp"] = smallp
    d["gtb"] = np.ascontiguousarray(np.concatenate([ada_b[2048:3072], ada_b[5120:6144]]))
    cols = np.concatenate([np.arange(3072, 5120), np.arange(2048, 3072), np.arange(5120, 6144)])
    aw = ada_w[:, cols].reshape(8, 128, 8, 512).transpose(2, 1, 0, 3)
    d["adaw"] = np.ascontiguousarray(aw)
    d["wout"] = np.ascontiguousarray(inp["w_out"][l].reshape(8, 128, 1024).transpose(1, 0, 2))
    fu = inp["ffn_up"][l].reshape(8, 128, 2, 24, 128)
    d["fup"] = np.ascontiguousarray(fu.transpose(3, 1, 0, 2, 4).reshape(24, 128, 8, 256))
    d["fdown"] = np.ascontiguousarray(inp["ffn_down"][l].reshape(24, 128, 1024).transpose(1, 0, 2))
    return d


def ffn_inputs(l, b, th, xb, mixTb, inp):
    S = xb.shape[0]
    S2 = S // 2
    t0 = th * S2
    p0 = t0 - 128 if th > 0 else 0
    d = dict(ident=np.eye(128, dtype=np.float32))
    d["xin"] = np.ascontiguousarray(np.concatenate([xb[p0:p0 + 128], xb[t0:t0 + S2]], axis=0), dtype=np.float32)
    d["mixin"] = np.ascontiguousarray(np.concatenate([mixTb[:, p0:p0 + 128], mixTb[:, t0:t0 + S2]], axis=1),
                                      dtype=np.float32)
    d.update(ffn_weights(l, b, 1.0 if th > 0 else 0.0, inp))
    return d


_MW = ("smallp", "rgwa", "rgwx", "adaw", "win")
_FW = ("smallp", "gtb", "adaw", "wout", "fup", "fdown")


def build_fused(S, depth=2, skip=()):
    nc = bass.Bass("TRN2", target_bir_lowering=False)
    ext = {}

    def inp(name, shape, dt=F32):
        ext[name] = nc.dram_tensor(name, list(shape), dt, kind="ExternalInput").ap()
    inp("x", [S, D])
    for n_ in ("ident", "tri", "trin", "ones"):
        inp(n_, [128, 128])
    inp("masks", [128, 2048])
    for l in range(depth):
        inp(f"m{l}_smallp", [128, M_NP])
        inp(f"m{l}_rgwa", [128, 8, 128])
        inp(f"m{l}_rgwx", [128, 8, 128])
        inp(f"m{l}_adaw", [4, 128, 8, 512])
        inp(f"m{l}_win", [8, 128, 8, 896])
        inp(f"f{l}_smallp", [128, F_NP])
        inp(f"f{l}_gtb", [2048])
        inp(f"f{l}_adaw", [8, 128, 8, 512])
        inp(f"f{l}_wout", [128, 8, 1024])
        inp(f"f{l}_fup", [24, 128, 8, 256])
        inp(f"f{l}_fdown", [128, 24, 1024])
    out = nc.dram_tensor("out", [S, D], F32, kind="ExternalOutput").ap()
    mixT_d = nc.dram_tensor("mixT_d", [D, S], BF16).ap()
    xmid = [nc.dram_tensor(f"xmid{l}", [S, D], F32).ap() for l in range(depth - 1)]
    with contextlib.ExitStack() as st:
        k = K(nc, st)
        P = [k.ps(f"ps{i}", [128, 512], F32) for i in range(8)]
        for l in range(depth):
            x_in = ext["x"] if l == 0 else xmid[l - 1]
            x_out = out if l == depth - 1 else xmid[l]
            io = dict(x=x_in, mixT=mixT_d, ident=ext["ident"], tri=ext["tri"], trin=ext["trin"], ones=ext["ones"],
                      masks=ext["masks"])
            for n_ in _MW:
                io[n_] = ext[f"m{l}_{n_}"]
            if f"m{l}" not in skip:
                emit_mixer(k, nc, S, io, pfx=f"m{l}", P=P, ncc=8, mix_bf16=True)
            io = dict(xin=x_in, mixin=mixT_d, xout=x_out, ident=ext["ident"])
            for n_ in _FW:
                io[n_] = ext[f"f{l}_{n_}"]
            if f"f{l}" not in skip:
                emit_ffn(k, nc, S, io, pfx=f"f{l}", P=P, pre=False, mix_bf16=True)
        k.finish()
    return nc


def fused_inputs(b, inp, depth):
    d = dict(_consts())
    d["x"] = np.ascontiguousarray(inp["x"][b], dtype=np.float32)
    for l in range(depth):
        for n_, v in mixer_weights(l, b, None, inp).items():
            d[f"m{l}_{n_}"] = v
        for n_, v in ffn_weights(l, b, 0.0, inp).items():
            d[f"f{l}_{n_}"] = v
    return d
```

```python
import contextlib
import re
import numpy as np
import concourse.bass as bass
import concourse.mybir as mybir
from concourse.bass_utils import run_bass_kernel_spmd

F32 = mybir.dt.float32
BF16 = mybir.dt.bfloat16
AF = mybir.ActivationFunctionType
ALU = mybir.AluOpType

D = 1024
NH = 8
EPS = 1e-6
DFF = 3072


class _Eng:
    def __init__(self, name, sem, is_pe=False):
        self.name = name
        self.sem = sem
        self.count = 0
        self.waited = {}
        self.prog = []
        self.is_pe = is_pe
        self.is_slot = False
        self.pending = []


class _Slot:
    def __init__(self, name, sem):
        self.name = name
        self.sem = sem
        self.count = 0
        self.is_slot = True


class Buf:
    __slots__ = ("last_w", "readers", "name")

    def __init__(self, name=""):
        self.last_w = None
        self.readers = {}
        self.name = name


class TT:
    def __init__(self, t, name):
        self.t = t
        self.buf = Buf(name)

    def __getitem__(self, key):
        return self.t[key]


class K:
    def __init__(self, nc, stack):
        self.nc = nc
        self.stack = stack
        self.engs = {}
        for name in ("pe", "act", "dve", "pool", "sp"):
            sem = stack.enter_context(nc.semaphore("sem_" + name))
            self.engs[name] = _Eng(name, sem, is_pe=(name == "pe"))
        self.slots = []
        self._slot_cache = {}
        self.dbufs = {}
        self.n_inst = 0

    def slot(self, name):
        key = re.sub(r"^([mf])\d", r"\1", name)
        if key in self._slot_cache:
            return self._slot_cache[key]
        sem = self.stack.enter_context(self.nc.semaphore("slot_" + key))
        s = _Slot(key, sem)
        self.slots.append(s)
        self._slot_cache[key] = s
        return s

    def sb(self, name, shape, dtype, stack=None):
        st = stack if stack is not None else self.stack
        t = st.enter_context(self.nc.sbuf_tensor(name, list(shape), dtype))
        return TT(t, name)

    def ps(self, name, shape, dtype, stack=None):
        st = stack if stack is not None else self.stack
        t = st.enter_context(self.nc.psum_tensor(name, list(shape), dtype))
        return TT(t, name)

    def dbuf(self, key):
        b = self.dbufs.get(key)
        if b is None:
            b = Buf(str(key))
            self.dbufs[key] = b
        return b

    @staticmethod
    def _b(x):
        return x.buf if isinstance(x, TT) else x

    def _waits(self, E, reads, writes):
        deps = {}

        def add(obj, val):
            if deps.get(obj, 0) < val:
                deps[obj] = val
        for b in reads:
            b = self._b(b)
            if b.last_w is not None:
                add(*b.last_w)
        for b in writes:
            b = self._b(b)
            if b.last_w is not None:
                add(*b.last_w)
            for o, v in b.readers.items():
                add(o, v)
        waits = []
        for obj, val in deps.items():
            if obj.is_slot:
                val = obj.count
            elif obj is E:
                if E.is_pe:
                    continue
                if E.count - val >= 2:
                    continue
            if E.waited.get(obj, 0) >= val:
                continue
            E.waited[obj] = val
            waits.append((obj.sem, val))
        return waits

    def barrier(self):
        for E in self.engs.values():
            for o in list(self.engs.values()) + self.slots:
                if o is E or o.count == 0:
                    continue
                if E.waited.get(o, 0) >= o.count:
                    continue
                E.waited[o] = o.count
                E.pending.append((o.sem, o.count))

    def op(self, eng, fn, r=(), w=()):
        E = self.engs[eng]
        waits = self._waits(E, r, w)
        waits = E.pending + waits
        E.pending = []
        E.count += 1
        idx = E.count
        E.prog.append((waits, fn, (E.sem, 1)))
        for b in r:
            b = self._b(b)
            if b.readers.get(E, 0) < idx:
                b.readers[E] = idx
        for b in w:
            b = self._b(b)
            b.last_w = (E, idx)
            b.readers = {}
        self.n_inst += 1

    def dma(self, q, slot, out, in_, r=(), w=()):
        E = self.engs[q]
        waits = self._waits(E, r, w)
        waits = E.pending + waits
        E.pending = []
        slot.count += 16
        val = slot.count
        E.prog.append((waits, lambda e, out=out, in_=in_: e.dma_start(out=out, in_=in_), (slot.sem, 16)))
        for b in r:
            b = self._b(b)
            b.readers[slot] = val
        for b in w:
            b = self._b(b)
            b.last_w = (slot, val)
            b.readers = {}
        self.n_inst += 1

    def finish(self):
        E = self.engs["sp"]
        final_waits = [(s.sem, s.count) for s in self.slots if s.count > 0]
        for n in ("pe", "act", "dve", "pool"):
            e2 = self.engs[n]
            if e2.count > 0:
                final_waits.append((e2.sem, e2.count))
        progs = {n: e.prog for n, e in self.engs.items()}
        nc = self.nc
        with nc.Block() as block:
            def run(e, prog, extra=()):
                for waits, fn, inc in prog:
                    for sem, val in waits:
                        e.wait_ge(sem, val)
                    ins = fn(e)
                    ins.then_inc(inc[0], inc[1])
                for sem, val in extra:
                    e.wait_ge(sem, val)

            @block.sync
            def _(e):
                run(e, progs["sp"], final_waits)

            @block.tensor
            def _(e):
                run(e, progs["pe"])

            @block.scalar
            def _(e):
                run(e, progs["act"])

            @block.vector
            def _(e):
                run(e, progs["dve"])

            @block.gpsimd
            def _(e):
                run(e, progs["pool"])


def _act(k, out, in_, func, r, w, **kw):
    k.op("act", lambda e: e.activation(out=out, in_=in_, func=func, **kw), r=r, w=w)


def _mm(k, out, lhsT, rhs, start, stop, r, w):
    k.op("pe", lambda e: e.matmul(out, lhsT=lhsT, rhs=rhs, start=start, stop=stop), r=r, w=w)


def _tt(k, eng, out, in0, in1, op, r, w):
    k.op(eng, lambda e: e.tensor_tensor(out=out, in0=in0, in1=in1, op=op), r=r, w=w)


def _ts(k, eng, out, in0, s1, s2, op0, op1, r, w):
    if op1 is None:
        k.op(eng, lambda e: e.tensor_scalar(out=out, in0=in0, scalar1=s1, scalar2=None, op0=op0), r=r, w=w)
    else:
        k.op(eng, lambda e: e.tensor_scalar(out=out, in0=in0, scalar1=s1, scalar2=s2, op0=op0, op1=op1), r=r, w=w)


def _stt(k, out, in0, scalar, in1, op0, op1, r, w):
    k.op("dve", lambda e: e.scalar_tensor_tensor(out=out, in0=in0, scalar=scalar, in1=in1, op0=op0, op1=op1),
         r=r, w=w)


def _copy(k, eng, out, in_, r, w):
    k.op(eng, lambda e: e.tensor_copy(out=out, in_=in_), r=r, w=w)


M_CV, M_BSH, M_BSC, M_G1, M_CW, M_CB, M_BA, M_BX, M_LAM, M_QG, M_KG, M_NP = 0, 8, 16, 24, 32, 64, 72, 80, 88, 96, 97, 98


def emit_mixer(k, nc, S, io, pfx="m", P=None, ncc=4, mix_bf16=False):
    NT = S // 512
    NB = S // 128
    x_d, mix_d = io["x"], io["mixT"]
    if P is None:
        P = [k.ps(f"{pfx}ps{i}", [128, 512], F32) for i in range(8)]
    top = contextlib.ExitStack()
    bank_ctr = [0]

    def bank():
        b = P[bank_ctr[0] % 8]
        bank_ctr[0] += 1
        return b

    hT_d = io["dbg_hT"] if "dbg_hT" in io else nc.dram_tensor(f"{pfx}_hT_d", [128, 8, S], BF16).ap()

    ident = k.sb(f"{pfx}ident", [128, 128], BF16, stack=top)
    tri = k.sb(f"{pfx}tri", [128, 128], BF16, stack=top)
    trin = k.sb(f"{pfx}trin", [128, 128], BF16, stack=top)
    ones = k.sb(f"{pfx}ones", [128, 128], BF16, stack=top)
    masks = k.sb(f"{pfx}masks", [128, 4 * 512], F32, stack=top)
    smp = k.sb(f"{pfx}smp", [128, M_NP], F32, stack=top)
    wa = k.sb(f"{pfx}wa", [128, ncc, 128], BF16, stack=top)
    wx = k.sb(f"{pfx}wx", [128, ncc, 128], BF16, stack=top)
    s_c = k.slot(f"{pfx}const")
    s_c2 = k.slot(f"{pfx}const2")
    k.dma("pool", s_c, ident[:], io["ident"], w=[ident])
    k.dma("pool", s_c, tri[:], io["tri"], w=[tri])
    k.dma("pool", s_c, trin[:], io["trin"], w=[trin])
    k.dma("pool", s_c, ones[:], io["ones"], w=[ones])
    k.dma("pool", s_c, wa[:], io["rgwa"], w=[wa])
    k.dma("pool", s_c, wx[:], io["rgwx"], w=[wx])
    k.dma("sp", s_c2, masks[:], io["masks"], w=[masks])
    k.dma("sp", s_c2, smp[:], io["smallp"], w=[smp])

    modt = k.sb(f"{pfx}modt", [128, 16], F32, stack=top)
    A1 = k.sb(f"{pfx}A1", [128, 8], F32, stack=top)
    negc = k.sb(f"{pfx}negc", [128, 8], F32, stack=top)
    negc2 = k.sb(f"{pfx}negc2", [128, 8], F32, stack=top)
    t4 = k.sb(f"{pfx}t4", [128, 8], F32, stack=top)
    gqs = k.sb(f"{pfx}gqs", [128, 1], F32, stack=top)

    with contextlib.ExitStack() as ps:
        adaw_t = [k.sb(f"{pfx}adaw{i}", [128, 8, 512], F32, stack=ps) for i in range(2)]
        s_aw = [k.slot(f"{pfx}adaw{i}") for i in range(2)]
        pm = bank()
        for piece in range(4):
            at = adaw_t[piece % 2]
            k.dma("sp", s_aw[piece % 2], at[:], io["adaw"][piece], w=[at])
            for jj in range(4):
                j = piece * 4 + jj
                for kc in range(8):
                    _mm(k, pm[:, j:j + 1], at[:, kc, jj * 128:(jj + 1) * 128], smp[:, M_CV + kc:M_CV + kc + 1],
                        kc == 0, kc == 7, r=[at, smp], w=[pm])
        _tt(k, "dve", modt[:], pm[:, 0:16], smp[:, M_BSH:M_BSH + 16], ALU.add, r=[pm, smp], w=[modt])
        _ts(k, "dve", A1[:], modt[:, 8:16], 1.0, None, ALU.add, None, r=[modt], w=[A1])
        _tt(k, "dve", A1[:], A1[:], smp[:, M_G1:M_G1 + 8], ALU.mult, r=[A1, smp], w=[A1])
        _act(k, t4[:], smp[:, M_LAM:M_LAM + 8], AF.Exp, r=[smp], w=[t4], scale=-1.0)
        _act(k, t4[:], t4[:], AF.Ln, r=[t4], w=[t4], bias=1.0)
        _ts(k, "dve", negc[:], t4[:], -8.0, None, ALU.mult, None, r=[t4], w=[negc])
        _ts(k, "dve", negc2[:], t4[:], -16.0, None, ALU.mult, None, r=[t4], w=[negc2])
        _ts(k, "dve", gqs[:], smp[:, M_QG:M_QG + 1], 1.0 / float(np.sqrt(128.0)), None, ALU.mult, None,
            r=[smp], w=[gqs])
        k.barrier()

    with contextlib.ExitStack() as ps:
        xb = [k.sb(f"{pfx}xb{i}", [128, 1024], F32, stack=ps) for i in range(3)]
        s_x = [k.slot(f"{pfx}x{i}") for i in range(3)]
        xh = [k.sb(f"{pfx}xh{i}", [128, 1024], BF16, stack=ps) for i in range(2)]
        junk = k.sb(f"{pfx}junk", [128, 1024], BF16, stack=ps)
        ssq = [k.sb(f"{pfx}ssq{i}", [128, 1], F32, stack=ps) for i in range(2)]
        hTo = [k.sb(f"{pfx}hTo{i}", [128, 8, 512], BF16, stack=ps) for i in range(2)]
        s_ho = [k.slot(f"{pfx}ho{i}") for i in range(2)]
        for tt in range(NT):
            banks = [P[(tt % 2) * 4 + i] for i in range(4)]
            bviews = [b[:].bitcast(BF16) for b in banks]
            for sub in range(4):
                blk = tt * 4 + sub
                xt = xb[blk % 3]
                k.dma("sp", s_x[blk % 3], xt[:], x_d[blk * 128:(blk + 1) * 128, :], w=[xt])
                sq = ssq[blk % 2]
                _act(k, junk[:], xt[:], AF.Square, r=[xt], w=[junk, sq], accum_out=sq[:])
                _act(k, sq[:], sq[:], AF.Sqrt, r=[sq], w=[sq], scale=1.0 / D, bias=EPS)
                k.op("dve", lambda e, sq=sq: e.reciprocal(out=sq[:], in_=sq[:]), r=[sq], w=[sq])
                xht = xh[blk % 2]
                _ts(k, "dve", xht[:, 0:512], xt[:, 0:512], sq[:, 0:1], None, ALU.mult, None, r=[xt, sq], w=[xht])
                _ts(k, "pool", xht[:, 512:1024], xt[:, 512:1024], sq[:, 0:1], None, ALU.mult, None, r=[xt, sq], w=[xht])
                for fc in range(8):
                    bi = fc // 2
                    c0 = (fc % 2) * 512 + sub * 128
                    k.op("pe", lambda e, o=bviews[bi][:, c0:c0 + 128], i_=xht[:, fc * 128:(fc + 1) * 128]:
                         e.transpose(o, i_, ident[:]), r=[xht, ident], w=[banks[bi]])
            ho = hTo[tt % 2]
            for fc in range(8):
                bi = fc // 2
                c0 = (fc % 2) * 512
                _act(k, ho[:, fc, :], bviews[bi][:, c0:c0 + 512], AF.Identity, r=[banks[bi], A1, modt], w=[ho],
                     scale=A1[:, fc:fc + 1], bias=modt[:, fc:fc + 1])
            k.dma("sp", s_ho[tt % 2], hT_d[:, :, tt * 512:(tt + 1) * 512], ho[:], r=[ho],
                  w=[k.dbuf((pfx + "hT", tt))])
        k.barrier()

    with contextlib.ExitStack() as ps:
        W = k.sb(f"{pfx}W", [128, 8, 896], BF16, stack=ps)
        s_w = k.slot(f"{pfx}w")
        hT = [k.sb(f"{pfx}hT{i}", [128, 8, 512], BF16, stack=ps) for i in range(2)]
        s_h = [k.slot(f"{pfx}h{i}") for i in range(2)]
        qT = k.sb(f"{pfx}qT", [128, S], BF16, stack=ps)
        kT = k.sb(f"{pfx}kT", [128, S], BF16, stack=ps)
        vA = k.sb(f"{pfx}vA", [128, NB * 128], BF16, stack=ps)
        mixA = k.sb(f"{pfx}mixA", [128, S], BF16, stack=ps)
        sgb = k.sb(f"{pfx}sgb", [128, S], BF16, stack=ps)
        xraw = k.sb(f"{pfx}xraw", [128, 515], F32, stack=ps)
        state = k.sb(f"{pfx}state", [128, 1], F32, stack=ps)

        def f32t(n):
            return k.sb(f"{pfx}{n}", [128, 512], F32, stack=ps)

        def bf16t(n):
            return k.sb(f"{pfx}{n}", [128, 512], BF16, stack=ps)
        names32 = ("xc", "r", "ig", "a", "a2", "u", "hs", "gl", "ya", "sga", "rq", "rk")
        dbl32 = [[f32t(f"{n}{i}") for n in names32] for i in range(2)]
        dbl16 = [[bf16t(f"{n}{i}") for n in ("xcb", "sqq", "sqk")] for i in range(2)]
        EB = [f32t(f"e{i}") for i in range(4)]
        LB = [bf16t(f"L{i}") for i in range(4)]
        XB = [f32t(f"X{i}") for i in range(3)]
        ATB = [bf16t(f"at{i}") for i in range(4)]
        tmpo = [f32t(f"tmpo{i}") for i in range(2)]
        mixo = [(bf16t if mix_bf16 else f32t)(f"mixo{i}") for i in range(2)]
        s_mo = [k.slot(f"{pfx}mo{i}") for i in range(2)]
        mo_ctr = [0]

        for cc in range(ncc):
            k.dma("pool", s_w, W[:], io["win"][cc], w=[W])
            k.op("dve", lambda e: e.memset(state[:], 0.0), w=[state])
            k.op("dve", lambda e: e.memset(xraw[:, 0:3], 0.0), w=[xraw])
            def load_h(t_):
                k.dma("sp", s_h[t_ % 2], hT[t_ % 2][:], hT_d[:, :, t_ * 512:(t_ + 1) * 512],
                      r=[k.dbuf((pfx + "hT", t_))], w=[hT[t_ % 2]])
            load_h(0)
            for tt in range(NT):
                tsl = slice(tt * 512, (tt + 1) * 512)
                h = hT[tt % 2]
                xc, r_t, ig, a_t, a2, u_t, hs, gl, ya, sga, rq, rk = dbl32[tt % 2]
                xcb, sqq, sqk = dbl16[tt % 2]

                def proj(pb, s_idx):
                    for kc in range(8):
                        _mm(k, pb[:], W[:, kc, s_idx * 128:(s_idx + 1) * 128], h[:, kc, :], kc == 0, kc == 7,
                            r=[W, h], w=[pb])
                pb = bank()
                proj(pb, 0)
                _act(k, xraw[:, 3:515], pb[:], AF.Copy, r=[pb], w=[xraw])
                pb = bank()
                proj(pb, 1)
                _act(k, gl[:], pb[:], AF.Gelu_apprx_tanh, r=[pb], w=[gl])
                pbq = bank()
                proj(pbq, 2)
                _act(k, sqq[:], pbq[:], AF.Square, r=[pbq], w=[sqq])
                pbk = bank()
                proj(pbk, 3)
                _act(k, sqk[:], pbk[:], AF.Square, r=[pbk], w=[sqk])
                pb = bank()
                proj(pb, 5)
                _act(k, sga[:], pb[:], AF.Sigmoid, r=[pb], w=[sga])
                pb = bank()
                proj(pb, 6)
                _act(k, sgb[:, tsl], pb[:], AF.Sigmoid, r=[pb], w=[sgb])
                pbv = bank()
                for sub in range(4):
                    for kc in range(8):
                        _mm(k, pbv[:, sub * 128:(sub + 1) * 128], h[:, kc, sub * 128:(sub + 1) * 128],
                            W[:, kc, 4 * 128:5 * 128], kc == 0, kc == 7, r=[W, h], w=[pbv])
                _copy(k, "dve", vA[:, tt * 512:(tt + 1) * 512], pbv[:], r=[pbv], w=[vA])
                if tt + 1 < NT:
                    load_h(tt + 1)
                cw = M_CW + cc * 4
                _ts(k, "dve", xc[:], xraw[:, 0:512], smp[:, cw:cw + 1], smp[:, M_CB + cc:M_CB + cc + 1],
                    ALU.mult, ALU.add, r=[xraw, smp], w=[xc])
                for tap in range(1, 4):
                    _stt(k, xc[:], xraw[:, tap:tap + 512], smp[:, cw + tap:cw + tap + 1], xc[:], ALU.mult, ALU.add,
                         r=[xraw, xc, smp], w=[xc])
                _copy(k, "pool", xcb[:], xc[:], r=[xc], w=[xcb])
                _copy(k, "pool", xraw[:, 0:3], xraw[:, 512:515], r=[xraw], w=[xraw])
                pbr = bank()
                _mm(k, pbr[:], wa[:, cc, :], xcb[:], True, True, r=[wa, xcb], w=[pbr])
                pbi = bank()
                _mm(k, pbi[:], wx[:, cc, :], xcb[:], True, True, r=[wx, xcb], w=[pbi])
                _act(k, r_t[:], pbr[:], AF.Sigmoid, r=[pbr, smp], w=[r_t], bias=smp[:, M_BA + cc:M_BA + cc + 1])
                _act(k, ig[:], pbi[:], AF.Sigmoid, r=[pbi, smp], w=[ig], bias=smp[:, M_BX + cc:M_BX + cc + 1])
                _act(k, a_t[:], r_t[:], AF.Exp, r=[r_t, negc], w=[a_t], scale=negc[:, cc:cc + 1])
                _act(k, a2[:], r_t[:], AF.Exp, r=[r_t, negc2], w=[a2], scale=negc2[:, cc:cc + 1])
                _act(k, a2[:], a2[:], AF.Sqrt, r=[a2], w=[a2], scale=-1.0, bias=1.0)
                _tt(k, "pool", u_t[:], ig[:], xc[:], ALU.mult, r=[ig, xc], w=[u_t])
                _tt(k, "pool", u_t[:], u_t[:], a2[:], ALU.mult, r=[u_t, a2], w=[u_t])
                k.op("dve", lambda e, hs=hs, a_t=a_t, u_t=u_t: e.tensor_tensor_scan(
                    out=hs[:], data0=a_t[:], data1=u_t[:], initial=state[:, 0:1], op0=ALU.mult, op1=ALU.add),
                     r=[a_t, u_t, state], w=[hs])
                _copy(k, "dve", state[:], hs[:, 511:512], r=[hs], w=[state])
                _tt(k, "pool", ya[:], hs[:], gl[:], ALU.mult, r=[hs, gl], w=[ya])
                _tt(k, "pool", mixA[:, tsl], ya[:], sga[:], ALU.mult, r=[ya, sga], w=[mixA])
                for (pbx, sqx, rx, gsc, dst) in ((pbq, sqq, rq, gqs[:, 0:1], qT),
                                                 (pbk, sqk, rk, smp[:, M_KG:M_KG + 1], kT)):
                    pbs = bank()
                    _mm(k, pbs[:], ones[:], sqx[:], True, True, r=[ones, sqx], w=[pbs])
                    _act(k, rx[:], pbs[:], AF.Sqrt, r=[pbs], w=[rx], scale=1.0 / 128.0, bias=EPS)
                    k.op("dve", lambda e, rx=rx: e.reciprocal(out=rx[:], in_=rx[:]), r=[rx], w=[rx])
                    _stt(k, dst[:, tsl], pbx[:], gsc, rx[:], ALU.mult, ALU.mult, r=[pbx, rx, gqs, smp], w=[dst])

            steps = []
            for qt in range(NT):
                topkb = 4 * qt + 3
                for kb in range(topkb, -1, -1):
                    steps.append(dict(i=len(steps), qt=qt, kb=kb, first=(kb == topkb), last=(kb == 0),
                                      diag=(kb - 4 * qt) if kb >= 4 * qt else None))
            ZP = [P[0], P[1]]
            ACC = [P[2], P[3]]
            OB = [P[4], P[5], P[6]]

            def stA(s):
                zp = ZP[s["i"] % 2]
                kb, qt = s["kb"], s["qt"]
                _mm(k, zp[:], kT[:, kb * 128:(kb + 1) * 128], qT[:, qt * 512:(qt + 1) * 512], True, True,
                    r=[kT, qT], w=[zp])

            def stB(s):
                zp = ZP[s["i"] % 2]
                e_ = EB[s["i"] % 4]
                L = LB[s["i"] % 4]
                _act(k, e_[:], zp[:], AF.Exp, r=[zp], w=[e_])
                if s["diag"] is not None:
                    dg = s["diag"]
                    _tt(k, "dve", e_[:], e_[:], masks[:, dg * 512:(dg + 1) * 512], ALU.mult, r=[e_, masks], w=[e_])
                _act(k, L[:], e_[:], AF.Ln, r=[e_], w=[L], bias=1.0)

            def stC(s):
                acc = ACC[s["qt"] % 2]
                L = LB[s["i"] % 4]
                _mm(k, acc[:], tri[:], L[:], s["first"], s["last"], r=[tri, L], w=[acc])

            def stD(s):
                acc = ACC[s["qt"] % 2]
                X = XB[s["i"] % 3]
                _act(k, X[:], acc[:], AF.Exp, r=[acc], w=[X], scale=-1.0)

            def stE(s):
                if s["last"]:
                    return
                acc = ACC[s["qt"] % 2]
                L = LB[s["i"] % 4]
                _mm(k, acc[:], trin[:], L[:], False, False, r=[trin, L], w=[acc])

            def stF(s):
                at = ATB[s["i"] % 4]
                _tt(k, "dve", at[:], EB[s["i"] % 4][:], XB[s["i"] % 3][:], ALU.mult,
                    r=[EB[s["i"] % 4], XB[s["i"] % 3]], w=[at])

            def stG(s):
                o = OB[s["qt"] % 3]
                at = ATB[s["i"] % 4]
                kb, qt = s["kb"], s["qt"]
                _mm(k, o[:], vA[:, kb * 128:(kb + 1) * 128], at[:], s["first"], s["last"], r=[vA, at], w=[o])
                if s["last"]:
                    qsl = slice(qt * 512, (qt + 1) * 512)
                    j = mo_ctr[0] % 2
                    mo_ctr[0] += 1
                    _tt(k, "dve", tmpo[j][:], o[:], sgb[:, qsl], ALU.mult, r=[o, sgb], w=[tmpo[j]])
                    _tt(k, "pool", mixo[j][:], tmpo[j][:], mixA[:, qsl], ALU.add, r=[tmpo[j], mixA], w=[mixo[j]])
                    k.dma("sp", s_mo[j], mix_d[cc * 128:(cc + 1) * 128, qsl], mixo[j][:], r=[mixo[j]],
                          w=[k.dbuf((pfx + "mix", cc, qt))])

            if "dbg_q" in io and cc == ncc - 1:
                s_dbg = k.slot(f"{pfx}dbg")
                k.dma("sp", s_dbg, io["dbg_small"][:, 0:16], modt[:], r=[modt])
                k.dma("sp", s_dbg, io["dbg_small"][:, 16:24], A1[:], r=[A1])
                for nm, t in (("dbg_q", qT), ("dbg_k", kT), ("dbg_v", vA), ("dbg_mixA", mixA), ("dbg_sgb", sgb)):
                    k.dma("sp", s_dbg, io[nm], t[:], r=[t])
            n = len(steps)
            for it in range(n + 3):
                if it < n:
                    stA(steps[it])
                    stB(steps[it])
                if 0 <= it - 2 < n:
                    stE(steps[it - 2])
                if 0 <= it - 1 < n:
                    stC(steps[it - 1])
                    stD(steps[it - 1])
                if 0 <= it - 2 < n:
                    stF(steps[it - 2])
                if 0 <= it - 3 < n:
                    stG(steps[it - 3])
        k.barrier()
    top.close()


def build_mixer(S, dbg=False):
    nc = bass.Bass("TRN2", target_bir_lowering=False)
    io = {}

    def inp(name, shape, dt=F32):
        io[name] = nc.dram_tensor(name, list(shape), dt, kind="ExternalInput").ap()
    inp("x", [S, D])
    inp("ident", [128, 128])
    inp("tri", [128, 128])
    inp("trin", [128, 128])
    inp("ones", [128, 128])
    inp("masks", [128, 2048])
    inp("smallp", [128, M_NP])
    inp("rgwa", [128, 4, 128])
    inp("rgwx", [128, 4, 128])
    inp("adaw", [4, 128, 8, 512])
    inp("win", [4, 128, 8, 896])
    io["mixT"] = nc.dram_tensor("mixT", [512, S], F32, kind="ExternalOutput").ap()
    if dbg:
        io["dbg_hT"] = nc.dram_tensor("dbg_hT", [128, 8, S], BF16, kind="ExternalOutput").ap()
        io["dbg_small"] = nc.dram_tensor("dbg_small", [128, 24], F32, kind="ExternalOutput").ap()
        for nm in ("dbg_q", "dbg_k", "dbg_v", "dbg_mixA", "dbg_sgb"):
            io[nm] = nc.dram_tensor(nm, [128, S], BF16, kind="ExternalOutput").ap()
    with contextlib.ExitStack() as st:
        k = K(nc, st)
        emit_mixer(k, nc, S, io)
        k.finish()
    return nc


def _pk(v):
    v = np.asarray(v, np.float32)
    return np.ascontiguousarray(v.reshape(-1, 128).T)


def _consts():
    p = np.arange(128)[:, None]
    c = np.arange(128)[None, :]
    tri = (p >= c).astype(np.float32)
    trin = (p < c).astype(np.float32)
    cc = np.arange(512)[None, :]
    masks = np.concatenate([((128 * i + p) < cc).astype(np.float32) for i in range(4)], axis=1)
    return dict(ident=np.eye(128, dtype=np.float32), tri=tri, trin=trin, ones=np.ones((128, 128), np.float32),
                masks=np.ascontiguousarray(masks))


def mixer_weights(l, b, hh, inp):
    if hh is None:
        ch, h0, ncc = slice(0, 1024), 0, 8
    else:
        ch, h0, ncc = slice(hh * 512, (hh + 1) * 512), hh * 4, 4
    d = {}
    ada_w, ada_b = inp["ada_w"][l], inp["ada_b"][l]
    smallp = np.zeros((128, M_NP), np.float32)
    smallp[:, M_CV:M_CV + 8] = _pk(inp["c"][b])
    smallp[:, M_BSH:M_BSH + 8] = _pk(ada_b[0:1024])
    smallp[:, M_BSC:M_BSC + 8] = _pk(ada_b[1024:2048])
    smallp[:, M_G1:M_G1 + 8] = _pk(inp["norm1_g"][l])
    cw = inp["conv_w"][l][:, ch]
    for cc in range(ncc):
        smallp[:, M_CW + cc * 4:M_CW + cc * 4 + 4] = cw[:, cc * 128:(cc + 1) * 128].T
    smallp[:, M_CB:M_CB + ncc] = _pk(inp["conv_b"][l][ch])
    smallp[:, M_BA:M_BA + ncc] = _pk(inp["rg_ba"][l][ch])
    smallp[:, M_BX:M_BX + ncc] = _pk(inp["rg_bx"][l][ch])
    smallp[:, M_LAM:M_LAM + ncc] = _pk(inp["rg_lambda"][l][ch])
    smallp[:, M_QG] = inp["q_norm_g"][l]
    smallp[:, M_KG] = inp["k_norm_g"][l]
    d["smallp"] = smallp
    d["rgwa"] = np.ascontiguousarray(inp["rg_wa"][l][h0:h0 + ncc].transpose(1, 0, 2))
    d["rgwx"] = np.ascontiguousarray(inp["rg_wx"][l][h0:h0 + ncc].transpose(1, 0, 2))
    aw = ada_w[:, 0:2048].reshape(8, 128, 4, 512).transpose(2, 1, 0, 3)
    d["adaw"] = np.ascontiguousarray(aw)
    w_in = inp["w_in"][l]
    w7 = w_in.reshape(8, 128, 7, 8, 128)[:, :, :, h0:h0 + ncc, :]
    d["win"] = np.ascontiguousarray(w7.transpose(3, 1, 0, 2, 4).reshape(ncc, 128, 8, 896))
    return d


def mixer_inputs(l, b, hh, xb, inp):
    d = dict(_consts())
    d["x"] = np.ascontiguousarray(xb, dtype=np.float32)
    d.update(mixer_weights(l, b, hh, inp))
    return d


_PROGS = {}


def _prog(kind, n):
    key = (kind, n)
    if key not in _PROGS:
        _PROGS[key] = build_mixer(n) if kind == "mixer" else build_ffn(n)
    return _PROGS[key]


FUSED = True


def kernel(**inputs):
    inp = {k_: np.asarray(v, dtype=np.float32) for k_, v in inputs.items()}
    x = inp["x"]
    B, S, _ = x.shape
    depth = inp["w_in"].shape[0]
    cores = list(range(8))
    if FUSED:
        key = ("fused", S, depth)
        if key not in _PROGS:
            _PROGS[key] = build_fused(S, depth)
        per_b = [fused_inputs(b, inp, depth) for b in range(B)]
        res = run_bass_kernel_spmd(_PROGS[key], [per_b[c // 2] for c in cores], core_ids=cores)
        half = S // 2
        return np.stack([np.concatenate([np.asarray(res.results[2 * b]["out"])[:half],
                                         np.asarray(res.results[2 * b + 1]["out"])[half:]], axis=0)
                         for b in range(B)], axis=0).astype(np.float32)
    for l in range(depth):
        nc = _prog("mixer", S)
        in_maps = [mixer_inputs(l, c // 2, c % 2, x[c // 2], inp) for c in cores]
        res = run_bass_kernel_spmd(nc, in_maps, core_ids=cores)
        mixT = [np.concatenate([np.asarray(res.results[2 * b]["mixT"]), np.asarray(res.results[2 * b + 1]["mixT"])],
                               axis=0) for b in range(B)]
        nc = _prog("ffn", S // 2)
        in_maps = [ffn_inputs(l, c // 2, c % 2, x[c // 2], mixT[c // 2], inp) for c in cores]
        res = run_bass_kernel_spmd(nc, in_maps, core_ids=cores)
        x = np.stack([np.concatenate([np.asarray(res.results[2 * b]["xout"]), np.asarray(res.results[2 * b + 1]["xout"])],
                                     axis=0) for b in range(B)], axis=0).astype(np.float32)
    return x


F_CV, F_BSH, F_BSC, F_G2, F_CW, F_CB, F_FLAG, F_NP = 0, 8, 16, 24, 32, 176, 224, 225


def emit_ffn(k, nc, S2, io, pfx="f", P=None, pre=True, mix_bf16=False):
    NT = S2 // 512
    x_d, mix_d, out_d = io["xin"], io["mixin"], io["xout"]
    if P is None:
        P = [k.ps(f"{pfx}ps{i}", [128, 512], F32) for i in range(8)]
    top = contextlib.ExitStack()
    bank_ctr = [0]

    def bank():
        b = P[4 + bank_ctr[0] % 4]
        bank_ctr[0] += 1
        return b

    ident = k.sb(f"{pfx}ident", [128, 128], BF16, stack=top)
    smp = k.sb(f"{pfx}smp", [128, F_NP], F32, stack=top)
    wo = k.sb(f"{pfx}wo", [128, 8, 1024], BF16, stack=top)
    wd = k.sb(f"{pfx}wd", [128, 24, 1024], BF16, stack=top)
    gtB = k.sb(f"{pfx}gtB", [128, 2048], F32, stack=top)
    modt = k.sb(f"{pfx}modt", [128, 16], F32, stack=top)
    A2 = k.sb(f"{pfx}A2", [128, 8], F32, stack=top)
    halo = k.sb(f"{pfx}halo", [128, 48, 2], F32, stack=top)
    s_c = k.slot(f"{pfx}const")
    s_c2 = k.slot(f"{pfx}const2")
    k.dma("sp", s_c2, smp[:], io["smallp"], w=[smp])
    k.dma("pool", s_c, ident[:], io["ident"], w=[ident])
    k.dma("pool", s_c, wo[:], io["wout"], w=[wo])
    for q in range(4):
        k.dma("pool", s_c, wd[:, q * 6:(q + 1) * 6, :], io["fdown"][:, q * 6:(q + 1) * 6, :], w=[wd])

    with contextlib.ExitStack() as ps:
        adaw_t = [k.sb(f"{pfx}adaw{i}", [128, 8, 512], F32, stack=ps) for i in range(2)]
        s_aw = [k.slot(f"{pfx}adaw{i}") for i in range(2)]
        cB = k.sb(f"{pfx}cB", [128, 8, 128], F32, stack=ps)
        onesf = k.sb(f"{pfx}onesf", [128, 128], F32, stack=ps)
        gtb_t = k.sb(f"{pfx}gtb_t", [128, 2048], F32, stack=ps)
        k.dma("sp", s_c2, gtb_t[:], io["gtb"].partition_broadcast(128), w=[gtb_t])
        k.op("dve", lambda e: e.memset(onesf[:], 1.0), w=[onesf])
        for kc in range(8):
            _ts(k, "dve", cB[:, kc, :], onesf[:], smp[:, F_CV + kc:F_CV + kc + 1], None, ALU.mult, None,
                r=[onesf, smp], w=[cB])
        pm = bank()
        for piece in range(4):
            at = adaw_t[piece % 2]
            k.dma("sp", s_aw[piece % 2], at[:], io["adaw"][piece], w=[at])
            for jj in range(4):
                j = piece * 4 + jj
                for kc in range(8):
                    _mm(k, pm[:, j:j + 1], at[:, kc, jj * 128:(jj + 1) * 128], smp[:, F_CV + kc:F_CV + kc + 1],
                        kc == 0, kc == 7, r=[at, smp], w=[pm])
        _tt(k, "dve", modt[:], pm[:, 0:16], smp[:, F_BSH:F_BSH + 16], ALU.add, r=[pm, smp], w=[modt])
        _ts(k, "dve", A2[:], modt[:, 8:16], 1.0, None, ALU.add, None, r=[modt], w=[A2])
        _tt(k, "dve", A2[:], A2[:], smp[:, F_G2:F_G2 + 8], ALU.mult, r=[A2, smp], w=[A2])
        for piece in range(4, 8):
            at = adaw_t[piece % 2]
            k.dma("sp", s_aw[piece % 2], at[:], io["adaw"][piece], w=[at])
            pb = bank()
            for kc in range(8):
                _mm(k, pb[:], cB[:, kc, :], at[:, kc, :], kc == 0, kc == 7, r=[cB, at], w=[pb])
            c0 = (piece - 4) * 512
            _tt(k, "dve", gtB[:, c0:c0 + 512], pb[:], gtb_t[:, c0:c0 + 512], ALU.add, r=[pb, gtb_t], w=[gtB])
        k.barrier()

    with contextlib.ExitStack() as ps:
        mx = [k.sb(f"{pfx}mx{i}", [128, 8, 512], BF16, stack=ps) for i in range(2)]
        s_mx = [k.slot(f"{pfx}mx{i}") for i in range(2)]
        xt = [k.sb(f"{pfx}xt{i}", [128, 1024], F32, stack=ps) for i in range(2)]
        s_xt = [k.slot(f"{pfx}xt{i}") for i in range(2)]
        x1t = [k.sb(f"{pfx}x1t{i}", [128, 1024], F32, stack=ps) for i in range(4)]
        xh = [k.sb(f"{pfx}xh{i}", [128, 1024], BF16, stack=ps) for i in range(2)]
        junk = k.sb(f"{pfx}junk", [128, 1024], BF16, stack=ps)
        ssq = [k.sb(f"{pfx}ssq{i}", [128, 1], F32, stack=ps) for i in range(2)]
        h2T = k.sb(f"{pfx}h2T", [128, 8, 512], BF16, stack=ps)
        Wj = [k.sb(f"{pfx}Wj{i}", [128, 8, 256], BF16, stack=ps) for i in range(3)]
        s_wj = [k.slot(f"{pfx}wj{i}") for i in range(3)]
        raw = [k.sb(f"{pfx}raw{i}", [128, 514], F32, stack=ps) for i in range(4)]
        tcv = [k.sb(f"{pfx}tcv{i}", [128, 512], F32, stack=ps) for i in range(4)]
        gg = [k.sb(f"{pfx}gg{i}", [128, 512], F32, stack=ps) for i in range(2)]
        actT = k.sb(f"{pfx}actT", [128, 24, 512], BF16, stack=ps)
        tmp = [k.sb(f"{pfx}tmp{i}", [128, 512], F32, stack=ps) for i in range(2)]
        xo = [k.sb(f"{pfx}xo{i}", [128, 1024], F32, stack=ps) for i in range(2)]
        s_xo = [k.slot(f"{pfx}xo{i}") for i in range(2)]
        ctr = dict(x=0, w=0, raw=0, t=0, xo=0, g=0, tmp=0, tile=0)
        mixv = mix_d.rearrange("(kc p) t -> p kc t", p=128)

        tiles = ([(0, 128, True, 0)] if pre else []) + \
            [((128 if pre else 0) + tt * 512, 512, False, tt * 512) for tt in range(NT)]
        wst = dict(issued=0, total=24 * len(tiles))

        def issue_w(upto):
            while wst["issued"] <= upto and wst["issued"] < wst["total"]:
                g = wst["issued"]
                k.dma("pool", s_wj[g % 3], Wj[g % 3][:], io["fup"][g % 24], w=[Wj[g % 3]])
                wst["issued"] += 1

        def load_mx(ti):
            row0_, ntok_ = tiles[ti][0], tiles[ti][1]
            m_ = mx[ti % 2]
            k.dma("sp" if mix_bf16 else "pool", s_mx[ti % 2], m_[:, :, 0:ntok_], mixv[:, :, row0_:row0_ + ntok_],
                  w=[m_])

        def do_tile(row0, ntok, pre, orow0):
            nsub = ntok // 128
            ti = ctr["tile"]
            ctr["tile"] += 1
            m = mx[ti % 2]
            if ti == 0:
                load_mx(0)
            issue_w(ti * 24 + 1)
            tb = [P[i] for i in range(4)]
            tbv = [b[:].bitcast(BF16) for b in tb]
            for sub in range(nsub):
                xi = ctr["x"] % 2
                ctr["x"] += 1
                xx = xt[xi]
                k.dma("sp", s_xt[xi], xx[:], x_d[row0 + sub * 128:row0 + (sub + 1) * 128, :], w=[xx])
                x1 = x1t[sub]
                for half in range(2):
                    hs_ = slice(half * 512, (half + 1) * 512)
                    po = bank()
                    for kc in range(8):
                        _mm(k, po[:], m[:, kc, sub * 128:(sub + 1) * 128], wo[:, kc, hs_], kc == 0, kc == 7,
                            r=[m, wo], w=[po])
                    tp = tmp[ctr["tmp"] % 2]
                    ctr["tmp"] += 1
                    _tt(k, "dve", tp[:], po[:], gtB[:, hs_], ALU.mult, r=[po, gtB], w=[tp])
                    _tt(k, "pool", x1[:, hs_], tp[:], xx[:, hs_], ALU.add, r=[tp, xx], w=[x1])
                sq = ssq[sub % 2]
                _act(k, junk[:], x1[:], AF.Square, r=[x1], w=[junk, sq], accum_out=sq[:])
                _act(k, sq[:], sq[:], AF.Sqrt, r=[sq], w=[sq], scale=1.0 / D, bias=EPS)
                k.op("dve", lambda e, sq=sq: e.reciprocal(out=sq[:], in_=sq[:]), r=[sq], w=[sq])
                xht = xh[sub % 2]
                _ts(k, "pool", xht[:], x1[:], sq[:, 0:1], None, ALU.mult, None, r=[x1, sq], w=[xht])
                for fc in range(8):
                    bi = fc // 2
                    c0 = (fc % 2) * 512 + sub * 128
                    k.op("pe", lambda e, o=tbv[bi][:, c0:c0 + 128], i_=xht[:, fc * 128:(fc + 1) * 128]:
                         e.transpose(o, i_, ident[:]), r=[xht, ident], w=[tb[bi]])
            for fc in range(8):
                bi = fc // 2
                c0 = (fc % 2) * 512
                _act(k, h2T[:, fc, 0:ntok], tbv[bi][:, c0:c0 + ntok], AF.Identity, r=[tb[bi], A2, modt], w=[h2T],
                     scale=A2[:, fc:fc + 1], bias=modt[:, fc:fc + 1])
            for j in range(24):
                g = ti * 24 + j
                issue_w(g + 2)
                wj = Wj[g % 3]
                if j == 8 and ti + 1 < len(tiles):
                    load_mx(ti + 1)
                res = []
                for br in range(2):
                    cidx = br * 24 + j
                    pb = bank()
                    for kc in range(8):
                        _mm(k, pb[:, 0:ntok], wj[:, kc, br * 128:(br + 1) * 128], h2T[:, kc, 0:ntok], kc == 0, kc == 7,
                            r=[wj, h2T], w=[pb])
                    if pre:
                        _ts(k, "dve", halo[:, cidx, :], pb[:, ntok - 2:ntok], smp[:, F_FLAG:F_FLAG + 1], None,
                            ALU.mult, None, r=[pb, smp], w=[halo])
                        continue
                    rw = raw[ctr["raw"] % 4]
                    ctr["raw"] += 1
                    _act(k, rw[:, 2:2 + ntok], pb[:, 0:ntok], AF.Copy, r=[pb], w=[rw])
                    _copy(k, "pool", rw[:, 0:2], halo[:, cidx, :], r=[halo], w=[rw])
                    tc = tcv[ctr["t"] % 4]
                    ctr["t"] += 1
                    cw = F_CW + cidx * 3
                    _ts(k, "dve", tc[:], rw[:, 0:512], smp[:, cw:cw + 1], smp[:, F_CB + cidx:F_CB + cidx + 1],
                        ALU.mult, ALU.add, r=[rw, smp], w=[tc])
                    for tap in (1, 2):
                        _stt(k, tc[:], rw[:, tap:tap + 512], smp[:, cw + tap:cw + tap + 1], tc[:], ALU.mult, ALU.add,
                             r=[rw, tc, smp], w=[tc])
                    _copy(k, "pool", halo[:, cidx, :], rw[:, 512:514], r=[rw], w=[halo])
                    res.append(tc)
                if pre:
                    continue
                g_ = gg[ctr["g"] % 2]
                ctr["g"] += 1
                _act(k, g_[:], res[0][:], AF.Gelu_apprx_tanh, r=[res[0]], w=[g_])
                _tt(k, "pool", actT[:, j, :], g_[:], res[1][:], ALU.mult, r=[g_, res[1]], w=[actT])
            if pre:
                return
            for sub in range(nsub):
                oi = ctr["xo"] % 2
                ctr["xo"] += 1
                xo_ = xo[oi]
                for half in range(2):
                    hs_ = slice(half * 512, (half + 1) * 512)
                    pd = bank()
                    for j in range(24):
                        _mm(k, pd[:], actT[:, j, sub * 128:(sub + 1) * 128], wd[:, j, hs_], j == 0, j == 23,
                            r=[actT, wd], w=[pd])
                    tp = tmp[ctr["tmp"] % 2]
                    ctr["tmp"] += 1
                    _tt(k, "dve", tp[:], pd[:], gtB[:, 1024 + half * 512:1024 + (half + 1) * 512], ALU.mult,
                        r=[pd, gtB], w=[tp])
                    _tt(k, "pool", xo_[:, hs_], tp[:], x1t[sub][:, hs_], ALU.add, r=[tp, x1t[sub]], w=[xo_])
                r0 = orow0 + sub * 128
                k.dma("sp", s_xo[oi], out_d[r0:r0 + 128, :], xo_[:], r=[xo_], w=[k.dbuf((pfx + "out", r0))])

        if not pre:
            k.op("dve", lambda e: e.memset(halo[:], 0.0), w=[halo])
        for tl in tiles:
            do_tile(*tl)
        k.barrier()
    top.close()


def build_ffn(S2):
    nc = bass.Bass("TRN2", target_bir_lowering=False)
    io = {}

    def inp(name, shape, dt=F32):
        io[name] = nc.dram_tensor(name, list(shape), dt, kind="ExternalInput").ap()
    inp("xin", [128 + S2, D])
    inp("mixin", [D, 128 + S2])
    inp("ident", [128, 128])
    inp("smallp", [128, F_NP])
    inp("gtb", [2048])
    inp("adaw", [8, 128, 8, 512])
    inp("wout", [128, 8, 1024])
    inp("fup", [24, 128, 8, 256])
    inp("fdown", [128, 24, 1024])
    io["xout"] = nc.dram_tensor("xout", [S2, D], F32, kind="ExternalOutput").ap()
    with contextlib.ExitStack() as st:
        k = K(nc, st)
        emit_ffn(k, nc, S2, io)
        k.finish()
    return nc


def ffn_weights(l, b, flag, inp):
    d = {}
    ada_w, ada_b = inp["ada_w"][l], inp["ada_b"][l]
    smallp = np.zeros((128, F_NP), np.float32)
    smallp[:, F_CV:F_CV + 8] = _pk(inp["c"][b])
    smallp[:, F_BSH:F_BSH + 8] = _pk(ada_b[3072:4096])
    smallp[:, F_BSC:F_BSC + 8] = _pk(ada_b[4096:5120])
    smallp[:, F_G2:F_G2 + 8] = _pk(inp["norm2_g"][l])
    cw = inp["ffn_conv_w"][l]
    smallp[:, F_CW:F_CW + 144] = cw.reshape(3, 48, 128).transpose(2, 1, 0).reshape(128, 144)
    smallp[:, F_CB:F_CB + 48] = _pk(inp["ffn_conv_b"][l])
    smallp[:, F_FLAG] = flag
    d["smallp"] = smallp
    d["gtb"] = np.ascontiguousarray(np.concatenate([ada_b[2048:3072], ada_b[5120:6144]]))
    cols = np.concatenate([np.arange(3072, 5120), np.arange(2048, 3072), np.arange(5120, 6144)])
    aw = ada_w[:, cols].reshape(8, 128, 8, 512).transpose(2, 1, 0, 3)
    d["adaw"] = np.ascontiguousarray(aw)
    d["wout"] = np.ascontiguousarray(inp["w_out"][l].reshape(8, 128, 1024).transpose(1, 0, 2))
    fu = inp["ffn_up"][l].reshape(8, 128, 2, 24, 128)
    d["fup"] = np.ascontiguousarray(fu.transpose(3, 1, 0, 2, 4).reshape(24, 128, 8, 256))
    d["fdown"] = np.ascontiguousarray(inp["ffn_down"][l].reshape(24, 128, 1024).transpose(1, 0, 2))
    return d


def ffn_inputs(l, b, th, xb, mixTb, inp):
    S = xb.shape[0]
    S2 = S // 2
    t0 = th * S2
    p0 = t0 - 128 if th > 0 else 0
    d = dict(ident=np.eye(128, dtype=np.float32))
    d["xin"] = np.ascontiguousarray(np.concatenate([xb[p0:p0 + 128], xb[t0:t0 + S2]], axis=0), dtype=np.float32)
    d["mixin"] = np.ascontiguousarray(np.concatenate([mixTb[:, p0:p0 + 128], mixTb[:, t0:t0 + S2]], axis=1),
                                      dtype=np.float32)
    d.update(ffn_weights(l, b, 1.0 if th > 0 else 0.0, inp))
    return d


_MW = ("smallp", "rgwa", "rgwx", "adaw", "win")
_FW = ("smallp", "gtb", "adaw", "wout", "fup", "fdown")


def build_fused(S, depth=2, skip=()):
    nc = bass.Bass("TRN2", target_bir_lowering=False)
    ext = {}

    def inp(name, shape, dt=F32):
        ext[name] = nc.dram_tensor(name, list(shape), dt, kind="ExternalInput").ap()
    inp("x", [S, D])
    for n_ in ("ident", "tri", "trin", "ones"):
        inp(n_, [128, 128])
    inp("masks", [128, 2048])
    for l in range(depth):
        inp(f"m{l}_smallp", [128, M_NP])
        inp(f"m{l}_rgwa", [128, 8, 128])
        inp(f"m{l}_rgwx", [128, 8, 128])
        inp(f"m{l}_adaw", [4, 128, 8, 512])
        inp(f"m{l}_win", [8, 128, 8, 896])
        inp(f"f{l}_smallp", [128, F_NP])
        inp(f"f{l}_gtb", [2048])
        inp(f"f{l}_adaw", [8, 128, 8, 512])
        inp(f"f{l}_wout", [128, 8, 1024])
        inp(f"f{l}_fup", [24, 128, 8, 256])
        inp(f"f{l}_fdown", [128, 24, 1024])
    out = nc.dram_tensor("out", [S, D], F32, kind="ExternalOutput").ap()
    mixT_d = nc.dram_tensor("mixT_d", [D, S], BF16).ap()
    xmid = [nc.dram_tensor(f"xmid{l}", [S, D], F32).ap() for l in range(depth - 1)]
    with contextlib.ExitStack() as st:
        k = K(nc, st)
        P = [k.ps(f"ps{i}", [128, 512], F32) for i in range(8)]
        for l in range(depth):
            x_in = ext["x"] if l == 0 else xmid[l - 1]
            x_out = out if l == depth - 1 else xmid[l]
            io = dict(x=x_in, mixT=mixT_d, ident=ext["ident"], tri=ext["tri"], trin=ext["trin"], ones=ext["ones"],
                      masks=ext["masks"])
            for n_ in _MW:
                io[n_] = ext[f"m{l}_{n_}"]
            if f"m{l}" not in skip:
                emit_mixer(k, nc, S, io, pfx=f"m{l}", P=P, ncc=8, mix_bf16=True)
            io = dict(xin=x_in, mixin=mixT_d, xout=x_out, ident=ext["ident"])
            for n_ in _FW:
                io[n_] = ext[f"f{l}_{n_}"]
            if f"f{l}" not in skip:
                emit_ffn(k, nc, S, io, pfx=f"f{l}", P=P, pre=False, mix_bf16=True)
        k.finish()
    return nc


def fused_inputs(b, inp, depth):
    d = dict(_consts())
    d["x"] = np.ascontiguousarray(inp["x"][b], dtype=np.float32)
    for l in range(depth):
        for n_, v in mixer_weights(l, b, None, inp).items():
            d[f"m{l}_{n_}"] = v
        for n_, v in ffn_weights(l, b, 0.0, inp).items():
            d[f"f{l}_{n_}"] = v
    return d
```

```python
import contextlib
import re
import numpy as np
import concourse.bass as bass
import concourse.mybir as mybir
from concourse.bass_utils import run_bass_kernel_spmd

F32 = mybir.dt.float32
BF16 = mybir.dt.bfloat16
AF = mybir.ActivationFunctionType
ALU = mybir.AluOpType

D = 1024
NH = 8
EPS = 1e-6
DFF = 3072


class _Eng:
    def __init__(self, name, sem, is_pe=False):
        self.name = name
        self.sem = sem
        self.count = 0
        self.waited = {}
        self.prog = []
        self.is_pe = is_pe
        self.is_slot = False
        self.pending = []


class _Slot:
    def __init__(self, name, sem):
        self.name = name
        self.sem = sem
        self.count = 0
        self.is_slot = True


class Buf:
    __slots__ = ("last_w", "readers", "name")

    def __init__(self, name=""):
        self.last_w = None
        self.readers = {}
        self.name = name


class TT:
    def __init__(self, t, name):
        self.t = t
        self.buf = Buf(name)

    def __getitem__(self, key):
        return self.t[key]


class K:
    def __init__(self, nc, stack):
        self.nc = nc
        self.stack = stack
        self.engs = {}
        for name in ("pe", "act", "dve", "pool", "sp"):
            sem = stack.enter_context(nc.semaphore("sem_" + name))
            self.engs[name] = _Eng(name, sem, is_pe=(name == "pe"))
        self.slots = []
        self._slot_cache = {}
        self.dbufs = {}
        self.n_inst = 0

    def slot(self, name):
        key = re.sub(r"^([mf])\d", r"\1", name)
        if key in self._slot_cache:
            return self._slot_cache[key]
        sem = self.stack.enter_context(self.nc.semaphore("slot_" + key))
        s = _Slot(key, sem)
        self.slots.append(s)
        self._slot_cache[key] = s
        return s

    def sb(self, name, shape, dtype, stack=None):
        st = stack if stack is not None else self.stack
        t = st.enter_context(self.nc.sbuf_tensor(name, list(shape), dtype))
        return TT(t, name)

    def ps(self, name, shape, dtype, stack=None):
        st = stack if stack is not None else self.stack
        t = st.enter_context(self.nc.psum_tensor(name, list(shape), dtype))
        return TT(t, name)

    def dbuf(self, key):
        b = self.dbufs.get(key)
        if b is None:
            b = Buf(str(key))
            self.dbufs[key] = b
        return b

    @staticmethod
    def _b(x):
        return x.buf if isinstance(x, TT) else x

    def _waits(self, E, reads, writes):
        deps = {}

        def add(obj, val):
            if deps.get(obj, 0) < val:
                deps[obj] = val
        for b in reads:
            b = self._b(b)
            if b.last_w is not None:
                add(*b.last_w)
        for b in writes:
            b = self._b(b)
            if b.last_w is not None:
                add(*b.last_w)
            for o, v in b.readers.items():
                add(o, v)
        waits = []
        for obj, val in deps.items():
            if obj.is_slot:
                val = obj.count
            elif obj is E:
                if E.is_pe:
                    continue
                if E.count - val >= 2:
                    continue
            if E.waited.get(obj, 0) >= val:
                continue
            E.waited[obj] = val
            waits.append((obj.sem, val))
        return waits

    def barrier(self):
        for E in self.engs.values():
            for o in list(self.engs.values()) + self.slots:
                if o is E or o.count == 0:
                    continue
                if E.waited.get(o, 0) >= o.count:
                    continue
                E.waited[o] = o.count
                E.pending.append((o.sem, o.count))

    def op(self, eng, fn, r=(), w=()):
        E = self.engs[eng]
        waits = self._waits(E, r, w)
        waits = E.pending + waits
        E.pending = []
        E.count += 1
        idx = E.count
        E.prog.append((waits, fn, (E.sem, 1)))
        for b in r:
            b = self._b(b)
            if b.readers.get(E, 0) < idx:
                b.readers[E] = idx
        for b in w:
            b = self._b(b)
            b.last_w = (E, idx)
            b.readers = {}
        self.n_inst += 1

    def dma(self, q, slot, out, in_, r=(), w=()):
        E = self.engs[q]
        waits = self._waits(E, r, w)
        waits = E.pending + waits
        E.pending = []
        slot.count += 16
        val = slot.count
        E.prog.append((waits, lambda e, out=out, in_=in_: e.dma_start(out=out, in_=in_), (slot.sem, 16)))
        for b in r:
            b = self._b(b)
            b.readers[slot] = val
        for b in w:
            b = self._b(b)
            b.last_w = (slot, val)
            b.readers = {}
        self.n_inst += 1

    def finish(self):
        E = self.engs["sp"]
        final_waits = [(s.sem, s.count) for s in self.slots if s.count > 0]
        for n in ("pe", "act", "dve", "pool"):
            e2 = self.engs[n]
            if e2.count > 0:
                final_waits.append((e2.sem, e2.count))
        progs = {n: e.prog for n, e in self.engs.items()}
        nc = self.nc
        with nc.Block() as block:
            def run(e, prog, extra=()):
                for waits, fn, inc in prog:
                    for sem, val in waits:
                        e.wait_ge(sem, val)
                    ins = fn(e)
                    ins.then_inc(inc[0], inc[1])
                for sem, val in extra:
                    e.wait_ge(sem, val)

            @block.sync
            def _(e):
                run(e, progs["sp"], final_waits)

            @block.tensor
            def _(e):
                run(e, progs["pe"])

            @block.scalar
            def _(e):
                run(e, progs["act"])

            @block.vector
            def _(e):
                run(e, progs["dve"])

            @block.gpsimd
            def _(e):
                run(e, progs["pool"])


def _act(k, out, in_, func, r, w, **kw):
    k.op("act", lambda e: e.activation(out=out, in_=in_, func=func, **kw), r=r, w=w)


def _mm(k, out, lhsT, rhs, start, stop, r, w):
    k.op("pe", lambda e: e.matmul(out, lhsT=lhsT, rhs=rhs, start=start, stop=stop), r=r, w=w)


def _tt(k, eng, out, in0, in1, op, r, w):
    k.op(eng, lambda e: e.tensor_tensor(out=out, in0=in0, in1=in1, op=op), r=r, w=w)


def _ts(k, eng, out, in0, s1, s2, op0, op1, r, w):
    if op1 is None:
        k.op(eng, lambda e: e.tensor_scalar(out=out, in0=in0, scalar1=s1, scalar2=None, op0=op0), r=r, w=w)
    else:
        k.op(eng, lambda e: e.tensor_scalar(out=out, in0=in0, scalar1=s1, scalar2=s2, op0=op0, op1=op1), r=r, w=w)


def _stt(k, out, in0, scalar, in1, op0, op1, r, w):
    k.op("dve", lambda e: e.scalar_tensor_tensor(out=out, in0=in0, scalar=scalar, in1=in1, op0=op0, op1=op1),
         r=r, w=w)


def _copy(k, eng, out, in_, r, w):
    k.op(eng, lambda e: e.tensor_copy(out=out, in_=in_), r=r, w=w)


M_CV, M_BSH, M_BSC, M_G1, M_CW, M_CB, M_BA, M_BX, M_LAM, M_QG, M_KG, M_NP = 0, 8, 16, 24, 32, 64, 72, 80, 88, 96, 97, 98


def emit_mixer(k, nc, S, io, pfx="m", P=None, ncc=4, mix_bf16=False):
    NT = S // 512
    NB = S // 128
    x_d, mix_d = io["x"], io["mixT"]
    if P is None:
        P = [k.ps(f"{pfx}ps{i}", [128, 512], F32) for i in range(8)]
    top = contextlib.ExitStack()
    bank_ctr = [0]

    def bank():
        b = P[bank_ctr[0] % 8]
        bank_ctr[0] += 1
        return b

    hT_d = io["dbg_hT"] if "dbg_hT" in io else nc.dram_tensor(f"{pfx}_hT_d", [128, 8, S], BF16).ap()

    ident = k.sb(f"{pfx}ident", [128, 128], BF16, stack=top)
    tri = k.sb(f"{pfx}tri", [128, 128], BF16, stack=top)
    trin = k.sb(f"{pfx}trin", [128, 128], BF16, stack=top)
    ones = k.sb(f"{pfx}ones", [128, 128], BF16, stack=top)
    masks = k.sb(f"{pfx}masks", [128, 4 * 512], F32, stack=top)
    smp = k.sb(f"{pfx}smp", [128, M_NP], F32, stack=top)
    wa = k.sb(f"{pfx}wa", [128, ncc, 128], BF16, stack=top)
    wx = k.sb(f"{pfx}wx", [128, ncc, 128], BF16, stack=top)
    s_c = k.slot(f"{pfx}const")
    s_c2 = k.slot(f"{pfx}const2")
    k.dma("pool", s_c, ident[:], io["ident"], w=[ident])
    k.dma("pool", s_c, tri[:], io["tri"], w=[tri])
    k.dma("pool", s_c, trin[:], io["trin"], w=[trin])
    k.dma("pool", s_c, ones[:], io["ones"], w=[ones])
    k.dma("pool", s_c, wa[:], io["rgwa"], w=[wa])
    k.dma("pool", s_c, wx[:], io["rgwx"], w=[wx])
    k.dma("sp", s_c2, masks[:], io["masks"], w=[masks])
    k.dma("sp", s_c2, smp[:], io["smallp"], w=[smp])

    modt = k.sb(f"{pfx}modt", [128, 16], F32, stack=top)
    A1 = k.sb(f"{pfx}A1", [128, 8], F32, stack=top)
    negc = k.sb(f"{pfx}negc", [128, 8], F32, stack=top)
    negc2 = k.sb(f"{pfx}negc2", [128, 8], F32, stack=top)
    t4 = k.sb(f"{pfx}t4", [128, 8], F32, stack=top)
    gqs = k.sb(f"{pfx}gqs", [128, 1], F32, stack=top)

    with contextlib.ExitStack() as ps:
        adaw_t = [k.sb(f"{pfx}adaw{i}", [128, 8, 512], F32, stack=ps) for i in range(2)]
        s_aw = [k.slot(f"{pfx}adaw{i}") for i in range(2)]
        pm = bank()
        for piece in range(4):
            at = adaw_t[piece % 2]
            k.dma("sp", s_aw[piece % 2], at[:], io["adaw"][piece], w=[at])
            for jj in range(4):
                j = piece * 4 + jj
                for kc in range(8):
                    _mm(k, pm[:, j:j + 1], at[:, kc, jj * 128:(jj + 1) * 128], smp[:, M_CV + kc:M_CV + kc + 1],
                        kc == 0, kc == 7, r=[at, smp], w=[pm])
        _tt(k, "dve", modt[:], pm[:, 0:16], smp[:, M_BSH:M_BSH + 16], ALU.add, r=[pm, smp], w=[modt])
        _ts(k, "dve", A1[:], modt[:, 8:16], 1.0, None, ALU.add, None, r=[modt], w=[A1])
        _tt(k, "dve", A1[:], A1[:], smp[:, M_G1:M_G1 + 8], ALU.mult, r=[A1, smp], w=[A1])
        _act(k, t4[:], smp[:, M_LAM:M_LAM + 8], AF.Exp, r=[smp], w=[t4], scale=-1.0)
        _act(k, t4[:], t4[:], AF.Ln, r=[t4], w=[t4], bias=1.0)
        _ts(k, "dve", negc[:], t4[:], -8.0, None, ALU.mult, None, r=[t4], w=[negc])
        _ts(k, "dve", negc2[:], t4[:], -16.0, None, ALU.mult, None, r=[t4], w=[negc2])
        _ts(k, "dve", gqs[:], smp[:, M_QG:M_QG + 1], 1.0 / float(np.sqrt(128.0)), None, ALU.mult, None,
            r=[smp], w=[gqs])
        k.barrier()

    with contextlib.ExitStack() as ps:
        xb = [k.sb(f"{pfx}xb{i}", [128, 1024], F32, stack=ps) for i in range(3)]
        s_x = [k.slot(f"{pfx}x{i}") for i in range(3)]
        xh = [k.sb(f"{pfx}xh{i}", [128, 1024], BF16, stack=ps) for i in range(2)]
        junk = k.sb(f"{pfx}junk", [128, 1024], BF16, stack=ps)
        ssq = [k.sb(f"{pfx}ssq{i}", [128, 1], F32, stack=ps) for i in range(2)]
        hTo = [k.sb(f"{pfx}hTo{i}", [128, 8, 512], BF16, stack=ps) for i in range(2)]
        s_ho = [k.slot(f"{pfx}ho{i}") for i in range(2)]
        for tt in range(NT):
            banks = [P[(tt % 2) * 4 + i] for i in range(4)]
            bviews = [b[:].bitcast(BF16) for b in banks]
            for sub in range(4):
                blk = tt * 4 + sub
                xt = xb[blk % 3]
                k.dma("sp", s_x[blk % 3], xt[:], x_d[blk * 128:(blk + 1) * 128, :], w=[xt])
                sq = ssq[blk % 2]
                _act(k, junk[:], xt[:], AF.Square, r=[xt], w=[junk, sq], accum_out=sq[:])
                _act(k, sq[:], sq[:], AF.Sqrt, r=[sq], w=[sq], scale=1.0 / D, bias=EPS)
                k.op("dve", lambda e, sq=sq: e.reciprocal(out=sq[:], in_=sq[:]), r=[sq], w=[sq])
                xht = xh[blk % 2]
                _ts(k, "dve", xht[:, 0:512], xt[:, 0:512], sq[:, 0:1], None, ALU.mult, None, r=[xt, sq], w=[xht])
                _ts(k, "pool", xht[:, 512:1024], xt[:, 512:1024], sq[:, 0:1], None, ALU.mult, None, r=[xt, sq], w=[xht])
                for fc in range(8):
                    bi = fc // 2
                    c0 = (fc % 2) * 512 + sub * 128
                    k.op("pe", lambda e, o=bviews[bi][:, c0:c0 + 128], i_=xht[:, fc * 128:(fc + 1) * 128]:
                         e.transpose(o, i_, ident[:]), r=[xht, ident], w=[banks[bi]])
            ho = hTo[tt % 2]
            for fc in range(8):
                bi = fc // 2
                c0 = (fc % 2) * 512
                _act(k, ho[:, fc, :], bviews[bi][:, c0:c0 + 512], AF.Identity, r=[banks[bi], A1, modt], w=[ho],
                     scale=A1[:, fc:fc + 1], bias=modt[:, fc:fc + 1])
            k.dma("sp", s_ho[tt % 2], hT_d[:, :, tt * 512:(tt + 1) * 512], ho[:], r=[ho],
                  w=[k.dbuf((pfx + "hT", tt))])
        k.barrier()

    with contextlib.ExitStack() as ps:
        W = k.sb(f"{pfx}W", [128, 8, 896], BF16, stack=ps)
        s_w = k.slot(f"{pfx}w")
        hT = [k.sb(f"{pfx}hT{i}", [128, 8, 512], BF16, stack=ps) for i in range(2)]
        s_h = [k.slot(f"{pfx}h{i}") for i in range(2)]
        qT = k.sb(f"{pfx}qT", [128, S], BF16, stack=ps)
        kT = k.sb(f"{pfx}kT", [128, S], BF16, stack=ps)
        vA = k.sb(f"{pfx}vA", [128, NB * 128], BF16, stack=ps)
        mixA = k.sb(f"{pfx}mixA", [128, S], BF16, stack=ps)
        sgb = k.sb(f"{pfx}sgb", [128, S], BF16, stack=ps)
        xraw = k.sb(f"{pfx}xraw", [128, 515], F32, stack=ps)
        state = k.sb(f"{pfx}state", [128, 1], F32, stack=ps)

        def f32t(n):
            return k.sb(f"{pfx}{n}", [128, 512], F32, stack=ps)

        def bf16t(n):
            return k.sb(f"{pfx}{n}", [128, 512], BF16, stack=ps)
        names32 = ("xc", "r", "ig", "a", "a2", "u", "hs", "gl", "ya", "sga", "rq", "rk")
        dbl32 = [[f32t(f"{n}{i}") for n in names32] for i in range(2)]
        dbl16 = [[bf16t(f"{n}{i}") for n in ("xcb", "sqq", "sqk")] for i in range(2)]
        EB = [f32t(f"e{i}") for i in range(4)]
        LB = [bf16t(f"L{i}") for i in range(4)]
        XB = [f32t(f"X{i}") for i in range(3)]
        ATB = [bf16t(f"at{i}") for i in range(4)]
        tmpo = [f32t(f"tmpo{i}") for i in range(2)]
        mixo = [(bf16t if mix_bf16 else f32t)(f"mixo{i}") for i in range(2)]
        s_mo = [k.slot(f"{pfx}mo{i}") for i in range(2)]
        mo_ctr = [0]

        for cc in range(ncc):
            k.dma("pool", s_w, W[:], io["win"][cc], w=[W])
            k.op("dve", lambda e: e.memset(state[:], 0.0), w=[state])
            k.op("dve", lambda e: e.memset(xraw[:, 0:3], 0.0), w=[xraw])
            def load_h(t_):
                k.dma("sp", s_h[t_ % 2], hT[t_ % 2][:], hT_d[:, :, t_ * 512:(t_ + 1) * 512],
                      r=[k.dbuf((pfx + "hT", t_))], w=[hT[t_ % 2]])
            load_h(0)
            for tt in range(NT):
                tsl = slice(tt * 512, (tt + 1) * 512)
                h = hT[tt % 2]
                xc, r_t, ig, a_t, a2, u_t, hs, gl, ya, sga, rq, rk = dbl32[tt % 2]
                xcb, sqq, sqk = dbl16[tt % 2]

                def proj(pb, s_idx):
                    for kc in range(8):
                        _mm(k, pb[:], W[:, kc, s_idx * 128:(s_idx + 1) * 128], h[:, kc, :], kc == 0, kc == 7,
                            r=[W, h], w=[pb])
                pbq = bank()
                proj(pbq, 2)
                _act(k, sqq[:], pbq[:], AF.Square, r=[pbq], w=[sqq])
                pbk = bank()
                proj(pbk, 3)
                _act(k, sqk[:], pbk[:], AF.Square, r=[pbk], w=[sqk])
                pb = bank()
                proj(pb, 0)
                _act(k, xraw[:, 3:515], pb[:], AF.Copy, r=[pb], w=[xraw])
                pb = bank()
                proj(pb, 1)
                _act(k, gl[:], pb[:], AF.Gelu_apprx_tanh, r=[pb], w=[gl])
                cw = M_CW + cc * 4
                _ts(k, "dve", xc[:], xraw[:, 0:512], smp[:, cw:cw + 1], smp[:, M_CB + cc:M_CB + cc + 1],
                    ALU.mult, ALU.add, r=[xraw, smp], w=[xc])
                for tap in range(1, 4):
                    _stt(k, xc[:], xraw[:, tap:tap + 512], smp[:, cw + tap:cw + tap + 1], xc[:], ALU.mult, ALU.add,
                         r=[xraw, xc, smp], w=[xc])
                _copy(k, "pool", xcb[:], xc[:], r=[xc], w=[xcb])
                _copy(k, "pool", xraw[:, 0:3], xraw[:, 512:515], r=[xraw], w=[xraw])
                for (pbx, sqx, rx, gsc, dst) in ((pbq, sqq, rq, gqs[:, 0:1], qT),
                                                 (pbk, sqk, rk, smp[:, M_KG:M_KG + 1], kT)):
                    pbs = bank()
                    _mm(k, pbs[:], ones[:], sqx[:], True, True, r=[ones, sqx], w=[pbs])
                    _act(k, rx[:], pbs[:], AF.Sqrt, r=[pbs], w=[rx], scale=1.0 / 128.0, bias=EPS)
                    k.op("dve", lambda e, rx=rx: e.reciprocal(out=rx[:], in_=rx[:]), r=[rx], w=[rx])
                    _stt(k, dst[:, tsl], pbx[:], gsc, rx[:], ALU.mult, ALU.mult, r=[pbx, rx, gqs, smp], w=[dst])
                pb = bank()
                proj(pb, 5)
                _act(k, sga[:], pb[:], AF.Sigmoid, r=[pb], w=[sga])
                pb = bank()
                proj(pb, 6)
                _act(k, sgb[:, tsl], pb[:], AF.Sigmoid, r=[pb], w=[sgb])
                pbv = bank()
                for sub in range(4):
                    for kc in range(8):
                        _mm(k, pbv[:, sub * 128:(sub + 1) * 128], h[:, kc, sub * 128:(sub + 1) * 128],
                            W[:, kc, 4 * 128:5 * 128], kc == 0, kc == 7, r=[W, h], w=[pbv])
                _copy(k, "dve", vA[:, tt * 512:(tt + 1) * 512], pbv[:], r=[pbv], w=[vA])
                if tt + 1 < NT:
                    load_h(tt + 1)
                pbr = bank()
                _mm(k, pbr[:], wa[:, cc, :], xcb[:], True, True, r=[wa, xcb], w=[pbr])
                pbi = bank()
                _mm(k, pbi[:], wx[:, cc, :], xcb[:], True, True, r=[wx, xcb], w=[pbi])
                _act(k, r_t[:], pbr[:], AF.Sigmoid, r=[pbr, smp], w=[r_t], bias=smp[:, M_BA + cc:M_BA + cc + 1])
                _act(k, ig[:], pbi[:], AF.Sigmoid, r=[pbi, smp], w=[ig], bias=smp[:, M_BX + cc:M_BX + cc + 1])
                _act(k, a_t[:], r_t[:], AF.Exp, r=[r_t, negc], w=[a_t], scale=negc[:, cc:cc + 1])
                _act(k, a2[:], r_t[:], AF.Exp, r=[r_t, negc2], w=[a2], scale=negc2[:, cc:cc + 1])
                _act(k, a2[:], a2[:], AF.Sqrt, r=[a2], w=[a2], scale=-1.0, bias=1.0)
                _tt(k, "pool", u_t[:], ig[:], xc[:], ALU.mult, r=[ig, xc], w=[u_t])
                _tt(k, "pool", u_t[:], u_t[:], a2[:], ALU.mult, r=[u_t, a2], w=[u_t])
                k.op("dve", lambda e, hs=hs, a_t=a_t, u_t=u_t: e.tensor_tensor_scan(
                    out=hs[:], data0=a_t[:], data1=u_t[:], initial=state[:, 0:1], op0=ALU.mult, op1=ALU.add),
                     r=[a_t, u_t, state], w=[hs])
                _copy(k, "dve", state[:], hs[:, 511:512], r=[hs], w=[state])
                _tt(k, "pool", ya[:], hs[:], gl[:], ALU.mult, r=[hs, gl], w=[ya])
                _tt(k, "pool", mixA[:, tsl], ya[:], sga[:], ALU.mult, r=[ya, sga], w=[mixA])

            steps = []
            for qt in range(NT):
                topkb = 4 * qt + 3
                for kb in range(topkb, -1, -1):
                    steps.append(dict(i=len(steps), qt=qt, kb=kb, first=(kb == topkb), last=(kb == 0),
                                      diag=(kb - 4 * qt) if kb >= 4 * qt else None))
            ZP = [P[0], P[1]]
            ACC = [P[2], P[3]]
            OB = [P[4], P[5], P[6]]

            def stA(s):
                zp = ZP[s["i"] % 2]
                kb, qt = s["kb"], s["qt"]
                _mm(k, zp[:], kT[:, kb * 128:(kb + 1) * 128], qT[:, qt * 512:(qt + 1) * 512], True, True,
                    r=[kT, qT], w=[zp])

            def stB(s):
                zp = ZP[s["i"] % 2]
                e_ = EB[s["i"] % 4]
                L = LB[s["i"] % 4]
                _act(k, e_[:], zp[:], AF.Exp, r=[zp], w=[e_])
                if s["diag"] is not None:
                    dg = s["diag"]
                    _tt(k, "dve", e_[:], e_[:], masks[:, dg * 512:(dg + 1) * 512], ALU.mult, r=[e_, masks], w=[e_])
                _act(k, L[:], e_[:], AF.Ln, r=[e_], w=[L], bias=1.0)

            def stC(s):
                acc = ACC[s["qt"] % 2]
                L = LB[s["i"] % 4]
                _mm(k, acc[:], tri[:], L[:], s["first"], s["last"], r=[tri, L], w=[acc])

            def stD(s):
                acc = ACC[s["qt"] % 2]
                X = XB[s["i"] % 3]
                _act(k, X[:], acc[:], AF.Exp, r=[acc], w=[X], scale=-1.0)

            def stE(s):
                if s["last"]:
                    return
                acc = ACC[s["qt"] % 2]
                L = LB[s["i"] % 4]
                _mm(k, acc[:], trin[:], L[:], False, False, r=[trin, L], w=[acc])

            def stF(s):
                at = ATB[s["i"] % 4]
                _tt(k, "dve", at[:], EB[s["i"] % 4][:], XB[s["i"] % 3][:], ALU.mult,
                    r=[EB[s["i"] % 4], XB[s["i"] % 3]], w=[at])

            def stG(s):
                o = OB[s["qt"] % 3]
                at = ATB[s["i"] % 4]
                kb, qt = s["kb"], s["qt"]
                _mm(k, o[:], vA[:, kb * 128:(kb + 1) * 128], at[:], s["first"], s["last"], r=[vA, at], w=[o])
                if s["last"]:
                    qsl = slice(qt * 512, (qt + 1) * 512)
                    j = mo_ctr[0] % 2
                    mo_ctr[0] += 1
                    _tt(k, "dve", tmpo[j][:], o[:], sgb[:, qsl], ALU.mult, r=[o, sgb], w=[tmpo[j]])
                    _tt(k, "pool", mixo[j][:], tmpo[j][:], mixA[:, qsl], ALU.add, r=[tmpo[j], mixA], w=[mixo[j]])
                    k.dma("sp", s_mo[j], mix_d[cc * 128:(cc + 1) * 128, qsl], mixo[j][:], r=[mixo[j]],
                          w=[k.dbuf((pfx + "mix", cc, qt))])

            if "dbg_q" in io and cc == ncc - 1:
                s_dbg = k.slot(f"{pfx}dbg")
                k.dma("sp", s_dbg, io["dbg_small"][:, 0:16], modt[:], r=[modt])
                k.dma("sp", s_dbg, io["dbg_small"][:, 16:24], A1[:], r=[A1])
                for nm, t in (("dbg_q", qT), ("dbg_k", kT), ("dbg_v", vA), ("dbg_mixA", mixA), ("dbg_sgb", sgb)):
                    k.dma("sp", s_dbg, io[nm], t[:], r=[t])
            n = len(steps)
            for it in range(n + 3):
                if it < n:
                    stA(steps[it])
                    stB(steps[it])
                if 0 <= it - 2 < n:
                    stE(steps[it - 2])
                if 0 <= it - 1 < n:
                    stC(steps[it - 1])
                    stD(steps[it - 1])
                if 0 <= it - 2 < n:
                    stF(steps[it - 2])
                if 0 <= it - 3 < n:
                    stG(steps[it - 3])
        k.barrier()
    top.close()


def build_mixer(S, dbg=False):
    nc = bass.Bass("TRN2", target_bir_lowering=False)
    io = {}

    def inp(name, shape, dt=F32):
        io[name] = nc.dram_tensor(name, list(shape), dt, kind="ExternalInput").ap()
    inp("x", [S, D])
    inp("ident", [128, 128])
    inp("tri", [128, 128])
    inp("trin", [128, 128])
    inp("ones", [128, 128])
    inp("masks", [128, 2048])
    inp("smallp", [128, M_NP])
    inp("rgwa", [128, 4, 128])
    inp("rgwx", [128, 4, 128])
    inp("adaw", [4, 128, 8, 512])
    inp("win", [4, 128, 8, 896])
    io["mixT"] = nc.dram_tensor("mixT", [512, S], F32, kind="ExternalOutput").ap()
    if dbg:
        io["dbg_hT"] = nc.dram_tensor("dbg_hT", [128, 8, S], BF16, kind="ExternalOutput").ap()
        io["dbg_small"] = nc.dram_tensor("dbg_small", [128, 24], F32, kind="ExternalOutput").ap()
        for nm in ("dbg_q", "dbg_k", "dbg_v", "dbg_mixA", "dbg_sgb"):
            io[nm] = nc.dram_tensor(nm, [128, S], BF16, kind="ExternalOutput").ap()
    with contextlib.ExitStack() as st:
        k = K(nc, st)
        emit_mixer(k, nc, S, io)
        k.finish()
    return nc


def _pk(v):
    v = np.asarray(v, np.float32)
    return np.ascontiguousarray(v.reshape(-1, 128).T)


def _consts():
    p = np.arange(128)[:, None]
    c = np.arange(128)[None, :]
    tri = (p >= c).astype(np.float32)
    trin = (p < c).astype(np.float32)
    cc = np.arange(512)[None, :]
    masks = np.concatenate([((128 * i + p) < cc).astype(np.float32) for i in range(4)], axis=1)
    return dict(ident=np.eye(128, dtype=np.float32), tri=tri, trin=trin, ones=np.ones((128, 128), np.float32),
                masks=np.ascontiguousarray(masks))


def mixer_weights(l, b, hh, inp):
    if hh is None:
        ch, h0, ncc = slice(0, 1024), 0, 8
    else:
        ch, h0, ncc = slice(hh * 512, (hh + 1) * 512), hh * 4, 4
    d = {}
    ada_w, ada_b = inp["ada_w"][l], inp["ada_b"][l]
    smallp = np.zeros((128, M_NP), np.float32)
    smallp[:, M_CV:M_CV + 8] = _pk(inp["c"][b])
    smallp[:, M_BSH:M_BSH + 8] = _pk(ada_b[0:1024])
    smallp[:, M_BSC:M_BSC + 8] = _pk(ada_b[1024:2048])
    smallp[:, M_G1:M_G1 + 8] = _pk(inp["norm1_g"][l])
    cw = inp["conv_w"][l][:, ch]
    for cc in range(ncc):
        smallp[:, M_CW + cc * 4:M_CW + cc * 4 + 4] = cw[:, cc * 128:(cc + 1) * 128].T
    smallp[:, M_CB:M_CB + ncc] = _pk(inp["conv_b"][l][ch])
    smallp[:, M_BA:M_BA + ncc] = _pk(inp["rg_ba"][l][ch])
    smallp[:, M_BX:M_BX + ncc] = _pk(inp["rg_bx"][l][ch])
    smallp[:, M_LAM:M_LAM + ncc] = _pk(inp["rg_lambda"][l][ch])
    smallp[:, M_QG] = inp["q_norm_g"][l]
    smallp[:, M_KG] = inp["k_norm_g"][l]
    d["smallp"] = smallp
    d["rgwa"] = np.ascontiguousarray(inp["rg_wa"][l][h0:h0 + ncc].transpose(1, 0, 2))
    d["rgwx"] = np.ascontiguousarray(inp["rg_wx"][l][h0:h0 + ncc].transpose(1, 0, 2))
    aw = ada_w[:, 0:2048].reshape(8, 128, 4, 512).transpose(2, 1, 0, 3)
    d["adaw"] = np.ascontiguousarray(aw)
    w_in = inp["w_in"][l]
    w7 = w_in.reshape(8, 128, 7, 8, 128)[:, :, :, h0:h0 + ncc, :]
    d["win"] = np.ascontiguousarray(w7.transpose(3, 1, 0, 2, 4).reshape(ncc, 128, 8, 896))
    return d


def mixer_inputs(l, b, hh, xb, inp):
    d = dict(_consts())
    d["x"] = np.ascontiguousarray(xb, dtype=np.float32)
    d.update(mixer_weights(l, b, hh, inp))
    return d


_PROGS = {}


def _prog(kind, n):
    key = (kind, n)
    if key not in _PROGS:
        _PROGS[key] = build_mixer(n) if kind == "mixer" else build_ffn(n)
    return _PROGS[key]


FUSED = True


def kernel(**inputs):
    inp = {k_: np.asarray(v, dtype=np.float32) for k_, v in inputs.items()}
    x = inp["x"]
    B, S, _ = x.shape
    depth = inp["w_in"].shape[0]
    cores = list(range(8))
    if FUSED:
        key = ("fused", S, depth)
        if key not in _PROGS:
            _PROGS[key] = build_fused(S, depth)
        per_b = [fused_inputs(b, inp, depth) for b in range(B)]
        res = run_bass_kernel_spmd(_PROGS[key], [per_b[c // 2] for c in cores], core_ids=cores)
        half = S // 2
        return np.stack([np.concatenate([np.asarray(res.results[2 * b]["out"])[:half],
                                         np.asarray(res.results[2 * b + 1]["out"])[half:]], axis=0)
                         for b in range(B)], axis=0).astype(np.float32)
    for l in range(depth):
        nc = _prog("mixer", S)
        in_maps = [mixer_inputs(l, c // 2, c % 2, x[c // 2], inp) for c in cores]
        res = run_bass_kernel_spmd(nc, in_maps, core_ids=cores)
        mixT = [np.concatenate([np.asarray(res.results[2 * b]["mixT"]), np.asarray(res.results[2 * b + 1]["mixT"])],
                               axis=0) for b in range(B)]
        nc = _prog("ffn", S // 2)
        in_maps = [ffn_inputs(l, c // 2, c % 2, x[c // 2], mixT[c // 2], inp) for c in cores]
        res = run_bass_kernel_spmd(nc, in_maps, core_ids=cores)
        x = np.stack([np.concatenate([np.asarray(res.results[2 * b]["xout"]), np.asarray(res.results[2 * b + 1]["xout"])],
                                     axis=0) for b in range(B)], axis=0).astype(np.float32)
    return x


F_CV, F_BSH, F_BSC, F_G2, F_CW, F_CB, F_FLAG, F_NP = 0, 8, 16, 24, 32, 176, 224, 225


def emit_ffn(k, nc, S2, io, pfx="f", P=None, pre=True, mix_bf16=False):
    NT = S2 // 512
    x_d, mix_d, out_d = io["xin"], io["mixin"], io["xout"]
    if P is None:
        P = [k.ps(f"{pfx}ps{i}", [128, 512], F32) for i in range(8)]
    top = contextlib.ExitStack()
    bank_ctr = [0]

    def bank():
        b = P[4 + bank_ctr[0] % 4]
        bank_ctr[0] += 1
        return b

    ident = k.sb(f"{pfx}ident", [128, 128], BF16, stack=top)
    smp = k.sb(f"{pfx}smp", [128, F_NP], F32, stack=top)
    wo = k.sb(f"{pfx}wo", [128, 8, 1024], BF16, stack=top)
    wd = k.sb(f"{pfx}wd", [128, 24, 1024], BF16, stack=top)
    gtB = k.sb(f"{pfx}gtB", [128, 2048], F32, stack=top)
    modt = k.sb(f"{pfx}modt", [128, 16], F32, stack=top)
    A2 = k.sb(f"{pfx}A2", [128, 8], F32, stack=top)
    halo = k.sb(f"{pfx}halo", [128, 48, 2], F32, stack=top)
    s_c = k.slot(f"{pfx}const")
    s_c2 = k.slot(f"{pfx}const2")
    k.dma("sp", s_c2, smp[:], io["smallp"], w=[smp])
    k.dma("pool", s_c, ident[:], io["ident"], w=[ident])
    k.dma("pool", s_c, wo[:], io["wout"], w=[wo])
    for q in range(4):
        k.dma("pool", s_c, wd[:, q * 6:(q + 1) * 6, :], io["fdown"][:, q * 6:(q + 1) * 6, :], w=[wd])

    with contextlib.ExitStack() as ps:
        adaw_t = [k.sb(f"{pfx}adaw{i}", [128, 8, 512], F32, stack=ps) for i in range(2)]
        s_aw = [k.slot(f"{pfx}adaw{i}") for i in range(2)]
        cB = k.sb(f"{pfx}cB", [128, 8, 128], F32, stack=ps)
        onesf = k.sb(f"{pfx}onesf", [128, 128], F32, stack=ps)
        gtb_t = k.sb(f"{pfx}gtb_t", [128, 2048], F32, stack=ps)
        k.dma("sp", s_c2, gtb_t[:], io["gtb"].partition_broadcast(128), w=[gtb_t])
        k.op("dve", lambda e: e.memset(onesf[:], 1.0), w=[onesf])
        for kc in range(8):
            _ts(k, "dve", cB[:, kc, :], onesf[:], smp[:, F_CV + kc:F_CV + kc + 1], None, ALU.mult, None,
                r=[onesf, smp], w=[cB])
        pm = bank()
        for piece in range(4):
            at = adaw_t[piece % 2]
            k.dma("sp", s_aw[piece % 2], at[:], io["adaw"][piece], w=[at])
            for jj in range(4):
                j = piece * 4 + jj
                for kc in range(8):
                    _mm(k, pm[:, j:j + 1], at[:, kc, jj * 128:(jj + 1) * 128], smp[:, F_CV + kc:F_CV + kc + 1],
                        kc == 0, kc == 7, r=[at, smp], w=[pm])
        _tt(k, "dve", modt[:], pm[:, 0:16], smp[:, F_BSH:F_BSH + 16], ALU.add, r=[pm, smp], w=[modt])
        _ts(k, "dve", A2[:], modt[:, 8:16], 1.0, None, ALU.add, None, r=[modt], w=[A2])
        _tt(k, "dve", A2[:], A2[:], smp[:, F_G2:F_G2 + 8], ALU.mult, r=[A2, smp], w=[A2])
        for piece in range(4, 8):
            at = adaw_t[piece % 2]
            k.dma("sp", s_aw[piece % 2], at[:], io["adaw"][piece], w=[at])
            pb = bank()
            for kc in range(8):
                _mm(k, pb[:], cB[:, kc, :], at[:, kc, :], kc == 0, kc == 7, r=[cB, at], w=[pb])
            c0 = (piece - 4) * 512
            _tt(k, "dve", gtB[:, c0:c0 + 512], pb[:], gtb_t[:, c0:c0 + 512], ALU.add, r=[pb, gtb_t], w=[gtB])
        k.barrier()

    with contextlib.ExitStack() as ps:
        mx = k.sb(f"{pfx}mx", [128, 8, 512], BF16, stack=ps)
        s_mx = k.slot(f"{pfx}mx0")
        xt = [k.sb(f"{pfx}xt{i}", [128, 1024], F32, stack=ps) for i in range(2)]
        s_xt = [k.slot(f"{pfx}xt{i}") for i in range(2)]
        x1t = [[k.sb(f"{pfx}x1t{p_}{i}", [128, 1024], F32, stack=ps) for i in range(4)] for p_ in range(2)]
        s_xo = [[k.slot(f"{pfx}xo{p_}{i}") for i in range(4)] for p_ in range(2)]
        xh = [k.sb(f"{pfx}xh{i}", [128, 1024], BF16, stack=ps) for i in range(2)]
        ssq = [k.sb(f"{pfx}ssq{i}", [128, 1], F32, stack=ps) for i in range(2)]
        h2T = [k.sb(f"{pfx}h2T{i}", [128, 8, 512], BF16, stack=ps) for i in range(2)]
        Wj = [k.sb(f"{pfx}Wj{i}", [128, 8, 256], BF16, stack=ps) for i in range(3)]
        s_wj = [k.slot(f"{pfx}wj{i}") for i in range(3)]
        raw = [k.sb(f"{pfx}raw{i}", [128, 514], F32, stack=ps) for i in range(4)]
        tcv = [k.sb(f"{pfx}tcv{i}", [128, 512], F32, stack=ps) for i in range(4)]
        gg = [k.sb(f"{pfx}gg{i}", [128, 512], F32, stack=ps) for i in range(2)]
        actT = k.sb(f"{pfx}actT", [128, 24, 512], BF16, stack=ps)
        tmp = [k.sb(f"{pfx}tmp{i}", [128, 512], F32, stack=ps) for i in range(2)]
        ctr = dict(x=0, raw=0, t=0, g=0, tmp=0)
        mixv = mix_d.rearrange("(kc p) t -> p kc t", p=128)
        tiles = ([(0, 128, True, 0)] if pre else []) + \
            [((128 if pre else 0) + tt * 512, 512, False, tt * 512) for tt in range(NT)]
        wst = dict(issued=0, total=24 * len(tiles))
        tb = [P[i] for i in range(4)]
        tbv = [b_[:].bitcast(BF16) for b_ in tb]

        def issue_w(upto):
            while wst["issued"] <= upto and wst["issued"] < wst["total"]:
                g = wst["issued"]
                k.dma("pool", s_wj[g % 3], Wj[g % 3][:], io["fup"][g % 24], w=[Wj[g % 3]])
                wst["issued"] += 1

        def st1_load(ti):
            row0, ntok = tiles[ti][0], tiles[ti][1]
            k.dma("sp" if mix_bf16 else "pool", s_mx, mx[:, :, 0:ntok], mixv[:, :, row0:row0 + ntok], w=[mx])

        def st1_a(ti, sub):
            row0 = tiles[ti][0]
            xi = ctr["x"] % 2
            ctr["x"] += 1
            xx = xt[xi]
            k.dma("sp", s_xt[xi], xx[:], x_d[row0 + sub * 128:row0 + (sub + 1) * 128, :], w=[xx])
            x1 = x1t[ti % 2][sub]
            for half in range(2):
                hs_ = slice(half * 512, (half + 1) * 512)
                po = bank()
                for kc in range(8):
                    _mm(k, po[:], mx[:, kc, sub * 128:(sub + 1) * 128], wo[:, kc, hs_], kc == 0, kc == 7,
                        r=[mx, wo], w=[po])
                tp = tmp[ctr["tmp"] % 2]
                ctr["tmp"] += 1
                _tt(k, "dve", tp[:], po[:], gtB[:, hs_], ALU.mult, r=[po, gtB], w=[tp])
                _tt(k, "pool", x1[:, hs_], tp[:], xx[:, hs_], ALU.add, r=[tp, xx], w=[x1])
            sq = ssq[sub % 2]
            xht = xh[sub % 2]
            _act(k, xht[:], x1[:], AF.Square, r=[x1], w=[xht, sq], accum_out=sq[:])
            _act(k, sq[:], sq[:], AF.Sqrt, r=[sq], w=[sq], scale=1.0 / D, bias=EPS)
            k.op("dve", lambda e, sq=sq: e.reciprocal(out=sq[:], in_=sq[:]), r=[sq], w=[sq])
            _ts(k, "dve", xht[:, 0:512], x1[:, 0:512], sq[:, 0:1], None, ALU.mult, None, r=[x1, sq], w=[xht])
            _ts(k, "pool", xht[:, 512:1024], x1[:, 512:1024], sq[:, 0:1], None, ALU.mult, None, r=[x1, sq], w=[xht])

        def st1_b(ti, sub):
            xht = xh[sub % 2]
            for fc in range(8):
                bi = fc // 2
                c0 = (fc % 2) * 512 + sub * 128
                k.op("pe", lambda e, o=tbv[bi][:, c0:c0 + 128], i_=xht[:, fc * 128:(fc + 1) * 128]:
                     e.transpose(o, i_, ident[:]), r=[xht, ident], w=[tb[bi]])

        def st1_fin(ti):
            ntok = tiles[ti][1]
            h2 = h2T[ti % 2]
            for fc in range(8):
                bi = fc // 2
                c0 = (fc % 2) * 512
                _act(k, h2[:, fc, 0:ntok], tbv[bi][:, c0:c0 + ntok], AF.Identity, r=[tb[bi], A2, modt], w=[h2],
                     scale=A2[:, fc:fc + 1], bias=modt[:, fc:fc + 1])

        def up_chunk(ti, j):
            ntok, is_pre = tiles[ti][1], tiles[ti][2]
            h2 = h2T[ti % 2]
            g = ti * 24 + j
            issue_w(g + 2)
            wj = Wj[g % 3]
            res = []
            for br in range(2):
                cidx = br * 24 + j
                pb = bank()
                for kc in range(8):
                    _mm(k, pb[:, 0:ntok], wj[:, kc, br * 128:(br + 1) * 128], h2[:, kc, 0:ntok], kc == 0, kc == 7,
                        r=[wj, h2], w=[pb])
                if is_pre:
                    _ts(k, "dve", halo[:, cidx, :], pb[:, ntok - 2:ntok], smp[:, F_FLAG:F_FLAG + 1], None,
                        ALU.mult, None, r=[pb, smp], w=[halo])
                    continue
                rw = raw[ctr["raw"] % 4]
                ctr["raw"] += 1
                _act(k, rw[:, 2:2 + ntok], pb[:, 0:ntok], AF.Copy, r=[pb], w=[rw])
                _copy(k, "pool", rw[:, 0:2], halo[:, cidx, :], r=[halo], w=[rw])
                tc = tcv[ctr["t"] % 4]
                ctr["t"] += 1
                cw = F_CW + cidx * 3
                _ts(k, "dve", tc[:], rw[:, 0:512], smp[:, cw:cw + 1], smp[:, F_CB + cidx:F_CB + cidx + 1],
                    ALU.mult, ALU.add, r=[rw, smp], w=[tc])
                for tap in (1, 2):
                    _stt(k, tc[:], rw[:, tap:tap + 512], smp[:, cw + tap:cw + tap + 1], tc[:], ALU.mult, ALU.add,
                         r=[rw, tc, smp], w=[tc])
                _copy(k, "pool", halo[:, cidx, :], rw[:, 512:514], r=[rw], w=[halo])
                res.append(tc)
            if is_pre:
                return
            g_ = gg[ctr["g"] % 2]
            ctr["g"] += 1
            _act(k, g_[:], res[0][:], AF.Gelu_apprx_tanh, r=[res[0]], w=[g_])
            _tt(k, "pool", actT[:, j, :], g_[:], res[1][:], ALU.mult, r=[g_, res[1]], w=[actT])

        def down(ti):
            ntok, orow0 = tiles[ti][1], tiles[ti][3]
            for sub in range(ntok // 128):
                x1 = x1t[ti % 2][sub]
                for half in range(2):
                    hs_ = slice(half * 512, (half + 1) * 512)
                    pd = bank()
                    for j in range(24):
                        _mm(k, pd[:], actT[:, j, sub * 128:(sub + 1) * 128], wd[:, j, hs_], j == 0, j == 23,
                            r=[actT, wd], w=[pd])
                    tp = tmp[ctr["tmp"] % 2]
                    ctr["tmp"] += 1
                    _tt(k, "dve", tp[:], pd[:], gtB[:, 1024 + half * 512:1024 + (half + 1) * 512], ALU.mult,
                        r=[pd, gtB], w=[tp])
                    _tt(k, "pool", x1[:, hs_], tp[:], x1[:, hs_], ALU.add, r=[tp, x1], w=[x1])
                r0 = orow0 + sub * 128
                k.dma("sp", s_xo[ti % 2][sub], out_d[r0:r0 + 128, :], x1[:], r=[x1], w=[k.dbuf((pfx + "out", r0))])

        if not pre:
            k.op("dve", lambda e: e.memset(halo[:], 0.0), w=[halo])
        nt_all = len(tiles)
        st1_load(0)
        for sub in range(tiles[0][1] // 128):
            st1_a(0, sub)
            st1_b(0, sub)
        st1_fin(0)
        for ti in range(nt_all):
            issue_w(ti * 24 + 1)
            sched = {}
            if ti + 1 < nt_all:
                st1_load(ti + 1)
                for sub in range(tiles[ti + 1][1] // 128):
                    sched.setdefault(1 + 5 * sub, []).append(lambda ti=ti, sub=sub: st1_a(ti + 1, sub))
                    sched.setdefault(4 + 5 * sub, []).append(lambda ti=ti, sub=sub: st1_b(ti + 1, sub))
                sched.setdefault(22, []).append(lambda ti=ti: st1_fin(ti + 1))
            for j in range(24):
                up_chunk(ti, j)
                for f_ in sched.get(j, []):
                    f_()
            if not tiles[ti][2]:
                down(ti)
        k.barrier()
    top.close()


def build_ffn(S2):
    nc = bass.Bass("TRN2", target_bir_lowering=False)
    io = {}

    def inp(name, shape, dt=F32):
        io[name] = nc.dram_tensor(name, list(shape), dt, kind="ExternalInput").ap()
    inp("xin", [128 + S2, D])
    inp("mixin", [D, 128 + S2])
    inp("ident", [128, 128])
    inp("smallp", [128, F_NP])
    inp("gtb", [2048])
    inp("adaw", [8, 128, 8, 512])
    inp("wout", [128, 8, 1024])
    inp("fup", [24, 128, 8, 256])
    inp("fdown", [128, 24, 1024])
    io["xout"] = nc.dram_tensor("xout", [S2, D], F32, kind="ExternalOutput").ap()
    with contextlib.ExitStack() as st:
        k = K(nc, st)
        emit_ffn(k, nc, S2, io)
        k.finish()
    return nc


def ffn_weights(l, b, flag, inp):
    d = {}
    ada_w, ada_b = inp["ada_w"][l], inp["ada_b"][l]
    smallp = np.zeros((128, F_NP), np.float32)
    smallp[:, F_CV:F_CV + 8] = _pk(inp["c"][b])
    smallp[:, F_BSH:F_BSH + 8] = _pk(ada_b[3072:4096])
    smallp[:, F_BSC:F_BSC + 8] = _pk(ada_b[4096:5120])
    smallp[:, F_G2:F_G2 + 8] = _pk(inp["norm2_g"][l])
    cw = inp["ffn_conv_w"][l]
    smallp[:, F_CW:F_CW + 144] = cw.reshape(3, 48, 128).transpose(2, 1, 0).reshape(128, 144)
    smallp[:, F_CB:F_CB + 48] = _pk(inp["ffn_conv_b"][l])
    smallp[:, F_FLAG] = flag
    d["smallp"] = smallp
    d["gtb"] = np.ascontiguousarray(np.concatenate([ada_b[2048:3072], ada_b[5120:6144]]))
    cols = np.concatenate([np.arange(3072, 5120), np.arange(2048, 3072), np.arange(5120, 6144)])
    aw = ada_w[:, cols].reshape(8, 128, 8, 512).transpose(2, 1, 0, 3)
    d["adaw"] = np.ascontiguousarray(aw)
    d["wout"] = np.ascontiguousarray(inp["w_out"][l].reshape(8, 128, 1024).transpose(1, 0, 2))
    fu = inp["ffn_up"][l].reshape(8, 128, 2, 24, 128)
    d["fup"] = np.ascontiguousarray(fu.transpose(3, 1, 0, 2, 4).reshape(24, 128, 8, 256))
    d["fdown"] = np.ascontiguousarray(inp["ffn_down"][l].reshape(24, 128, 1024).transpose(1, 0, 2))
    return d


def ffn_inputs(l, b, th, xb, mixTb, inp):
    S = xb.shape[0]
    S2 = S // 2
    t0 = th * S2
    p0 = t0 - 128 if th > 0 else 0
    d = dict(ident=np.eye(128, dtype=np.float32))
    d["xin"] = np.ascontiguousarray(np.concatenate([xb[p0:p0 + 128], xb[t0:t0 + S2]], axis=0), dtype=np.float32)
    d["mixin"] = np.ascontiguousarray(np.concatenate([mixTb[:, p0:p0 + 128], mixTb[:, t0:t0 + S2]], axis=1),
                                      dtype=np.float32)
    d.update(ffn_weights(l, b, 1.0 if th > 0 else 0.0, inp))
    return d


_MW = ("smallp", "rgwa", "rgwx", "adaw", "win")
_FW = ("smallp", "gtb", "adaw", "wout", "fup", "fdown")


def build_fused(S, depth=2, skip=()):
    nc = bass.Bass("TRN2", target_bir_lowering=False)
    ext = {}

    def inp(name, shape, dt=F32):
        ext[name] = nc.dram_tensor(name, list(shape), dt, kind="ExternalInput").ap()
    inp("x", [S, D])
    for n_ in ("ident", "tri", "trin", "ones"):
        inp(n_, [128, 128])
    inp("masks", [128, 2048])
    for l in range(depth):
        inp(f"m{l}_smallp", [128, M_NP])
        inp(f"m{l}_rgwa", [128, 8, 128])
        inp(f"m{l}_rgwx", [128, 8, 128])
        inp(f"m{l}_adaw", [4, 128, 8, 512])
        inp(f"m{l}_win", [8, 128, 8, 896])
        inp(f"f{l}_smallp", [128, F_NP])
        inp(f"f{l}_gtb", [2048])
        inp(f"f{l}_adaw", [8, 128, 8, 512])
        inp(f"f{l}_wout", [128, 8, 1024])
        inp(f"f{l}_fup", [24, 128, 8, 256])
        inp(f"f{l}_fdown", [128, 24, 1024])
    out = nc.dram_tensor("out", [S, D], F32, kind="ExternalOutput").ap()
    mixT_d = nc.dram_tensor("mixT_d", [D, S], BF16).ap()
    xmid = [nc.dram_tensor(f"xmid{l}", [S, D], F32).ap() for l in range(depth - 1)]
    with contextlib.ExitStack() as st:
        k = K(nc, st)
        P = [k.ps(f"ps{i}", [128, 512], F32) for i in range(8)]
        for l in range(depth):
            x_in = ext["x"] if l == 0 else xmid[l - 1]
            x_out = out if l == depth - 1 else xmid[l]
            io = dict(x=x_in, mixT=mixT_d, ident=ext["ident"], tri=ext["tri"], trin=ext["trin"], ones=ext["ones"],
                      masks=ext["masks"])
            for n_ in _MW:
                io[n_] = ext[f"m{l}_{n_}"]
            if f"m{l}" not in skip:
                emit_mixer(k, nc, S, io, pfx=f"m{l}", P=P, ncc=8, mix_bf16=True)
            io = dict(xin=x_in, mixin=mixT_d, xout=x_out, ident=ext["ident"])
            for n_ in _FW:
                io[n_] = ext[f"f{l}_{n_}"]
            if f"f{l}" not in skip:
                emit_ffn(k, nc, S, io, pfx=f"f{l}", P=P, pre=False, mix_bf16=True)
        k.finish()
    return nc


def fused_inputs(b, inp, depth):
    d = dict(_consts())
    d["x"] = np.ascontiguousarray(inp["x"][b], dtype=np.float32)
    for l in range(depth):
        for n_, v in mixer_weights(l, b, None, inp).items():
            d[f"m{l}_{n_}"] = v
        for n_, v in ffn_weights(l, b, 0.0, inp).items():
            d[f"f{l}_{n_}"] = v
    return d
```

```python
import contextlib
import re
import numpy as np
import concourse.bass as bass
import concourse.mybir as mybir
from concourse.bass_utils import run_bass_kernel_spmd

F32 = mybir.dt.float32
BF16 = mybir.dt.bfloat16
AF = mybir.ActivationFunctionType
ALU = mybir.AluOpType

D = 1024
NH = 8
EPS = 1e-6
DFF = 3072


class _Eng:
    def __init__(self, name, sem, is_pe=False):
        self.name = name
        self.sem = sem
        self.count = 0
        self.waited = {}
        self.prog = []
        self.is_pe = is_pe
        self.is_slot = False
        self.pending = []


class _Slot:
    def __init__(self, name, sem):
        self.name = name
        self.sem = sem
        self.count = 0
        self.is_slot = True


class Buf:
    __slots__ = ("last_w", "readers", "name")

    def __init__(self, name=""):
        self.last_w = None
        self.readers = {}
        self.name = name


class TT:
    def __init__(self, t, name):
        self.t = t
        self.buf = Buf(name)

    def __getitem__(self, key):
        return self.t[key]


class K:
    def __init__(self, nc, stack):
        self.nc = nc
        self.stack = stack
        self.engs = {}
        for name in ("pe", "act", "dve", "pool", "sp"):
            sem = stack.enter_context(nc.semaphore("sem_" + name))
            self.engs[name] = _Eng(name, sem, is_pe=(name == "pe"))
        self.slots = []
        self._slot_cache = {}
        self.dbufs = {}
        self.n_inst = 0

    def slot(self, name):
        key = re.sub(r"^([mf])\d", r"\1", name)
        if key in self._slot_cache:
            return self._slot_cache[key]
        sem = self.stack.enter_context(self.nc.semaphore("slot_" + key))
        s = _Slot(key, sem)
        self.slots.append(s)
        self._slot_cache[key] = s
        return s

    def sb(self, name, shape, dtype, stack=None):
        st = stack if stack is not None else self.stack
        t = st.enter_context(self.nc.sbuf_tensor(name, list(shape), dtype))
        return TT(t, name)

    def ps(self, name, shape, dtype, stack=None):
        st = stack if stack is not None else self.stack
        t = st.enter_context(self.nc.psum_tensor(name, list(shape), dtype))
        return TT(t, name)

    def dbuf(self, key):
        b = self.dbufs.get(key)
        if b is None:
            b = Buf(str(key))
            self.dbufs[key] = b
        return b

    @staticmethod
    def _b(x):
        return x.buf if isinstance(x, TT) else x

    def _waits(self, E, reads, writes):
        deps = {}

        def add(obj, val):
            if deps.get(obj, 0) < val:
                deps[obj] = val
        for b in reads:
            b = self._b(b)
            if b.last_w is not None:
                add(*b.last_w)
        for b in writes:
            b = self._b(b)
            if b.last_w is not None:
                add(*b.last_w)
            for o, v in b.readers.items():
                add(o, v)
        waits = []
        for obj, val in deps.items():
            if obj.is_slot:
                val = obj.count
            elif obj is E:
                if E.is_pe:
                    continue
                if E.count - val >= 2:
                    continue
            if E.waited.get(obj, 0) >= val:
                continue
            E.waited[obj] = val
            waits.append((obj.sem, val))
        return waits

    def barrier(self):
        for E in self.engs.values():
            for o in list(self.engs.values()) + self.slots:
                if o is E or o.count == 0:
                    continue
                if E.waited.get(o, 0) >= o.count:
                    continue
                E.waited[o] = o.count
                E.pending.append((o.sem, o.count))

    def op(self, eng, fn, r=(), w=()):
        E = self.engs[eng]
        waits = self._waits(E, r, w)
        waits = E.pending + waits
        E.pending = []
        E.count += 1
        idx = E.count
        E.prog.append((waits, fn, (E.sem, 1)))
        for b in r:
            b = self._b(b)
            if b.readers.get(E, 0) < idx:
                b.readers[E] = idx
        for b in w:
            b = self._b(b)
            b.last_w = (E, idx)
            b.readers = {}
        self.n_inst += 1

    def dma(self, q, slot, out, in_, r=(), w=()):
        E = self.engs[q]
        waits = self._waits(E, r, w)
        waits = E.pending + waits
        E.pending = []
        slot.count += 16
        val = slot.count
        E.prog.append((waits, lambda e, out=out, in_=in_: e.dma_start(out=out, in_=in_), (slot.sem, 16)))
        for b in r:
            b = self._b(b)
            b.readers[slot] = val
        for b in w:
            b = self._b(b)
            b.last_w = (slot, val)
            b.readers = {}
        self.n_inst += 1

    def finish(self):
        E = self.engs["sp"]
        final_waits = [(s.sem, s.count) for s in self.slots if s.count > 0]
        for n in ("pe", "act", "dve", "pool"):
            e2 = self.engs[n]
            if e2.count > 0:
                final_waits.append((e2.sem, e2.count))
        progs = {n: e.prog for n, e in self.engs.items()}
        nc = self.nc
        with nc.Block() as block:
            def run(e, prog, extra=()):
                for waits, fn, inc in prog:
                    for sem, val in waits:
                        e.wait_ge(sem, val)
                    ins = fn(e)
                    ins.then_inc(inc[0], inc[1])
                for sem, val in extra:
                    e.wait_ge(sem, val)

            @block.sync
            def _(e):
                run(e, progs["sp"], final_waits)

            @block.tensor
            def _(e):
                run(e, progs["pe"])

            @block.scalar
            def _(e):
                run(e, progs["act"])

            @block.vector
            def _(e):
                run(e, progs["dve"])

            @block.gpsimd
            def _(e):
                run(e, progs["pool"])


def _act(k, out, in_, func, r, w, **kw):
    k.op("act", lambda e: e.activation(out=out, in_=in_, func=func, **kw), r=r, w=w)


def _mm(k, out, lhsT, rhs, start, stop, r, w):
    k.op("pe", lambda e: e.matmul(out, lhsT=lhsT, rhs=rhs, start=start, stop=stop), r=r, w=w)


def _tt(k, eng, out, in0, in1, op, r, w):
    k.op(eng, lambda e: e.tensor_tensor(out=out, in0=in0, in1=in1, op=op), r=r, w=w)


def _ts(k, eng, out, in0, s1, s2, op0, op1, r, w):
    if op1 is None:
        k.op(eng, lambda e: e.tensor_scalar(out=out, in0=in0, scalar1=s1, scalar2=None, op0=op0), r=r, w=w)
    else:
        k.op(eng, lambda e: e.tensor_scalar(out=out, in0=in0, scalar1=s1, scalar2=s2, op0=op0, op1=op1), r=r, w=w)


def _stt(k, out, in0, scalar, in1, op0, op1, r, w):
    k.op("dve", lambda e: e.scalar_tensor_tensor(out=out, in0=in0, scalar=scalar, in1=in1, op0=op0, op1=op1),
         r=r, w=w)


def _copy(k, eng, out, in_, r, w):
    k.op(eng, lambda e: e.tensor_copy(out=out, in_=in_), r=r, w=w)


M_CV, M_BSH, M_BSC, M_G1, M_CW, M_CB, M_BA, M_BX, M_LAM, M_QG, M_KG, M_NP = 0, 8, 16, 24, 32, 64, 72, 80, 88, 96, 97, 98


def emit_mixer(k, nc, S, io, pfx="m", P=None, ncc=4, mix_bf16=False):
    NT = S // 512
    NB = S // 128
    x_d, mix_d = io["x"], io["mixT"]
    if P is None:
        P = [k.ps(f"{pfx}ps{i}", [128, 512], F32) for i in range(8)]
    top = contextlib.ExitStack()
    bank_ctr = [0]

    def bank():
        b = P[bank_ctr[0] % 8]
        bank_ctr[0] += 1
        return b

    hT_d = io["dbg_hT"] if "dbg_hT" in io else nc.dram_tensor(f"{pfx}_hT_d", [128, 8, S], BF16).ap()

    ident = k.sb(f"{pfx}ident", [128, 128], BF16, stack=top)
    tri = k.sb(f"{pfx}tri", [128, 128], BF16, stack=top)
    trin = k.sb(f"{pfx}trin", [128, 128], BF16, stack=top)
    ones = k.sb(f"{pfx}ones", [128, 128], BF16, stack=top)
    masks = k.sb(f"{pfx}masks", [128, 4 * 512], F32, stack=top)
    smp = k.sb(f"{pfx}smp", [128, M_NP], F32, stack=top)
    wa = k.sb(f"{pfx}wa", [128, ncc, 128], BF16, stack=top)
    wx = k.sb(f"{pfx}wx", [128, ncc, 128], BF16, stack=top)
    s_c = k.slot(f"{pfx}const")
    s_c2 = k.slot(f"{pfx}const2")
    k.dma("pool", s_c, ident[:], io["ident"], w=[ident])
    k.dma("pool", s_c, tri[:], io["tri"], w=[tri])
    k.dma("pool", s_c, trin[:], io["trin"], w=[trin])
    k.dma("pool", s_c, ones[:], io["ones"], w=[ones])
    k.dma("pool", s_c, wa[:], io["rgwa"], w=[wa])
    k.dma("pool", s_c, wx[:], io["rgwx"], w=[wx])
    k.dma("sp", s_c2, masks[:], io["masks"], w=[masks])
    k.dma("sp", s_c2, smp[:], io["smallp"], w=[smp])

    modt = k.sb(f"{pfx}modt", [128, 16], F32, stack=top)
    A1 = k.sb(f"{pfx}A1", [128, 8], F32, stack=top)
    negc = k.sb(f"{pfx}negc", [128, 8], F32, stack=top)
    negc2 = k.sb(f"{pfx}negc2", [128, 8], F32, stack=top)
    t4 = k.sb(f"{pfx}t4", [128, 8], F32, stack=top)
    gqs = k.sb(f"{pfx}gqs", [128, 1], F32, stack=top)

    with contextlib.ExitStack() as ps:
        adaw_t = [k.sb(f"{pfx}adaw{i}", [128, 8, 512], F32, stack=ps) for i in range(2)]
        s_aw = [k.slot(f"{pfx}adaw{i}") for i in range(2)]
        pm = bank()
        for piece in range(4):
            at = adaw_t[piece % 2]
            k.dma("sp", s_aw[piece % 2], at[:], io["adaw"][piece], w=[at])
            for jj in range(4):
                j = piece * 4 + jj
                for kc in range(8):
                    _mm(k, pm[:, j:j + 1], at[:, kc, jj * 128:(jj + 1) * 128], smp[:, M_CV + kc:M_CV + kc + 1],
                        kc == 0, kc == 7, r=[at, smp], w=[pm])
        _tt(k, "dve", modt[:], pm[:, 0:16], smp[:, M_BSH:M_BSH + 16], ALU.add, r=[pm, smp], w=[modt])
        _ts(k, "dve", A1[:], modt[:, 8:16], 1.0, None, ALU.add, None, r=[modt], w=[A1])
        _tt(k, "dve", A1[:], A1[:], smp[:, M_G1:M_G1 + 8], ALU.mult, r=[A1, smp], w=[A1])
        _act(k, t4[:], smp[:, M_LAM:M_LAM + 8], AF.Exp, r=[smp], w=[t4], scale=-1.0)
        _act(k, t4[:], t4[:], AF.Ln, r=[t4], w=[t4], bias=1.0)
        _ts(k, "dve", negc[:], t4[:], -8.0, None, ALU.mult, None, r=[t4], w=[negc])
        _ts(k, "dve", negc2[:], t4[:], -16.0, None, ALU.mult, None, r=[t4], w=[negc2])
        _ts(k, "dve", gqs[:], smp[:, M_QG:M_QG + 1], 1.0 / float(np.sqrt(128.0)), None, ALU.mult, None,
            r=[smp], w=[gqs])
        k.barrier()

    with contextlib.ExitStack() as ps:
        xb = [k.sb(f"{pfx}xb{i}", [128, 1024], F32, stack=ps) for i in range(3)]
        s_x = [k.slot(f"{pfx}x{i}") for i in range(3)]
        xh = [k.sb(f"{pfx}xh{i}", [128, 1024], BF16, stack=ps) for i in range(2)]
        junk = k.sb(f"{pfx}junk", [128, 1024], BF16, stack=ps)
        ssq = [k.sb(f"{pfx}ssq{i}", [128, 1], F32, stack=ps) for i in range(2)]
        hTo = [k.sb(f"{pfx}hTo{i}", [128, 8, 512], BF16, stack=ps) for i in range(2)]
        s_ho = [k.slot(f"{pfx}ho{i}") for i in range(2)]
        for tt in range(NT):
            banks = [P[(tt % 2) * 4 + i] for i in range(4)]
            bviews = [b[:].bitcast(BF16) for b in banks]
            for sub in range(4):
                blk = tt * 4 + sub
                xt = xb[blk % 3]
                k.dma("sp", s_x[blk % 3], xt[:], x_d[blk * 128:(blk + 1) * 128, :], w=[xt])
                sq = ssq[blk % 2]
                _act(k, junk[:], xt[:], AF.Square, r=[xt], w=[junk, sq], accum_out=sq[:])
                _act(k, sq[:], sq[:], AF.Sqrt, r=[sq], w=[sq], scale=1.0 / D, bias=EPS)
                k.op("dve", lambda e, sq=sq: e.reciprocal(out=sq[:], in_=sq[:]), r=[sq], w=[sq])
                xht = xh[blk % 2]
                _ts(k, "dve", xht[:, 0:512], xt[:, 0:512], sq[:, 0:1], None, ALU.mult, None, r=[xt, sq], w=[xht])
                _ts(k, "pool", xht[:, 512:1024], xt[:, 512:1024], sq[:, 0:1], None, ALU.mult, None, r=[xt, sq], w=[xht])
                for fc in range(8):
                    bi = fc // 2
                    c0 = (fc % 2) * 512 + sub * 128
                    k.op("pe", lambda e, o=bviews[bi][:, c0:c0 + 128], i_=xht[:, fc * 128:(fc + 1) * 128]:
                         e.transpose(o, i_, ident[:]), r=[xht, ident], w=[banks[bi]])
            ho = hTo[tt % 2]
            for fc in range(8):
                bi = fc // 2
                c0 = (fc % 2) * 512
                _act(k, ho[:, fc, :], bviews[bi][:, c0:c0 + 512], AF.Identity, r=[banks[bi], A1, modt], w=[ho],
                     scale=A1[:, fc:fc + 1], bias=modt[:, fc:fc + 1])
            k.dma("sp", s_ho[tt % 2], hT_d[:, :, tt * 512:(tt + 1) * 512], ho[:], r=[ho],
                  w=[k.dbuf((pfx + "hT", tt))])
        k.barrier()

    with contextlib.ExitStack() as ps:
        W = k.sb(f"{pfx}W", [128, 8, 896], BF16, stack=ps)
        s_w = k.slot(f"{pfx}w")
        hT = [k.sb(f"{pfx}hT{i}", [128, 8, 512], BF16, stack=ps) for i in range(2)]
        s_h = [k.slot(f"{pfx}h{i}") for i in range(2)]
        qT = k.sb(f"{pfx}qT", [128, S], BF16, stack=ps)
        kT = k.sb(f"{pfx}kT", [128, S], BF16, stack=ps)
        vA = k.sb(f"{pfx}vA", [128, NB * 128], BF16, stack=ps)
        mixA = k.sb(f"{pfx}mixA", [128, S], BF16, stack=ps)
        sgb = k.sb(f"{pfx}sgb", [128, S], BF16, stack=ps)
        xraw = k.sb(f"{pfx}xraw", [128, 515], F32, stack=ps)
        state = k.sb(f"{pfx}state", [128, 1], F32, stack=ps)

        def f32t(n):
            return k.sb(f"{pfx}{n}", [128, 512], F32, stack=ps)

        def bf16t(n):
            return k.sb(f"{pfx}{n}", [128, 512], BF16, stack=ps)
        names32 = ("xc", "r", "ig", "a", "a2", "u", "hs", "gl", "ya", "sga", "rq", "rk")
        dbl32 = [[f32t(f"{n}{i}") for n in names32] for i in range(2)]
        dbl16 = [[bf16t(f"{n}{i}") for n in ("xcb", "sqq", "sqk")] for i in range(2)]
        EB = [f32t(f"e{i}") for i in range(4)]
        LB = [bf16t(f"L{i}") for i in range(4)]
        XB = [f32t(f"X{i}") for i in range(3)]
        ATB = [bf16t(f"at{i}") for i in range(4)]
        tmpo = [f32t(f"tmpo{i}") for i in range(2)]
        mixo = [(bf16t if mix_bf16 else f32t)(f"mixo{i}") for i in range(2)]
        s_mo = [k.slot(f"{pfx}mo{i}") for i in range(2)]
        mo_ctr = [0]

        for cc in range(ncc):
            k.dma("pool", s_w, W[:], io["win"][cc], w=[W])
            k.op("dve", lambda e: e.memset(state[:], 0.0), w=[state])
            k.op("dve", lambda e: e.memset(xraw[:, 0:3], 0.0), w=[xraw])
            def load_h(t_):
                k.dma("sp", s_h[t_ % 2], hT[t_ % 2][:], hT_d[:, :, t_ * 512:(t_ + 1) * 512],
                      r=[k.dbuf((pfx + "hT", t_))], w=[hT[t_ % 2]])
            load_h(0)
            for tt in range(NT):
                tsl = slice(tt * 512, (tt + 1) * 512)
                h = hT[tt % 2]
                xc, r_t, ig, a_t, a2, u_t, hs, gl, ya, sga, rq, rk = dbl32[tt % 2]
                xcb, sqq, sqk = dbl16[tt % 2]

                def proj(pb, s_idx):
                    for kc in range(8):
                        _mm(k, pb[:], W[:, kc, s_idx * 128:(s_idx + 1) * 128], h[:, kc, :], kc == 0, kc == 7,
                            r=[W, h], w=[pb])
                pbq = bank()
                proj(pbq, 2)
                _act(k, sqq[:], pbq[:], AF.Square, r=[pbq], w=[sqq])
                pbk = bank()
                proj(pbk, 3)
                _act(k, sqk[:], pbk[:], AF.Square, r=[pbk], w=[sqk])
                pb = bank()
                proj(pb, 0)
                _act(k, xraw[:, 3:515], pb[:], AF.Copy, r=[pb], w=[xraw])
                pb = bank()
                proj(pb, 1)
                _act(k, gl[:], pb[:], AF.Gelu_apprx_tanh, r=[pb], w=[gl])
                cw = M_CW + cc * 4
                _ts(k, "dve", xc[:], xraw[:, 0:512], smp[:, cw:cw + 1], smp[:, M_CB + cc:M_CB + cc + 1],
                    ALU.mult, ALU.add, r=[xraw, smp], w=[xc])
                for tap in range(1, 4):
                    _stt(k, xc[:], xraw[:, tap:tap + 512], smp[:, cw + tap:cw + tap + 1], xc[:], ALU.mult, ALU.add,
                         r=[xraw, xc, smp], w=[xc])
                _copy(k, "pool", xcb[:], xc[:], r=[xc], w=[xcb])
                _copy(k, "pool", xraw[:, 0:3], xraw[:, 512:515], r=[xraw], w=[xraw])
                for (pbx, sqx, rx, gsc, dst) in ((pbq, sqq, rq, gqs[:, 0:1], qT),
                                                 (pbk, sqk, rk, smp[:, M_KG:M_KG + 1], kT)):
                    pbs = bank()
                    _mm(k, pbs[:], ones[:], sqx[:], True, True, r=[ones, sqx], w=[pbs])
                    _act(k, rx[:], pbs[:], AF.Sqrt, r=[pbs], w=[rx], scale=1.0 / 128.0, bias=EPS)
                    k.op("dve", lambda e, rx=rx: e.reciprocal(out=rx[:], in_=rx[:]), r=[rx], w=[rx])
                    _stt(k, dst[:, tsl], pbx[:], gsc, rx[:], ALU.mult, ALU.mult, r=[pbx, rx, gqs, smp], w=[dst])
                pb = bank()
                proj(pb, 5)
                _act(k, sga[:], pb[:], AF.Sigmoid, r=[pb], w=[sga])
                pb = bank()
                proj(pb, 6)
                _act(k, sgb[:, tsl], pb[:], AF.Sigmoid, r=[pb], w=[sgb])
                pbv = bank()
                for sub in range(4):
                    for kc in range(8):
                        _mm(k, pbv[:, sub * 128:(sub + 1) * 128], h[:, kc, sub * 128:(sub + 1) * 128],
                            W[:, kc, 4 * 128:5 * 128], kc == 0, kc == 7, r=[W, h], w=[pbv])
                _copy(k, "dve", vA[:, tt * 512:(tt + 1) * 512], pbv[:], r=[pbv], w=[vA])
                if tt + 1 < NT:
                    load_h(tt + 1)
                pbr = bank()
                _mm(k, pbr[:], wa[:, cc, :], xcb[:], True, True, r=[wa, xcb], w=[pbr])
                pbi = bank()
                _mm(k, pbi[:], wx[:, cc, :], xcb[:], True, True, r=[wx, xcb], w=[pbi])
                _act(k, r_t[:], pbr[:], AF.Sigmoid, r=[pbr, smp], w=[r_t], bias=smp[:, M_BA + cc:M_BA + cc + 1])
                _act(k, ig[:], pbi[:], AF.Sigmoid, r=[pbi, smp], w=[ig], bias=smp[:, M_BX + cc:M_BX + cc + 1])
                _act(k, a_t[:], r_t[:], AF.Exp, r=[r_t, negc], w=[a_t], scale=negc[:, cc:cc + 1])
                _act(k, a2[:], r_t[:], AF.Exp, r=[r_t, negc2], w=[a2], scale=negc2[:, cc:cc + 1])
                _act(k, a2[:], a2[:], AF.Sqrt, r=[a2], w=[a2], scale=-1.0, bias=1.0)
                _tt(k, "pool", u_t[:], ig[:], xc[:], ALU.mult, r=[ig, xc], w=[u_t])
                _tt(k, "pool", u_t[:], u_t[:], a2[:], ALU.mult, r=[u_t, a2], w=[u_t])
                k.op("dve", lambda e, hs=hs, a_t=a_t, u_t=u_t: e.tensor_tensor_scan(
                    out=hs[:], data0=a_t[:], data1=u_t[:], initial=state[:, 0:1], op0=ALU.mult, op1=ALU.add),
                     r=[a_t, u_t, state], w=[hs])
                _copy(k, "dve", state[:], hs[:, 511:512], r=[hs], w=[state])
                _tt(k, "pool", ya[:], hs[:], gl[:], ALU.mult, r=[hs, gl], w=[ya])
                _tt(k, "pool", mixA[:, tsl], ya[:], sga[:], ALU.mult, r=[ya, sga], w=[mixA])

            steps = []
            for qt in range(NT):
                topkb = 4 * qt + 3
                for kb in range(topkb, -1, -1):
                    steps.append(dict(i=len(steps), qt=qt, kb=kb, first=(kb == topkb), last=(kb == 0),
                                      diag=(kb - 4 * qt) if kb >= 4 * qt else None))
            ZP = [P[0], P[1]]
            ACC = [P[2], P[3]]
            OB = [P[4], P[5], P[6]]

            def stA(s):
                zp = ZP[s["i"] % 2]
                kb, qt = s["kb"], s["qt"]
                _mm(k, zp[:], kT[:, kb * 128:(kb + 1) * 128], qT[:, qt * 512:(qt + 1) * 512], True, True,
                    r=[kT, qT], w=[zp])

            def stB(s):
                zp = ZP[s["i"] % 2]
                e_ = EB[s["i"] % 4]
                L = LB[s["i"] % 4]
                _act(k, e_[:], zp[:], AF.Exp, r=[zp], w=[e_])
                if s["diag"] is not None:
                    dg = s["diag"]
                    _tt(k, "dve", e_[:], e_[:], masks[:, dg * 512:(dg + 1) * 512], ALU.mult, r=[e_, masks], w=[e_])
                _act(k, L[:], e_[:], AF.Ln, r=[e_], w=[L], bias=1.0)

            def stC(s):
                acc = ACC[s["qt"] % 2]
                L = LB[s["i"] % 4]
                _mm(k, acc[:], tri[:], L[:], s["first"], s["last"], r=[tri, L], w=[acc])

            def stD(s):
                acc = ACC[s["qt"] % 2]
                X = XB[s["i"] % 3]
                _act(k, X[:], acc[:], AF.Exp, r=[acc], w=[X], scale=-1.0)

            def stE(s):
                if s["last"]:
                    return
                acc = ACC[s["qt"] % 2]
                L = LB[s["i"] % 4]
                _mm(k, acc[:], trin[:], L[:], False, False, r=[trin, L], w=[acc])

            def stF(s):
                at = ATB[s["i"] % 4]
                _tt(k, "dve", at[:], EB[s["i"] % 4][:], XB[s["i"] % 3][:], ALU.mult,
                    r=[EB[s["i"] % 4], XB[s["i"] % 3]], w=[at])

            def stG(s):
                o = OB[s["qt"] % 3]
                at = ATB[s["i"] % 4]
                kb, qt = s["kb"], s["qt"]
                _mm(k, o[:], vA[:, kb * 128:(kb + 1) * 128], at[:], s["first"], s["last"], r=[vA, at], w=[o])
                if s["last"]:
                    qsl = slice(qt * 512, (qt + 1) * 512)
                    j = mo_ctr[0] % 2
                    mo_ctr[0] += 1
                    _tt(k, "dve", tmpo[j][:], o[:], sgb[:, qsl], ALU.mult, r=[o, sgb], w=[tmpo[j]])
                    _tt(k, "pool", mixo[j][:], tmpo[j][:], mixA[:, qsl], ALU.add, r=[tmpo[j], mixA], w=[mixo[j]])
                    k.dma("sp", s_mo[j], mix_d[cc * 128:(cc + 1) * 128, qsl], mixo[j][:], r=[mixo[j]],
                          w=[k.dbuf((pfx + "mix", cc, qt))])

            if "dbg_q" in io and cc == ncc - 1:
                s_dbg = k.slot(f"{pfx}dbg")
                k.dma("sp", s_dbg, io["dbg_small"][:, 0:16], modt[:], r=[modt])
                k.dma("sp", s_dbg, io["dbg_small"][:, 16:24], A1[:], r=[A1])
                for nm, t in (("dbg_q", qT), ("dbg_k", kT), ("dbg_v", vA), ("dbg_mixA", mixA), ("dbg_sgb", sgb)):
                    k.dma("sp", s_dbg, io[nm], t[:], r=[t])
            n = len(steps)
            for it in range(n + 3):
                if it < n:
                    stA(steps[it])
                    stB(steps[it])
                if 0 <= it - 2 < n:
                    stE(steps[it - 2])
                if 0 <= it - 1 < n:
                    stC(steps[it - 1])
                    stD(steps[it - 1])
                if 0 <= it - 2 < n:
                    stF(steps[it - 2])
                if 0 <= it - 3 < n:
                    stG(steps[it - 3])
        k.barrier()
    top.close()


def build_mixer(S, dbg=False):
    nc = bass.Bass("TRN2", target_bir_lowering=False)
    io = {}

    def inp(name, shape, dt=F32):
        io[name] = nc.dram_tensor(name, list(shape), dt, kind="ExternalInput").ap()
    inp("x", [S, D])
    inp("ident", [128, 128])
    inp("tri", [128, 128])
    inp("trin", [128, 128])
    inp("ones", [128, 128])
    inp("masks", [128, 2048])
    inp("smallp", [128, M_NP])
    inp("rgwa", [128, 4, 128])
    inp("rgwx", [128, 4, 128])
    inp("adaw", [4, 128, 8, 512])
    inp("win", [4, 128, 8, 896])
    io["mixT"] = nc.dram_tensor("mixT", [512, S], F32, kind="ExternalOutput").ap()
    if dbg:
        io["dbg_hT"] = nc.dram_tensor("dbg_hT", [128, 8, S], BF16, kind="ExternalOutput").ap()
        io["dbg_small"] = nc.dram_tensor("dbg_small", [128, 24], F32, kind="ExternalOutput").ap()
        for nm in ("dbg_q", "dbg_k", "dbg_v", "dbg_mixA", "dbg_sgb"):
            io[nm] = nc.dram_tensor(nm, [128, S], BF16, kind="ExternalOutput").ap()
    with contextlib.ExitStack() as st:
        k = K(nc, st)
        emit_mixer(k, nc, S, io)
        k.finish()
    return nc


def _pk(v):
    v = np.asarray(v, np.float32)
    return np.ascontiguousarray(v.reshape(-1, 128).T)


def _consts():
    p = np.arange(128)[:, None]
    c = np.arange(128)[None, :]
    tri = (p >= c).astype(np.float32)
    trin = (p < c).astype(np.float32)
    cc = np.arange(512)[None, :]
    masks = np.concatenate([((128 * i + p) < cc).astype(np.float32) for i in range(4)], axis=1)
    return dict(ident=np.eye(128, dtype=np.float32), tri=tri, trin=trin, ones=np.ones((128, 128), np.float32),
                masks=np.ascontiguousarray(masks))


def mixer_weights(l, b, hh, inp):
    if hh is None:
        ch, h0, ncc = slice(0, 1024), 0, 8
    else:
        ch, h0, ncc = slice(hh * 512, (hh + 1) * 512), hh * 4, 4
    d = {}
    ada_w, ada_b = inp["ada_w"][l], inp["ada_b"][l]
    smallp = np.zeros((128, M_NP), np.float32)
    smallp[:, M_CV:M_CV + 8] = _pk(inp["c"][b])
    smallp[:, M_BSH:M_BSH + 8] = _pk(ada_b[0:1024])
    smallp[:, M_BSC:M_BSC + 8] = _pk(ada_b[1024:2048])
    smallp[:, M_G1:M_G1 + 8] = _pk(inp["norm1_g"][l])
    cw = inp["conv_w"][l][:, ch]
    for cc in range(ncc):
        smallp[:, M_CW + cc * 4:M_CW + cc * 4 + 4] = cw[:, cc * 128:(cc + 1) * 128].T
    smallp[:, M_CB:M_CB + ncc] = _pk(inp["conv_b"][l][ch])
    smallp[:, M_BA:M_BA + ncc] = _pk(inp["rg_ba"][l][ch])
    smallp[:, M_BX:M_BX + ncc] = _pk(inp["rg_bx"][l][ch])
    smallp[:, M_LAM:M_LAM + ncc] = _pk(inp["rg_lambda"][l][ch])
    smallp[:, M_QG] = inp["q_norm_g"][l]
    smallp[:, M_KG] = inp["k_norm_g"][l]
    d["smallp"] = smallp
    d["rgwa"] = np.ascontiguousarray(inp["rg_wa"][l][h0:h0 + ncc].transpose(1, 0, 2))
    d["rgwx"] = np.ascontiguousarray(inp["rg_wx"][l][h0:h0 + ncc].transpose(1, 0, 2))
    aw = ada_w[:, 0:2048].reshape(8, 128, 4, 512).transpose(2, 1, 0, 3)
    d["adaw"] = np.ascontiguousarray(aw)
    w_in = inp["w_in"][l]
    w7 = w_in.reshape(8, 128, 7, 8, 128)[:, :, :, h0:h0 + ncc, :]
    d["win"] = np.ascontiguousarray(w7.transpose(3, 1, 0, 2, 4).reshape(ncc, 128, 8, 896))
    return d


def mixer_inputs(l, b, hh, xb, inp):
    d = dict(_consts())
    d["x"] = np.ascontiguousarray(xb, dtype=np.float32)
    d.update(mixer_weights(l, b, hh, inp))
    return d


_PROGS = {}


def _prog(kind, n):
    key = (kind, n)
    if key not in _PROGS:
        _PROGS[key] = build_mixer(n) if kind == "mixer" else build_ffn(n)
    return _PROGS[key]


FUSED = True


def kernel(**inputs):
    inp = {k_: np.asarray(v, dtype=np.float32) for k_, v in inputs.items()}
    x = inp["x"]
    B, S, _ = x.shape
    depth = inp["w_in"].shape[0]
    cores = list(range(8))
    if FUSED:
        key = ("fused", S, depth)
        if key not in _PROGS:
            _PROGS[key] = build_fused(S, depth)
        per_b = [fused_inputs(b, inp, depth) for b in range(B)]
        res = run_bass_kernel_spmd(_PROGS[key], [per_b[c // 2] for c in cores], core_ids=cores)
        half = S // 2
        return np.stack([np.concatenate([np.asarray(res.results[2 * b]["out"])[:half],
                                         np.asarray(res.results[2 * b + 1]["out"])[half:]], axis=0)
                         for b in range(B)], axis=0).astype(np.float32)
    for l in range(depth):
        nc = _prog("mixer", S)
        in_maps = [mixer_inputs(l, c // 2, c % 2, x[c // 2], inp) for c in cores]
        res = run_bass_kernel_spmd(nc, in_maps, core_ids=cores)
        mixT = [np.concatenate([np.asarray(res.results[2 * b]["mixT"]), np.asarray(res.results[2 * b + 1]["mixT"])],
                               axis=0) for b in range(B)]
        nc = _prog("ffn", S // 2)
        in_maps = [ffn_inputs(l, c // 2, c % 2, x[c // 2], mixT[c // 2], inp) for c in cores]
        res = run_bass_kernel_spmd(nc, in_maps, core_ids=cores)
        x = np.stack([np.concatenate([np.asarray(res.results[2 * b]["xout"]), np.asarray(res.results[2 * b + 1]["xout"])],
                                     axis=0) for b in range(B)], axis=0).astype(np.float32)
    return x


F_CV, F_BSH, F_BSC, F_G2, F_CW, F_CB, F_FLAG, F_NP = 0, 8, 16, 24, 32, 176, 224, 225


def emit_ffn(k, nc, S2, io, pfx="f", P=None, pre=True, mix_bf16=False):
    NT = S2 // 512
    x_d, mix_d, out_d = io["xin"], io["mixin"], io["xout"]
    if P is None:
        P = [k.ps(f"{pfx}ps{i}", [128, 512], F32) for i in range(8)]
    top = contextlib.ExitStack()
    bank_ctr = [0]

    def bank():
        b = P[4 + bank_ctr[0] % 4]
        bank_ctr[0] += 1
        return b

    ident = k.sb(f"{pfx}ident", [128, 128], BF16, stack=top)
    smp = k.sb(f"{pfx}smp", [128, F_NP], F32, stack=top)
    wo = k.sb(f"{pfx}wo", [128, 8, 1024], BF16, stack=top)
    wd = k.sb(f"{pfx}wd", [128, 24, 1024], BF16, stack=top)
    gtB = k.sb(f"{pfx}gtB", [128, 2048], F32, stack=top)
    modt = k.sb(f"{pfx}modt", [128, 16], F32, stack=top)
    A2 = k.sb(f"{pfx}A2", [128, 8], F32, stack=top)
    halo = k.sb(f"{pfx}halo", [128, 48, 2], F32, stack=top)
    s_c = k.slot(f"{pfx}const")
    s_c2 = k.slot(f"{pfx}const2")
    k.dma("sp", s_c2, smp[:], io["smallp"], w=[smp])
    k.dma("pool", s_c, ident[:], io["ident"], w=[ident])
    k.dma("pool", s_c, wo[:], io["wout"], w=[wo])
    for q in range(4):
        k.dma("pool", s_c, wd[:, q * 6:(q + 1) * 6, :], io["fdown"][:, q * 6:(q + 1) * 6, :], w=[wd])

    with contextlib.ExitStack() as ps:
        adaw_t = [k.sb(f"{pfx}adaw{i}", [128, 8, 512], F32, stack=ps) for i in range(2)]
        s_aw = [k.slot(f"{pfx}adaw{i}") for i in range(2)]
        cB = k.sb(f"{pfx}cB", [128, 8, 128], F32, stack=ps)
        onesf = k.sb(f"{pfx}onesf", [128, 128], F32, stack=ps)
        gtb_t = k.sb(f"{pfx}gtb_t", [128, 2048], F32, stack=ps)
        k.dma("sp", s_c2, gtb_t[:], io["gtb"].partition_broadcast(128), w=[gtb_t])
        k.op("dve", lambda e: e.memset(onesf[:], 1.0), w=[onesf])
        for kc in range(8):
            _ts(k, "dve", cB[:, kc, :], onesf[:], smp[:, F_CV + kc:F_CV + kc + 1], None, ALU.mult, None,
                r=[onesf, smp], w=[cB])
        pm = bank()
        for piece in range(4):
            at = adaw_t[piece % 2]
            k.dma("sp", s_aw[piece % 2], at[:], io["adaw"][piece], w=[at])
            for jj in range(4):
                j = piece * 4 + jj
                for kc in range(8):
                    _mm(k, pm[:, j:j + 1], at[:, kc, jj * 128:(jj + 1) * 128], smp[:, F_CV + kc:F_CV + kc + 1],
                        kc == 0, kc == 7, r=[at, smp], w=[pm])
        _tt(k, "dve", modt[:], pm[:, 0:16], smp[:, F_BSH:F_BSH + 16], ALU.add, r=[pm, smp], w=[modt])
        _ts(k, "dve", A2[:], modt[:, 8:16], 1.0, None, ALU.add, None, r=[modt], w=[A2])
        _tt(k, "dve", A2[:], A2[:], smp[:, F_G2:F_G2 + 8], ALU.mult, r=[A2, smp], w=[A2])
        for piece in range(4, 8):
            at = adaw_t[piece % 2]
            k.dma("sp", s_aw[piece % 2], at[:], io["adaw"][piece], w=[at])
            pb = bank()
            for kc in range(8):
                _mm(k, pb[:], cB[:, kc, :], at[:, kc, :], kc == 0, kc == 7, r=[cB, at], w=[pb])
            c0 = (piece - 4) * 512
            _tt(k, "dve", gtB[:, c0:c0 + 512], pb[:], gtb_t[:, c0:c0 + 512], ALU.add, r=[pb, gtb_t], w=[gtB])
        k.barrier()

    with contextlib.ExitStack() as ps:
        mx = k.sb(f"{pfx}mx", [128, 8, 512], BF16, stack=ps)
        s_mx = k.slot(f"{pfx}mx0")
        xt = [k.sb(f"{pfx}xt{i}", [128, 1024], F32, stack=ps) for i in range(2)]
        s_xt = [k.slot(f"{pfx}xt{i}") for i in range(2)]
        x1t = [[k.sb(f"{pfx}x1t{p_}{i}", [128, 1024], F32, stack=ps) for i in range(4)] for p_ in range(2)]
        s_xo = [[k.slot(f"{pfx}xo{p_}{i}") for i in range(4)] for p_ in range(2)]
        xh = [k.sb(f"{pfx}xh{i}", [128, 1024], BF16, stack=ps) for i in range(2)]
        ssq = [k.sb(f"{pfx}ssq{i}", [128, 1], F32, stack=ps) for i in range(2)]
        h2T = [k.sb(f"{pfx}h2T{i}", [128, 8, 512], BF16, stack=ps) for i in range(2)]
        Wj = [k.sb(f"{pfx}Wj{i}", [128, 8, 256], BF16, stack=ps) for i in range(3)]
        s_wj = [k.slot(f"{pfx}wj{i}") for i in range(3)]
        raw = [k.sb(f"{pfx}raw{i}", [128, 514], F32, stack=ps) for i in range(4)]
        tcv = [k.sb(f"{pfx}tcv{i}", [128, 512], F32, stack=ps) for i in range(4)]
        gg = [k.sb(f"{pfx}gg{i}", [128, 512], F32, stack=ps) for i in range(2)]
        actT = k.sb(f"{pfx}actT", [128, 24, 512], BF16, stack=ps)
        tmp = [k.sb(f"{pfx}tmp{i}", [128, 512], F32, stack=ps) for i in range(2)]
        tmp1 = [k.sb(f"{pfx}tmpa{i}", [128, 512], F32, stack=ps) for i in range(2)]
        ctr = dict(x=0, raw=0, t=0, g=0, tmp=0, tmp1=0)
        mixv = mix_d.rearrange("(kc p) t -> p kc t", p=128)
        tiles = ([(0, 128, True, 0)] if pre else []) + \
            [((128 if pre else 0) + tt * 512, 512, False, tt * 512) for tt in range(NT)]
        wst = dict(issued=0, total=24 * len(tiles))
        tb = [P[i] for i in range(4)]
        tbv = [b_[:].bitcast(BF16) for b_ in tb]

        def issue_w(upto):
            while wst["issued"] <= upto and wst["issued"] < wst["total"]:
                g = wst["issued"]
                k.dma("pool", s_wj[g % 3], Wj[g % 3][:], io["fup"][g % 24], w=[Wj[g % 3]])
                wst["issued"] += 1

        def st1_load(ti):
            row0, ntok = tiles[ti][0], tiles[ti][1]
            k.dma("sp" if mix_bf16 else "pool", s_mx, mx[:, :, 0:ntok], mixv[:, :, row0:row0 + ntok], w=[mx])

        st1x = {}

        def st1_a0(ti, sub):
            row0 = tiles[ti][0]
            xi = ctr["x"] % 2
            ctr["x"] += 1
            xx = xt[xi]
            k.dma("sp", s_xt[xi], xx[:], x_d[row0 + sub * 128:row0 + (sub + 1) * 128, :], w=[xx])
            tps = []
            for half in range(2):
                hs_ = slice(half * 512, (half + 1) * 512)
                po = bank()
                for kc in range(8):
                    _mm(k, po[:], mx[:, kc, sub * 128:(sub + 1) * 128], wo[:, kc, hs_], kc == 0, kc == 7,
                        r=[mx, wo], w=[po])
                tp = tmp1[ctr["tmp1"] % 2]
                ctr["tmp1"] += 1
                _tt(k, "dve", tp[:], po[:], gtB[:, hs_], ALU.mult, r=[po, gtB], w=[tp])
                tps.append(tp)
            st1x[(ti, sub)] = (xx, tps)

        def st1_a1(ti, sub):
            xx, tps = st1x[(ti, sub)]
            x1 = x1t[ti % 2][sub]
            for half in range(2):
                hs_ = slice(half * 512, (half + 1) * 512)
                _tt(k, "pool", x1[:, hs_], tps[half][:], xx[:, hs_], ALU.add, r=[tps[half], xx], w=[x1])

        def st1_a2(ti, sub):
            x1 = x1t[ti % 2][sub]
            sq = ssq[sub % 2]
            xht = xh[sub % 2]
            _act(k, xht[:], x1[:], AF.Square, r=[x1], w=[xht, sq], accum_out=sq[:])
            _act(k, sq[:], sq[:], AF.Sqrt, r=[sq], w=[sq], scale=1.0 / D, bias=EPS)

        def st1_a3(ti, sub):
            x1 = x1t[ti % 2][sub]
            sq = ssq[sub % 2]
            xht = xh[sub % 2]
            k.op("dve", lambda e, sq=sq: e.reciprocal(out=sq[:], in_=sq[:]), r=[sq], w=[sq])
            _ts(k, "dve", xht[:, 0:512], x1[:, 0:512], sq[:, 0:1], None, ALU.mult, None, r=[x1, sq], w=[xht])
            _ts(k, "pool", xht[:, 512:1024], x1[:, 512:1024], sq[:, 0:1], None, ALU.mult, None, r=[x1, sq], w=[xht])

        def st1_a(ti, sub):
            st1_a0(ti, sub)
            st1_a1(ti, sub)
            st1_a2(ti, sub)
            st1_a3(ti, sub)

        def st1_b(ti, sub):
            xht = xh[sub % 2]
            for fc in range(8):
                bi = fc // 2
                c0 = (fc % 2) * 512 + sub * 128
                k.op("pe", lambda e, o=tbv[bi][:, c0:c0 + 128], i_=xht[:, fc * 128:(fc + 1) * 128]:
                     e.transpose(o, i_, ident[:]), r=[xht, ident], w=[tb[bi]])

        def st1_fin(ti, fcs=range(8)):
            ntok = tiles[ti][1]
            h2 = h2T[ti % 2]
            for fc in fcs:
                bi = fc // 2
                c0 = (fc % 2) * 512
                _act(k, h2[:, fc, 0:ntok], tbv[bi][:, c0:c0 + ntok], AF.Identity, r=[tb[bi], A2, modt], w=[h2],
                     scale=A2[:, fc:fc + 1], bias=modt[:, fc:fc + 1])

        pend = []

        def up_chunk(ti, j):
            ntok, is_pre = tiles[ti][1], tiles[ti][2]
            h2 = h2T[ti % 2]
            g = ti * 24 + j
            issue_w(g + 2)
            wj = Wj[g % 3]
            res = []
            for br in range(2):
                cidx = br * 24 + j
                pb = bank()
                for kc in range(8):
                    _mm(k, pb[:, 0:ntok], wj[:, kc, br * 128:(br + 1) * 128], h2[:, kc, 0:ntok], kc == 0, kc == 7,
                        r=[wj, h2], w=[pb])
                if is_pre:
                    _ts(k, "dve", halo[:, cidx, :], pb[:, ntok - 2:ntok], smp[:, F_FLAG:F_FLAG + 1], None,
                        ALU.mult, None, r=[pb, smp], w=[halo])
                    continue
                rw = raw[ctr["raw"] % 4]
                ctr["raw"] += 1
                _act(k, rw[:, 2:2 + ntok], pb[:, 0:ntok], AF.Copy, r=[pb], w=[rw])
                _copy(k, "pool", rw[:, 0:2], halo[:, cidx, :], r=[halo], w=[rw])
                tc = tcv[ctr["t"] % 4]
                ctr["t"] += 1
                cw = F_CW + cidx * 3
                _act(k, tc[:], rw[:, 0:512], AF.Identity, r=[rw, smp], w=[tc], scale=smp[:, cw:cw + 1],
                     bias=smp[:, F_CB + cidx:F_CB + cidx + 1])
                for tap in (1, 2):
                    _stt(k, tc[:], rw[:, tap:tap + 512], smp[:, cw + tap:cw + tap + 1], tc[:], ALU.mult, ALU.add,
                         r=[rw, tc, smp], w=[tc])
                _copy(k, "pool", halo[:, cidx, :], rw[:, 512:514], r=[rw], w=[halo])
                res.append(tc)
            if is_pre:
                return
            pend.append((j, res))

        def up_tail():
            if not pend:
                return
            j, res = pend.pop(0)
            g_ = gg[ctr["g"] % 2]
            ctr["g"] += 1
            _act(k, g_[:], res[0][:], AF.Gelu_apprx_tanh, r=[res[0]], w=[g_])
            _tt(k, "dve" if j % 3 == 2 else "pool", actT[:, j, :], g_[:], res[1][:], ALU.mult, r=[g_, res[1]], w=[actT])

        def down(ti):
            ntok, orow0 = tiles[ti][1], tiles[ti][3]
            for sub in range(ntok // 128):
                x1 = x1t[ti % 2][sub]
                for half in range(2):
                    hs_ = slice(half * 512, (half + 1) * 512)
                    pd = bank()
                    for j in range(24):
                        _mm(k, pd[:], actT[:, j, sub * 128:(sub + 1) * 128], wd[:, j, hs_], j == 0, j == 23,
                            r=[actT, wd], w=[pd])
                    tp = tmp[ctr["tmp"] % 2]
                    ctr["tmp"] += 1
                    _tt(k, "dve", tp[:], pd[:], gtB[:, 1024 + half * 512:1024 + (half + 1) * 512], ALU.mult,
                        r=[pd, gtB], w=[tp])
                    _tt(k, "pool", x1[:, hs_], tp[:], x1[:, hs_], ALU.add, r=[tp, x1], w=[x1])
                r0 = orow0 + sub * 128
                k.dma("sp", s_xo[ti % 2][sub], out_d[r0:r0 + 128, :], x1[:], r=[x1], w=[k.dbuf((pfx + "out", r0))])

        if not pre:
            k.op("dve", lambda e: e.memset(halo[:], 0.0), w=[halo])
        nt_all = len(tiles)
        st1_load(0)
        for sub in range(tiles[0][1] // 128):
            st1_a(0, sub)
            st1_b(0, sub)
        st1_fin(0)
        for ti in range(nt_all):
            issue_w(ti * 24 + 1)
            sched = {}
            if ti + 1 < nt_all:
                st1_load(ti + 1)
                for sub in range(tiles[ti + 1][1] // 128):
                    j0 = 1 + 5 * sub
                    for dj, fn_ in enumerate((st1_a0, st1_a1, st1_a2, st1_a3, st1_b)):
                        sched.setdefault(j0 + dj, []).append(lambda ti=ti, sub=sub, fn_=fn_: fn_(ti + 1, sub))
                for q_ in range(3):
                    fcs = (range(0, 3), range(3, 6), range(6, 8))[q_]
                    sched.setdefault(21 + q_, []).append(lambda ti=ti, fcs=fcs: st1_fin(ti + 1, fcs))
            for j in range(24):
                up_chunk(ti, j)
                if j >= 1:
                    up_tail()
                for f_ in sched.get(j, []):
                    f_()
            up_tail()
            if not tiles[ti][2]:
                down(ti)
        k.barrier()
    top.close()


def build_ffn(S2):
    nc = bass.Bass("TRN2", target_bir_lowering=False)
    io = {}

    def inp(name, shape, dt=F32):
        io[name] = nc.dram_tensor(name, list(shape), dt, kind="ExternalInput").ap()
    inp("xin", [128 + S2, D])
    inp("mixin", [D, 128 + S2])
    inp("ident", [128, 128])
    inp("smallp", [128, F_NP])
    inp("gtb", [2048])
    inp("adaw", [8, 128, 8, 512])
    inp("wout", [128, 8, 1024])
    inp("fup", [24, 128, 8, 256])
    inp("fdown", [128, 24, 1024])
    io["xout"] = nc.dram_tensor("xout", [S2, D], F32, kind="ExternalOutput").ap()
    with contextlib.ExitStack() as st:
        k = K(nc, st)
        emit_ffn(k, nc, S2, io)
        k.finish()
    return nc


def ffn_weights(l, b, flag, inp):
    d = {}
    ada_w, ada_b = inp["ada_w"][l], inp["ada_b"][l]
    smallp = np.zeros((128, F_NP), np.float32)
    smallp[:, F_CV:F_CV + 8] = _pk(inp["c"][b])
    smallp[:, F_BSH:F_BSH + 8] = _pk(ada_b[3072:4096])
    smallp[:, F_BSC:F_BSC + 8] = _pk(ada_b[4096:5120])
    smallp[:, F_G2:F_G2 + 8] = _pk(inp["norm2_g"][l])
    cw = inp["ffn_conv_w"][l]
    smallp[:, F_CW:F_CW + 144] = cw.reshape(3, 48, 128).transpose(2, 1, 0).reshape(128, 144)
    smallp[:, F_CB:F_CB + 48] = _pk(inp["ffn_conv_b"][l])
    smallp[:, F_FLAG] = flag
    d["smallp"] = smallp
    d["gtb"] = np.ascontiguousarray(np.concatenate([ada_b[2048:3072], ada_b[5120:6144]]))
    cols = np.concatenate([np.arange(3072, 5120), np.arange(2048, 3072), np.arange(5120, 6144)])
    aw = ada_w[:, cols].reshape(8, 128, 8, 512).transpose(2, 1, 0, 3)
    d["adaw"] = np.ascontiguousarray(aw)
    d["wout"] = np.ascontiguousarray(inp["w_out"][l].reshape(8, 128, 1024).transpose(1, 0, 2))
    fu = inp["ffn_up"][l].reshape(8, 128, 2, 24, 128)
    d["fup"] = np.ascontiguousarray(fu.transpose(3, 1, 0, 2, 4).reshape(24, 128, 8, 256))
    d["fdown"] = np.ascontiguousarray(inp["ffn_down"][l].reshape(24, 128, 1024).transpose(1, 0, 2))
    return d


def ffn_inputs(l, b, th, xb, mixTb, inp):
    S = xb.shape[0]
    S2 = S // 2
    t0 = th * S2
    p0 = t0 - 128 if th > 0 else 0
    d = dict(ident=np.eye(128, dtype=np.float32))
    d["xin"] = np.ascontiguousarray(np.concatenate([xb[p0:p0 + 128], xb[t0:t0 + S2]], axis=0), dtype=np.float32)
    d["mixin"] = np.ascontiguousarray(np.concatenate([mixTb[:, p0:p0 + 128], mixTb[:, t0:t0 + S2]], axis=1),
                                      dtype=np.float32)
    d.update(ffn_weights(l, b, 1.0 if th > 0 else 0.0, inp))
    return d


_MW = ("smallp", "rgwa", "rgwx", "adaw", "win")
_FW = ("smallp", "gtb", "adaw", "wout", "fup", "fdown")


def build_fused(S, depth=2, skip=()):
    nc = bass.Bass("TRN2", target_bir_lowering=False)
    ext = {}

    def inp(name, shape, dt=F32):
        ext[name] = nc.dram_tensor(name, list(shape), dt, kind="ExternalInput").ap()
    inp("x", [S, D])
    for n_ in ("ident", "tri", "trin", "ones"):
        inp(n_, [128, 128])
    inp("masks", [128, 2048])
    for l in range(depth):
        inp(f"m{l}_smallp", [128, M_NP])
        inp(f"m{l}_rgwa", [128, 8, 128])
        inp(f"m{l}_rgwx", [128, 8, 128])
        inp(f"m{l}_adaw", [4, 128, 8, 512])
        inp(f"m{l}_win", [8, 128, 8, 896])
        inp(f"f{l}_smallp", [128, F_NP])
        inp(f"f{l}_gtb", [2048])
        inp(f"f{l}_adaw", [8, 128, 8, 512])
        inp(f"f{l}_wout", [128, 8, 1024])
        inp(f"f{l}_fup", [24, 128, 8, 256])
        inp(f"f{l}_fdown", [128, 24, 1024])
    out = nc.dram_tensor("out", [S, D], F32, kind="ExternalOutput").ap()
    mixT_d = nc.dram_tensor("mixT_d", [D, S], BF16).ap()
    xmid = [nc.dram_tensor(f"xmid{l}", [S, D], F32).ap() for l in range(depth - 1)]
    with contextlib.ExitStack() as st:
        k = K(nc, st)
        P = [k.ps(f"ps{i}", [128, 512], F32) for i in range(8)]
        for l in range(depth):
            x_in = ext["x"] if l == 0 else xmid[l - 1]
            x_out = out if l == depth - 1 else xmid[l]
            io = dict(x=x_in, mixT=mixT_d, ident=ext["ident"], tri=ext["tri"], trin=ext["trin"], ones=ext["ones"],
                      masks=ext["masks"])
            for n_ in _MW:
                io[n_] = ext[f"m{l}_{n_}"]
            if f"m{l}" not in skip:
                emit_mixer(k, nc, S, io, pfx=f"m{l}", P=P, ncc=8, mix_bf16=True)
            io = dict(xin=x_in, mixin=mixT_d, xout=x_out, ident=ext["ident"])
            for n_ in _FW:
                io[n_] = ext[f"f{l}_{n_}"]
            if f"f{l}" not in skip:
                emit_ffn(k, nc, S, io, pfx=f"f{l}", P=P, pre=False, mix_bf16=True)
        k.finish()
    return nc


def fused_inputs(b, inp, depth):
    d = dict(_consts())
    d["x"] = np.ascontiguousarray(inp["x"][b], dtype=np.float32)
    for l in range(depth):
        for n_, v in mixer_weights(l, b, None, inp).items():
            d[f"m{l}_{n_}"] = v
        for n_, v in ffn_weights(l, b, 0.0, inp).items():
            d[f"f{l}_{n_}"] = v
    return d
```

```python
import contextlib
import re
import numpy as np
import concourse.bass as bass
import concourse.mybir as mybir
from concourse.bass_utils import run_bass_kernel_spmd

F32 = mybir.dt.float32
BF16 = mybir.dt.bfloat16
AF = mybir.ActivationFunctionType
ALU = mybir.AluOpType

D = 1024
NH = 8
EPS = 1e-6
DFF = 3072


class _Eng:
    def __init__(self, name, sem, is_pe=False):
        self.name = name
        self.sem = sem
        self.count = 0
        self.waited = {}
        self.prog = []
        self.is_pe = is_pe
        self.is_slot = False
        self.pending = []


class _Slot:
    def __init__(self, name, sem):
        self.name = name
        self.sem = sem
        self.count = 0
        self.is_slot = True


class Buf:
    __slots__ = ("last_w", "readers", "name")

    def __init__(self, name=""):
        self.last_w = None
        self.readers = {}
        self.name = name


class TT:
    def __init__(self, t, name):
        self.t = t
        self.buf = Buf(name)

    def __getitem__(self, key):
        return self.t[key]


class K:
    def __init__(self, nc, stack):
        self.nc = nc
        self.stack = stack
        self.engs = {}
        for name in ("pe", "act", "dve", "pool", "sp"):
            sem = stack.enter_context(nc.semaphore("sem_" + name))
            self.engs[name] = _Eng(name, sem, is_pe=(name == "pe"))
        self.slots = []
        self._slot_cache = {}
        self.dbufs = {}
        self.n_inst = 0

    def slot(self, name):
        key = re.sub(r"^([mf])\d", r"\1", name)
        if key in self._slot_cache:
            return self._slot_cache[key]
        sem = self.stack.enter_context(self.nc.semaphore("slot_" + key))
        s = _Slot(key, sem)
        self.slots.append(s)
        self._slot_cache[key] = s
        return s

    def sb(self, name, shape, dtype, stack=None):
        st = stack if stack is not None else self.stack
        t = st.enter_context(self.nc.sbuf_tensor(name, list(shape), dtype))
        return TT(t, name)

    def ps(self, name, shape, dtype, stack=None):
        st = stack if stack is not None else self.stack
        t = st.enter_context(self.nc.psum_tensor(name, list(shape), dtype))
        return TT(t, name)

    def dbuf(self, key):
        b = self.dbufs.get(key)
        if b is None:
            b = Buf(str(key))
            self.dbufs[key] = b
        return b

    @staticmethod
    def _b(x):
        return x.buf if isinstance(x, TT) else x

    def _waits(self, E, reads, writes):
        deps = {}

        def add(obj, val):
            if deps.get(obj, 0) < val:
                deps[obj] = val
        for b in reads:
            b = self._b(b)
            if b.last_w is not None:
                add(*b.last_w)
        for b in writes:
            b = self._b(b)
            if b.last_w is not None:
                add(*b.last_w)
            for o, v in b.readers.items():
                add(o, v)
        waits = []
        for obj, val in deps.items():
            if obj.is_slot:
                val = obj.count
            elif obj is E:
                if E.is_pe:
                    continue
                if E.count - val >= 2:
                    continue
            if E.waited.get(obj, 0) >= val:
                continue
            E.waited[obj] = val
            waits.append((obj.sem, val))
        return waits

    def barrier(self):
        for E in self.engs.values():
            for o in list(self.engs.values()) + self.slots:
                if o is E or o.count == 0:
                    continue
                if E.waited.get(o, 0) >= o.count:
                    continue
                E.waited[o] = o.count
                E.pending.append((o.sem, o.count))

    def op(self, eng, fn, r=(), w=()):
        E = self.engs[eng]
        waits = self._waits(E, r, w)
        waits = E.pending + waits
        E.pending = []
        E.count += 1
        idx = E.count
        E.prog.append((waits, fn, (E.sem, 1)))
        for b in r:
            b = self._b(b)
            if b.readers.get(E, 0) < idx:
                b.readers[E] = idx
        for b in w:
            b = self._b(b)
            b.last_w = (E, idx)
            b.readers = {}
        self.n_inst += 1

    def dma(self, q, slot, out, in_, r=(), w=()):
        E = self.engs[q]
        waits = self._waits(E, r, w)
        waits = E.pending + waits
        E.pending = []
        slot.count += 16
        val = slot.count
        E.prog.append((waits, lambda e, out=out, in_=in_: e.dma_start(out=out, in_=in_), (slot.sem, 16)))
        for b in r:
            b = self._b(b)
            b.readers[slot] = val
        for b in w:
            b = self._b(b)
            b.last_w = (slot, val)
            b.readers = {}
        self.n_inst += 1

    def finish(self):
        E = self.engs["sp"]
        final_waits = [(s.sem, s.count) for s in self.slots if s.count > 0]
        for n in ("pe", "act", "dve", "pool"):
            e2 = self.engs[n]
            if e2.count > 0:
                final_waits.append((e2.sem, e2.count))
        progs = {n: e.prog for n, e in self.engs.items()}
        nc = self.nc
        with nc.Block() as block:
            def run(e, prog, extra=()):
                for waits, fn, inc in prog:
                    for sem, val in waits:
                        e.wait_ge(sem, val)
                    ins = fn(e)
                    ins.then_inc(inc[0], inc[1])
                for sem, val in extra:
                    e.wait_ge(sem, val)

            @block.sync
            def _(e):
                run(e, progs["sp"], final_waits)

            @block.tensor
            def _(e):
                run(e, progs["pe"])

            @block.scalar
            def _(e):
                run(e, progs["act"])

            @block.vector
            def _(e):
                run(e, progs["dve"])

            @block.gpsimd
            def _(e):
                run(e, progs["pool"])


def _act(k, out, in_, func, r, w, **kw):
    k.op("act", lambda e: e.activation(out=out, in_=in_, func=func, **kw), r=r, w=w)


def _mm(k, out, lhsT, rhs, start, stop, r, w):
    k.op("pe", lambda e: e.matmul(out, lhsT=lhsT, rhs=rhs, start=start, stop=stop), r=r, w=w)


def _tt(k, eng, out, in0, in1, op, r, w):
    k.op(eng, lambda e: e.tensor_tensor(out=out, in0=in0, in1=in1, op=op), r=r, w=w)


def _ts(k, eng, out, in0, s1, s2, op0, op1, r, w):
    if op1 is None:
        k.op(eng, lambda e: e.tensor_scalar(out=out, in0=in0, scalar1=s1, scalar2=None, op0=op0), r=r, w=w)
    else:
        k.op(eng, lambda e: e.tensor_scalar(out=out, in0=in0, scalar1=s1, scalar2=s2, op0=op0, op1=op1), r=r, w=w)


def _stt(k, out, in0, scalar, in1, op0, op1, r, w):
    k.op("dve", lambda e: e.scalar_tensor_tensor(out=out, in0=in0, scalar=scalar, in1=in1, op0=op0, op1=op1),
         r=r, w=w)


def _copy(k, eng, out, in_, r, w):
    k.op(eng, lambda e: e.tensor_copy(out=out, in_=in_), r=r, w=w)


M_CV, M_BSH, M_BSC, M_G1, M_CW, M_CB, M_BA, M_BX, M_LAM, M_QG, M_KG, M_NP = 0, 8, 16, 24, 32, 64, 72, 80, 88, 96, 97, 98


def emit_mixer(k, nc, S, io, pfx="m", P=None, ncc=4, mix_bf16=False):
    NT = S // 512
    NB = S // 128
    x_d, mix_d = io["x"], io["mixT"]
    if P is None:
        P = [k.ps(f"{pfx}ps{i}", [128, 512], F32) for i in range(8)]
    top = contextlib.ExitStack()
    bank_ctr = [0]

    def bank():
        b = P[bank_ctr[0] % 8]
        bank_ctr[0] += 1
        return b

    hT_d = io["dbg_hT"] if "dbg_hT" in io else nc.dram_tensor(f"{pfx}_hT_d", [128, 8, S], BF16).ap()

    ident = k.sb(f"{pfx}ident", [128, 128], BF16, stack=top)
    tri = k.sb(f"{pfx}tri", [128, 128], BF16, stack=top)
    trin = k.sb(f"{pfx}trin", [128, 128], BF16, stack=top)
    ones = k.sb(f"{pfx}ones", [128, 128], BF16, stack=top)
    masks = k.sb(f"{pfx}masks", [128, 4 * 512], F32, stack=top)
    smp = k.sb(f"{pfx}smp", [128, M_NP], F32, stack=top)
    wa = k.sb(f"{pfx}wa", [128, ncc, 128], BF16, stack=top)
    wx = k.sb(f"{pfx}wx", [128, ncc, 128], BF16, stack=top)
    s_c = k.slot(f"{pfx}const")
    s_c2 = k.slot(f"{pfx}const2")
    k.dma("pool", s_c, ident[:], io["ident"], w=[ident])
    k.dma("pool", s_c, tri[:], io["tri"], w=[tri])
    k.dma("pool", s_c, trin[:], io["trin"], w=[trin])
    k.dma("pool", s_c, ones[:], io["ones"], w=[ones])
    k.dma("pool", s_c, wa[:], io["rgwa"], w=[wa])
    k.dma("pool", s_c, wx[:], io["rgwx"], w=[wx])
    k.dma("sp", s_c2, masks[:], io["masks"], w=[masks])
    k.dma("sp", s_c2, smp[:], io["smallp"], w=[smp])

    modt = k.sb(f"{pfx}modt", [128, 16], F32, stack=top)
    A1 = k.sb(f"{pfx}A1", [128, 8], F32, stack=top)
    negc = k.sb(f"{pfx}negc", [128, 8], F32, stack=top)
    negc2 = k.sb(f"{pfx}negc2", [128, 8], F32, stack=top)
    t4 = k.sb(f"{pfx}t4", [128, 8], F32, stack=top)
    gqs = k.sb(f"{pfx}gqs", [128, 1], F32, stack=top)

    with contextlib.ExitStack() as ps:
        adaw_t = [k.sb(f"{pfx}adaw{i}", [128, 8, 512], F32, stack=ps) for i in range(2)]
        s_aw = [k.slot(f"{pfx}adaw{i}") for i in range(2)]
        pm = bank()
        for piece in range(4):
            at = adaw_t[piece % 2]
            k.dma("sp", s_aw[piece % 2], at[:], io["adaw"][piece], w=[at])
            for jj in range(4):
                j = piece * 4 + jj
                for kc in range(8):
                    _mm(k, pm[:, j:j + 1], at[:, kc, jj * 128:(jj + 1) * 128], smp[:, M_CV + kc:M_CV + kc + 1],
                        kc == 0, kc == 7, r=[at, smp], w=[pm])
        _tt(k, "dve", modt[:], pm[:, 0:16], smp[:, M_BSH:M_BSH + 16], ALU.add, r=[pm, smp], w=[modt])
        _ts(k, "dve", A1[:], modt[:, 8:16], 1.0, None, ALU.add, None, r=[modt], w=[A1])
        _tt(k, "dve", A1[:], A1[:], smp[:, M_G1:M_G1 + 8], ALU.mult, r=[A1, smp], w=[A1])
        _act(k, t4[:], smp[:, M_LAM:M_LAM + 8], AF.Exp, r=[smp], w=[t4], scale=-1.0)
        _act(k, t4[:], t4[:], AF.Ln, r=[t4], w=[t4], bias=1.0)
        _ts(k, "dve", negc[:], t4[:], -8.0, None, ALU.mult, None, r=[t4], w=[negc])
        _ts(k, "dve", negc2[:], t4[:], -16.0, None, ALU.mult, None, r=[t4], w=[negc2])
        _ts(k, "dve", gqs[:], smp[:, M_QG:M_QG + 1], 1.0 / float(np.sqrt(128.0)), None, ALU.mult, None,
            r=[smp], w=[gqs])
        k.barrier()

    with contextlib.ExitStack() as ps:
        xb = [k.sb(f"{pfx}xb{i}", [128, 1024], F32, stack=ps) for i in range(3)]
        s_x = [k.slot(f"{pfx}x{i}") for i in range(3)]
        xh = [k.sb(f"{pfx}xh{i}", [128, 1024], BF16, stack=ps) for i in range(2)]
        junk = k.sb(f"{pfx}junk", [128, 1024], BF16, stack=ps)
        ssq = [k.sb(f"{pfx}ssq{i}", [128, 1], F32, stack=ps) for i in range(2)]
        hTo = [k.sb(f"{pfx}hTo{i}", [128, 8, 512], BF16, stack=ps) for i in range(2)]
        s_ho = [k.slot(f"{pfx}ho{i}") for i in range(2)]
        for tt in range(NT):
            banks = [P[(tt % 2) * 4 + i] for i in range(4)]
            bviews = [b[:].bitcast(BF16) for b in banks]
            for sub in range(4):
                blk = tt * 4 + sub
                xt = xb[blk % 3]
                k.dma("sp", s_x[blk % 3], xt[:], x_d[blk * 128:(blk + 1) * 128, :], w=[xt])
                sq = ssq[blk % 2]
                _act(k, junk[:], xt[:], AF.Square, r=[xt], w=[junk, sq], accum_out=sq[:])
                _act(k, sq[:], sq[:], AF.Sqrt, r=[sq], w=[sq], scale=1.0 / D, bias=EPS)
                k.op("dve", lambda e, sq=sq: e.reciprocal(out=sq[:], in_=sq[:]), r=[sq], w=[sq])
                xht = xh[blk % 2]
                _ts(k, "dve", xht[:, 0:512], xt[:, 0:512], sq[:, 0:1], None, ALU.mult, None, r=[xt, sq], w=[xht])
                _ts(k, "pool", xht[:, 512:1024], xt[:, 512:1024], sq[:, 0:1], None, ALU.mult, None, r=[xt, sq], w=[xht])
                for fc in range(8):
                    bi = fc // 2
                    c0 = (fc % 2) * 512 + sub * 128
                    k.op("pe", lambda e, o=bviews[bi][:, c0:c0 + 128], i_=xht[:, fc * 128:(fc + 1) * 128]:
                         e.transpose(o, i_, ident[:]), r=[xht, ident], w=[banks[bi]])
            ho = hTo[tt % 2]
            for fc in range(8):
                bi = fc // 2
                c0 = (fc % 2) * 512
                _act(k, ho[:, fc, :], bviews[bi][:, c0:c0 + 512], AF.Identity, r=[banks[bi], A1, modt], w=[ho],
                     scale=A1[:, fc:fc + 1], bias=modt[:, fc:fc + 1])
            k.dma("sp", s_ho[tt % 2], hT_d[:, :, tt * 512:(tt + 1) * 512], ho[:], r=[ho],
                  w=[k.dbuf((pfx + "hT", tt))])
        k.barrier()

    with contextlib.ExitStack() as ps:
        W = k.sb(f"{pfx}W", [128, 8, 896], BF16, stack=ps)
        s_w = k.slot(f"{pfx}w")
        hT = [k.sb(f"{pfx}hT{i}", [128, 8, 512], BF16, stack=ps) for i in range(2)]
        s_h = [k.slot(f"{pfx}h{i}") for i in range(2)]
        qT = k.sb(f"{pfx}qT", [128, S], BF16, stack=ps)
        kT = k.sb(f"{pfx}kT", [128, S], BF16, stack=ps)
        vA = k.sb(f"{pfx}vA", [128, NB * 128], BF16, stack=ps)
        mixA = k.sb(f"{pfx}mixA", [128, S], BF16, stack=ps)
        sgb = k.sb(f"{pfx}sgb", [128, S], BF16, stack=ps)
        xraw = k.sb(f"{pfx}xraw", [128, 515], F32, stack=ps)
        state = k.sb(f"{pfx}state", [128, 1], F32, stack=ps)

        def f32t(n):
            return k.sb(f"{pfx}{n}", [128, 512], F32, stack=ps)

        def bf16t(n):
            return k.sb(f"{pfx}{n}", [128, 512], BF16, stack=ps)
        names32 = ("xc", "r", "ig", "a", "a2", "u", "hs", "gl", "ya", "sga", "rq", "rk")
        dbl32 = [[f32t(f"{n}{i}") for n in names32] for i in range(2)]
        dbl16 = [[bf16t(f"{n}{i}") for n in ("xcb", "sqq", "sqk")] for i in range(2)]
        EB = [f32t(f"e{i}") for i in range(4)]
        LB = [bf16t(f"L{i}") for i in range(4)]
        XB = [f32t(f"X{i}") for i in range(3)]
        ATB = [bf16t(f"at{i}") for i in range(4)]
        tmpo = [f32t(f"tmpo{i}") for i in range(2)]
        mixo = [(bf16t if mix_bf16 else f32t)(f"mixo{i}") for i in range(2)]
        s_mo = [k.slot(f"{pfx}mo{i}") for i in range(2)]
        mo_ctr = [0]

        for cc in range(ncc):
            k.dma("pool", s_w, W[:], io["win"][cc], w=[W])
            k.op("dve", lambda e: e.memset(state[:], 0.0), w=[state])
            k.op("dve", lambda e: e.memset(xraw[:, 0:3], 0.0), w=[xraw])
            def load_h(t_):
                k.dma("sp", s_h[t_ % 2], hT[t_ % 2][:], hT_d[:, :, t_ * 512:(t_ + 1) * 512],
                      r=[k.dbuf((pfx + "hT", t_))], w=[hT[t_ % 2]])
            load_h(0)
            for tt in range(NT):
                tsl = slice(tt * 512, (tt + 1) * 512)
                h = hT[tt % 2]
                xc, r_t, ig, a_t, a2, u_t, hs, gl, ya, sga, rq, rk = dbl32[tt % 2]
                xcb, sqq, sqk = dbl16[tt % 2]

                def proj(pb, s_idx):
                    for kc in range(8):
                        _mm(k, pb[:], W[:, kc, s_idx * 128:(s_idx + 1) * 128], h[:, kc, :], kc == 0, kc == 7,
                            r=[W, h], w=[pb])
                pbq = bank()
                proj(pbq, 2)
                _act(k, sqq[:], pbq[:], AF.Square, r=[pbq], w=[sqq])
                pbk = bank()
                proj(pbk, 3)
                _act(k, sqk[:], pbk[:], AF.Square, r=[pbk], w=[sqk])
                pb = bank()
                proj(pb, 0)
                _act(k, xraw[:, 3:515], pb[:], AF.Copy, r=[pb], w=[xraw])
                pb = bank()
                proj(pb, 1)
                _act(k, gl[:], pb[:], AF.Gelu_apprx_tanh, r=[pb], w=[gl])
                cw = M_CW + cc * 4
                _ts(k, "dve", xc[:], xraw[:, 0:512], smp[:, cw:cw + 1], smp[:, M_CB + cc:M_CB + cc + 1],
                    ALU.mult, ALU.add, r=[xraw, smp], w=[xc])
                for tap in range(1, 4):
                    _stt(k, xc[:], xraw[:, tap:tap + 512], smp[:, cw + tap:cw + tap + 1], xc[:], ALU.mult, ALU.add,
                         r=[xraw, xc, smp], w=[xc])
                _copy(k, "pool", xcb[:], xc[:], r=[xc], w=[xcb])
                _copy(k, "pool", xraw[:, 0:3], xraw[:, 512:515], r=[xraw], w=[xraw])
                for (pbx, sqx, rx, gsc, dst) in ((pbq, sqq, rq, gqs[:, 0:1], qT),
                                                 (pbk, sqk, rk, smp[:, M_KG:M_KG + 1], kT)):
                    pbs = bank()
                    _mm(k, pbs[:], ones[:], sqx[:], True, True, r=[ones, sqx], w=[pbs])
                    _act(k, rx[:], pbs[:], AF.Sqrt, r=[pbs], w=[rx], scale=1.0 / 128.0, bias=EPS)
                    k.op("dve", lambda e, rx=rx: e.reciprocal(out=rx[:], in_=rx[:]), r=[rx], w=[rx])
                    _stt(k, dst[:, tsl], pbx[:], gsc, rx[:], ALU.mult, ALU.mult, r=[pbx, rx, gqs, smp], w=[dst])
                pb = bank()
                proj(pb, 5)
                _act(k, sga[:], pb[:], AF.Sigmoid, r=[pb], w=[sga])
                pb = bank()
                proj(pb, 6)
                _act(k, sgb[:, tsl], pb[:], AF.Sigmoid, r=[pb], w=[sgb])
                pbv = bank()
                for sub in range(4):
                    for kc in range(8):
                        _mm(k, pbv[:, sub * 128:(sub + 1) * 128], h[:, kc, sub * 128:(sub + 1) * 128],
                            W[:, kc, 4 * 128:5 * 128], kc == 0, kc == 7, r=[W, h], w=[pbv])
                _copy(k, "dve", vA[:, tt * 512:(tt + 1) * 512], pbv[:], r=[pbv], w=[vA])
                if tt + 1 < NT:
                    load_h(tt + 1)
                pbr = bank()
                _mm(k, pbr[:], wa[:, cc, :], xcb[:], True, True, r=[wa, xcb], w=[pbr])
                pbi = bank()
                _mm(k, pbi[:], wx[:, cc, :], xcb[:], True, True, r=[wx, xcb], w=[pbi])
                _act(k, r_t[:], pbr[:], AF.Sigmoid, r=[pbr, smp], w=[r_t], bias=smp[:, M_BA + cc:M_BA + cc + 1])
                _act(k, ig[:], pbi[:], AF.Sigmoid, r=[pbi, smp], w=[ig], bias=smp[:, M_BX + cc:M_BX + cc + 1])
                _act(k, a_t[:], r_t[:], AF.Exp, r=[r_t, negc], w=[a_t], scale=negc[:, cc:cc + 1])
                _act(k, a2[:], r_t[:], AF.Exp, r=[r_t, negc2], w=[a2], scale=negc2[:, cc:cc + 1])
                _act(k, a2[:], a2[:], AF.Sqrt, r=[a2], w=[a2], scale=-1.0, bias=1.0)
                _tt(k, "pool", u_t[:], ig[:], xc[:], ALU.mult, r=[ig, xc], w=[u_t])
                _tt(k, "pool", u_t[:], u_t[:], a2[:], ALU.mult, r=[u_t, a2], w=[u_t])
                k.op("dve", lambda e, hs=hs, a_t=a_t, u_t=u_t: e.tensor_tensor_scan(
                    out=hs[:], data0=a_t[:], data1=u_t[:], initial=state[:, 0:1], op0=ALU.mult, op1=ALU.add),
                     r=[a_t, u_t, state], w=[hs])
                _copy(k, "dve", state[:], hs[:, 511:512], r=[hs], w=[state])
                _tt(k, "pool", ya[:], hs[:], gl[:], ALU.mult, r=[hs, gl], w=[ya])
                _tt(k, "pool", mixA[:, tsl], ya[:], sga[:], ALU.mult, r=[ya, sga], w=[mixA])

            steps = []
            for qt in range(NT):
                topkb = 4 * qt + 3
                for kb in range(topkb, -1, -1):
                    steps.append(dict(i=len(steps), qt=qt, kb=kb, first=(kb == topkb), last=(kb == 0),
                                      diag=(kb - 4 * qt) if kb >= 4 * qt else None))
            ZP = [P[0], P[1]]
            ACC = [P[2], P[3]]
            OB = [P[4], P[5], P[6]]

            def stA(s):
                zp = ZP[s["i"] % 2]
                kb, qt = s["kb"], s["qt"]
                _mm(k, zp[:], kT[:, kb * 128:(kb + 1) * 128], qT[:, qt * 512:(qt + 1) * 512], True, True,
                    r=[kT, qT], w=[zp])

            def stB(s):
                zp = ZP[s["i"] % 2]
                e_ = EB[s["i"] % 4]
                L = LB[s["i"] % 4]
                _act(k, e_[:], zp[:], AF.Exp, r=[zp], w=[e_])
                if s["diag"] is not None:
                    dg = s["diag"]
                    _tt(k, "dve", e_[:], e_[:], masks[:, dg * 512:(dg + 1) * 512], ALU.mult, r=[e_, masks], w=[e_])
                _act(k, L[:], e_[:], AF.Ln, r=[e_], w=[L], bias=1.0)

            def stC(s):
                acc = ACC[s["qt"] % 2]
                L = LB[s["i"] % 4]
                _mm(k, acc[:], tri[:], L[:], s["first"], s["last"], r=[tri, L], w=[acc])

            def stD(s):
                acc = ACC[s["qt"] % 2]
                X = XB[s["i"] % 3]
                _act(k, X[:], acc[:], AF.Exp, r=[acc], w=[X], scale=-1.0)

            def stE(s):
                if s["last"]:
                    return
                acc = ACC[s["qt"] % 2]
                L = LB[s["i"] % 4]
                _mm(k, acc[:], trin[:], L[:], False, False, r=[trin, L], w=[acc])

            def stF(s):
                at = ATB[s["i"] % 4]
                _tt(k, "dve", at[:], EB[s["i"] % 4][:], XB[s["i"] % 3][:], ALU.mult,
                    r=[EB[s["i"] % 4], XB[s["i"] % 3]], w=[at])

            def stG(s):
                o = OB[s["qt"] % 3]
                at = ATB[s["i"] % 4]
                kb, qt = s["kb"], s["qt"]
                _mm(k, o[:], vA[:, kb * 128:(kb + 1) * 128], at[:], s["first"], s["last"], r=[vA, at], w=[o])
                if s["last"]:
                    qsl = slice(qt * 512, (qt + 1) * 512)
                    j = mo_ctr[0] % 2
                    mo_ctr[0] += 1
                    _tt(k, "dve", tmpo[j][:], o[:], sgb[:, qsl], ALU.mult, r=[o, sgb], w=[tmpo[j]])
                    _tt(k, "pool", mixo[j][:], tmpo[j][:], mixA[:, qsl], ALU.add, r=[tmpo[j], mixA], w=[mixo[j]])
                    k.dma("sp", s_mo[j], mix_d[cc * 128:(cc + 1) * 128, qsl], mixo[j][:], r=[mixo[j]],
                          w=[k.dbuf((pfx + "mix", cc, qt))])

            if "dbg_q" in io and cc == ncc - 1:
                s_dbg = k.slot(f"{pfx}dbg")
                k.dma("sp", s_dbg, io["dbg_small"][:, 0:16], modt[:], r=[modt])
                k.dma("sp", s_dbg, io["dbg_small"][:, 16:24], A1[:], r=[A1])
                for nm, t in (("dbg_q", qT), ("dbg_k", kT), ("dbg_v", vA), ("dbg_mixA", mixA), ("dbg_sgb", sgb)):
                    k.dma("sp", s_dbg, io[nm], t[:], r=[t])
            n = len(steps)
            for it in range(n + 3):
                if it < n:
                    stA(steps[it])
                    stB(steps[it])
                if 0 <= it - 2 < n:
                    stE(steps[it - 2])
                if 0 <= it - 1 < n:
                    stC(steps[it - 1])
                    stD(steps[it - 1])
                if 0 <= it - 2 < n:
                    stF(steps[it - 2])
                if 0 <= it - 3 < n:
                    stG(steps[it - 3])
        k.barrier()
    top.close()


def build_mixer(S, dbg=False):
    nc = bass.Bass("TRN2", target_bir_lowering=False)
    io = {}

    def inp(name, shape, dt=F32):
        io[name] = nc.dram_tensor(name, list(shape), dt, kind="ExternalInput").ap()
    inp("x", [S, D])
    inp("ident", [128, 128])
    inp("tri", [128, 128])
    inp("trin", [128, 128])
    inp("ones", [128, 128])
    inp("masks", [128, 2048])
    inp("smallp", [128, M_NP])
    inp("rgwa", [128, 4, 128])
    inp("rgwx", [128, 4, 128])
    inp("adaw", [4, 128, 8, 512])
    inp("win", [4, 128, 8, 896])
    io["mixT"] = nc.dram_tensor("mixT", [512, S], F32, kind="ExternalOutput").ap()
    if dbg:
        io["dbg_hT"] = nc.dram_tensor("dbg_hT", [128, 8, S], BF16, kind="ExternalOutput").ap()
        io["dbg_small"] = nc.dram_tensor("dbg_small", [128, 24], F32, kind="ExternalOutput").ap()
        for nm in ("dbg_q", "dbg_k", "dbg_v", "dbg_mixA", "dbg_sgb"):
            io[nm] = nc.dram_tensor(nm, [128, S], BF16, kind="ExternalOutput").ap()
    with contextlib.ExitStack() as st:
        k = K(nc, st)
        emit_mixer(k, nc, S, io)
        k.finish()
    return nc


def _pk(v):
    v = np.asarray(v, np.float32)
    return np.ascontiguousarray(v.reshape(-1, 128).T)


def _consts():
    p = np.arange(128)[:, None]
    c = np.arange(128)[None, :]
    tri = (p >= c).astype(np.float32)
    trin = (p < c).astype(np.float32)
    cc = np.arange(512)[None, :]
    masks = np.concatenate([((128 * i + p) < cc).astype(np.float32) for i in range(4)], axis=1)
    return dict(ident=np.eye(128, dtype=np.float32), tri=tri, trin=trin, ones=np.ones((128, 128), np.float32),
                masks=np.ascontiguousarray(masks))


def mixer_weights(l, b, hh, inp):
    if hh is None:
        ch, h0, ncc = slice(0, 1024), 0, 8
    else:
        ch, h0, ncc = slice(hh * 512, (hh + 1) * 512), hh * 4, 4
    d = {}
    ada_w, ada_b = inp["ada_w"][l], inp["ada_b"][l]
    smallp = np.zeros((128, M_NP), np.float32)
    smallp[:, M_CV:M_CV + 8] = _pk(inp["c"][b])
    smallp[:, M_BSH:M_BSH + 8] = _pk(ada_b[0:1024])
    smallp[:, M_BSC:M_BSC + 8] = _pk(ada_b[1024:2048])
    smallp[:, M_G1:M_G1 + 8] = _pk(inp["norm1_g"][l])
    cw = inp["conv_w"][l][:, ch]
    for cc in range(ncc):
        smallp[:, M_CW + cc * 4:M_CW + cc * 4 + 4] = cw[:, cc * 128:(cc + 1) * 128].T
    smallp[:, M_CB:M_CB + ncc] = _pk(inp["conv_b"][l][ch])
    smallp[:, M_BA:M_BA + ncc] = _pk(inp["rg_ba"][l][ch])
    smallp[:, M_BX:M_BX + ncc] = _pk(inp["rg_bx"][l][ch])
    smallp[:, M_LAM:M_LAM + ncc] = _pk(inp["rg_lambda"][l][ch])
    smallp[:, M_QG] = inp["q_norm_g"][l]
    smallp[:, M_KG] = inp["k_norm_g"][l]
    d["smallp"] = smallp
    d["rgwa"] = np.ascontiguousarray(inp["rg_wa"][l][h0:h0 + ncc].transpose(1, 0, 2))
    d["rgwx"] = np.ascontiguousarray(inp["rg_wx"][l][h0:h0 + ncc].transpose(1, 0, 2))
    aw = ada_w[:, 0:2048].reshape(8, 128, 4, 512).transpose(2, 1, 0, 3)
    d["adaw"] = np.ascontiguousarray(aw)
    w_in = inp["w_in"][l]
    w7 = w_in.reshape(8, 128, 7, 8, 128)[:, :, :, h0:h0 + ncc, :]
    d["win"] = np.ascontiguousarray(w7.transpose(3, 1, 0, 2, 4).reshape(ncc, 128, 8, 896))
    return d


def mixer_inputs(l, b, hh, xb, inp):
    d = dict(_consts())
    d["x"] = np.ascontiguousarray(xb, dtype=np.float32)
    d.update(mixer_weights(l, b, hh, inp))
    return d


_PROGS = {}


def _prog(kind, n):
    key = (kind, n)
    if key not in _PROGS:
        _PROGS[key] = build_mixer(n) if kind == "mixer" else build_ffn(n)
    return _PROGS[key]


FUSED = True


def kernel(**inputs):
    inp = {k_: np.asarray(v, dtype=np.float32) for k_, v in inputs.items()}
    x = inp["x"]
    B, S, _ = x.shape
    depth = inp["w_in"].shape[0]
    cores = list(range(8))
    if FUSED:
        key = ("fused", S, depth)
        if key not in _PROGS:
            _PROGS[key] = build_fused(S, depth)
        per_b = [fused_inputs(b, inp, depth) for b in range(B)]
        res = run_bass_kernel_spmd(_PROGS[key], [per_b[c // 2] for c in cores], core_ids=cores)
        half = S // 2
        return np.stack([np.concatenate([np.asarray(res.results[2 * b]["out"])[:half],
                                         np.asarray(res.results[2 * b + 1]["out"])[half:]], axis=0)
                         for b in range(B)], axis=0).astype(np.float32)
    for l in range(depth):
        nc = _prog("mixer", S)
        in_maps = [mixer_inputs(l, c // 2, c % 2, x[c // 2], inp) for c in cores]
        res = run_bass_kernel_spmd(nc, in_maps, core_ids=cores)
        mixT = [np.concatenate([np.asarray(res.results[2 * b]["mixT"]), np.asarray(res.results[2 * b + 1]["mixT"])],
                               axis=0) for b in range(B)]
        nc = _prog("ffn", S // 2)
        in_maps = [ffn_inputs(l, c // 2, c % 2, x[c // 2], mixT[c // 2], inp) for c in cores]
        res = run_bass_kernel_spmd(nc, in_maps, core_ids=cores)
        x = np.stack([np.concatenate([np.asarray(res.results[2 * b]["xout"]), np.asarray(res.results[2 * b + 1]["xout"])],
                                     axis=0) for b in range(B)], axis=0).astype(np.float32)
    return x


F_CV, F_BSH, F_BSC, F_G2, F_CW, F_CB, F_FLAG, F_NP = 0, 8, 16, 24, 32, 176, 224, 225


def emit_ffn(k, nc, S2, io, pfx="f", P=None, pre=True, mix_bf16=False):
    NT = S2 // 512
    x_d, mix_d, out_d = io["xin"], io["mixin"], io["xout"]
    if P is None:
        P = [k.ps(f"{pfx}ps{i}", [128, 512], F32) for i in range(8)]
    top = contextlib.ExitStack()
    bank_ctr = [0]

    def bank():
        b = P[4 + bank_ctr[0] % 4]
        bank_ctr[0] += 1
        return b

    ident = k.sb(f"{pfx}ident", [128, 128], BF16, stack=top)
    smp = k.sb(f"{pfx}smp", [128, F_NP], F32, stack=top)
    wo = k.sb(f"{pfx}wo", [128, 8, 1024], BF16, stack=top)
    wd = k.sb(f"{pfx}wd", [128, 24, 1024], BF16, stack=top)
    gtB = k.sb(f"{pfx}gtB", [128, 2048], F32, stack=top)
    modt = k.sb(f"{pfx}modt", [128, 16], F32, stack=top)
    A2 = k.sb(f"{pfx}A2", [128, 8], F32, stack=top)
    halo = k.sb(f"{pfx}halo", [128, 48, 2], F32, stack=top)
    s_c = k.slot(f"{pfx}const")
    s_c2 = k.slot(f"{pfx}const2")
    k.dma("sp", s_c2, smp[:], io["smallp"], w=[smp])
    k.dma("pool", s_c, ident[:], io["ident"], w=[ident])
    k.dma("pool", s_c, wo[:], io["wout"], w=[wo])
    for q in range(4):
        k.dma("pool", s_c, wd[:, q * 6:(q + 1) * 6, :], io["fdown"][:, q * 6:(q + 1) * 6, :], w=[wd])

    with contextlib.ExitStack() as ps:
        adaw_t = [k.sb(f"{pfx}adaw{i}", [128, 8, 512], F32, stack=ps) for i in range(2)]
        s_aw = [k.slot(f"{pfx}adaw{i}") for i in range(2)]
        cB = k.sb(f"{pfx}cB", [128, 8, 128], F32, stack=ps)
        onesf = k.sb(f"{pfx}onesf", [128, 128], F32, stack=ps)
        gtb_t = k.sb(f"{pfx}gtb_t", [128, 2048], F32, stack=ps)
        k.dma("sp", s_c2, gtb_t[:], io["gtb"].partition_broadcast(128), w=[gtb_t])
        k.op("dve", lambda e: e.memset(onesf[:], 1.0), w=[onesf])
        for kc in range(8):
            _ts(k, "dve", cB[:, kc, :], onesf[:], smp[:, F_CV + kc:F_CV + kc + 1], None, ALU.mult, None,
                r=[onesf, smp], w=[cB])
        pm = bank()
        for piece in range(4):
            at = adaw_t[piece % 2]
            k.dma("sp", s_aw[piece % 2], at[:], io["adaw"][piece], w=[at])
            for jj in range(4):
                j = piece * 4 + jj
                for kc in range(8):
                    _mm(k, pm[:, j:j + 1], at[:, kc, jj * 128:(jj + 1) * 128], smp[:, F_CV + kc:F_CV + kc + 1],
                        kc == 0, kc == 7, r=[at, smp], w=[pm])
        _tt(k, "dve", modt[:], pm[:, 0:16], smp[:, F_BSH:F_BSH + 16], ALU.add, r=[pm, smp], w=[modt])
        _ts(k, "dve", A2[:], modt[:, 8:16], 1.0, None, ALU.add, None, r=[modt], w=[A2])
        _tt(k, "dve", A2[:], A2[:], smp[:, F_G2:F_G2 + 8], ALU.mult, r=[A2, smp], w=[A2])
        for piece in range(4, 8):
            at = adaw_t[piece % 2]
            k.dma("sp", s_aw[piece % 2], at[:], io["adaw"][piece], w=[at])
            pb = bank()
            for kc in range(8):
                _mm(k, pb[:], cB[:, kc, :], at[:, kc, :], kc == 0, kc == 7, r=[cB, at], w=[pb])
            c0 = (piece - 4) * 512
            _tt(k, "dve", gtB[:, c0:c0 + 512], pb[:], gtb_t[:, c0:c0 + 512], ALU.add, r=[pb, gtb_t], w=[gtB])
        k.barrier()

    with contextlib.ExitStack() as ps:
        mx = k.sb(f"{pfx}mx", [128, 8, 512], BF16, stack=ps)
        s_mx = k.slot(f"{pfx}mx0")
        xt = [k.sb(f"{pfx}xt{i}", [128, 1024], F32, stack=ps) for i in range(2)]
        s_xt = [k.slot(f"{pfx}xt{i}") for i in range(2)]
        x1t = [[k.sb(f"{pfx}x1t{p_}{i}", [128, 1024], F32, stack=ps) for i in range(4)] for p_ in range(2)]
        s_xo = [[k.slot(f"{pfx}xo{p_}{i}") for i in range(4)] for p_ in range(2)]
        xh = [k.sb(f"{pfx}xh{i}", [128, 1024], BF16, stack=ps) for i in range(2)]
        ssq = [k.sb(f"{pfx}ssq{i}", [128, 1], F32, stack=ps) for i in range(2)]
        h2T = [k.sb(f"{pfx}h2T{i}", [128, 8, 512], BF16, stack=ps) for i in range(2)]
        Wj = [k.sb(f"{pfx}Wj{i}", [128, 8, 256], BF16, stack=ps) for i in range(3)]
        s_wj = [k.slot(f"{pfx}wj{i}") for i in range(3)]
        raw = [k.sb(f"{pfx}raw{i}", [128, 514], F32, stack=ps) for i in range(4)]
        tcv = [k.sb(f"{pfx}tcv{i}", [128, 512], F32, stack=ps) for i in range(4)]
        gg = [k.sb(f"{pfx}gg{i}", [128, 512], F32, stack=ps) for i in range(2)]
        actT = k.sb(f"{pfx}actT", [128, 24, 512], BF16, stack=ps)
        tmp = [k.sb(f"{pfx}tmp{i}", [128, 512], F32, stack=ps) for i in range(2)]
        tmp1 = [k.sb(f"{pfx}tmpa{i}", [128, 512], F32, stack=ps) for i in range(2)]
        ctr = dict(x=0, raw=0, t=0, g=0, tmp=0, tmp1=0)
        mixv = mix_d.rearrange("(kc p) t -> p kc t", p=128)
        tiles = ([(0, 128, True, 0)] if pre else []) + \
            [((128 if pre else 0) + tt * 512, 512, False, tt * 512) for tt in range(NT)]
        wst = dict(issued=0, total=24 * len(tiles))
        tb = [P[i] for i in range(4)]
        tbv = [b_[:].bitcast(BF16) for b_ in tb]

        def issue_w(upto):
            while wst["issued"] <= upto and wst["issued"] < wst["total"]:
                g = wst["issued"]
                if "fup_bf" in io:
                    k.dma("sp", s_wj[g % 3], Wj[g % 3][:], io["fup_bf"][g % 24], w=[Wj[g % 3]])
                else:
                    k.dma("pool", s_wj[g % 3], Wj[g % 3][:], io["fup"][g % 24], w=[Wj[g % 3]])
                wst["issued"] += 1

        def st1_load(ti):
            row0, ntok = tiles[ti][0], tiles[ti][1]
            k.dma("sp" if mix_bf16 else "pool", s_mx, mx[:, :, 0:ntok], mixv[:, :, row0:row0 + ntok], w=[mx])

        st1x = {}

        def st1_a0(ti, sub):
            row0 = tiles[ti][0]
            xi = ctr["x"] % 2
            ctr["x"] += 1
            xx = xt[xi]
            k.dma("sp", s_xt[xi], xx[:], x_d[row0 + sub * 128:row0 + (sub + 1) * 128, :], w=[xx])
            tps = []
            for half in range(2):
                hs_ = slice(half * 512, (half + 1) * 512)
                po = bank()
                for kc in range(8):
                    _mm(k, po[:], mx[:, kc, sub * 128:(sub + 1) * 128], wo[:, kc, hs_], kc == 0, kc == 7,
                        r=[mx, wo], w=[po])
                tp = tmp1[ctr["tmp1"] % 2]
                ctr["tmp1"] += 1
                _tt(k, "dve", tp[:], po[:], gtB[:, hs_], ALU.mult, r=[po, gtB], w=[tp])
                tps.append(tp)
            st1x[(ti, sub)] = (xx, tps)

        def st1_a1(ti, sub):
            xx, tps = st1x[(ti, sub)]
            x1 = x1t[ti % 2][sub]
            for half in range(2):
                hs_ = slice(half * 512, (half + 1) * 512)
                _tt(k, "pool", x1[:, hs_], tps[half][:], xx[:, hs_], ALU.add, r=[tps[half], xx], w=[x1])

        def st1_a2(ti, sub):
            x1 = x1t[ti % 2][sub]
            sq = ssq[sub % 2]
            xht = xh[sub % 2]
            _act(k, xht[:], x1[:], AF.Square, r=[x1], w=[xht, sq], accum_out=sq[:])
            _act(k, sq[:], sq[:], AF.Sqrt, r=[sq], w=[sq], scale=1.0 / D, bias=EPS)

        def st1_a3(ti, sub):
            x1 = x1t[ti % 2][sub]
            sq = ssq[sub % 2]
            xht = xh[sub % 2]
            k.op("dve", lambda e, sq=sq: e.reciprocal(out=sq[:], in_=sq[:]), r=[sq], w=[sq])
            _ts(k, "dve", xht[:, 0:512], x1[:, 0:512], sq[:, 0:1], None, ALU.mult, None, r=[x1, sq], w=[xht])
            _ts(k, "pool", xht[:, 512:1024], x1[:, 512:1024], sq[:, 0:1], None, ALU.mult, None, r=[x1, sq], w=[xht])

        def st1_a(ti, sub):
            st1_a0(ti, sub)
            st1_a1(ti, sub)
            st1_a2(ti, sub)
            st1_a3(ti, sub)

        def st1_b(ti, sub):
            xht = xh[sub % 2]
            for fc in range(8):
                bi = fc // 2
                c0 = (fc % 2) * 512 + sub * 128
                k.op("pe", lambda e, o=tbv[bi][:, c0:c0 + 128], i_=xht[:, fc * 128:(fc + 1) * 128]:
                     e.transpose(o, i_, ident[:]), r=[xht, ident], w=[tb[bi]])

        def st1_fin(ti, fcs=range(8)):
            ntok = tiles[ti][1]
            h2 = h2T[ti % 2]
            for fc in fcs:
                bi = fc // 2
                c0 = (fc % 2) * 512
                _act(k, h2[:, fc, 0:ntok], tbv[bi][:, c0:c0 + ntok], AF.Identity, r=[tb[bi], A2, modt], w=[h2],
                     scale=A2[:, fc:fc + 1], bias=modt[:, fc:fc + 1])

        pend = []

        def up_chunk(ti, j):
            ntok, is_pre = tiles[ti][1], tiles[ti][2]
            h2 = h2T[ti % 2]
            g = ti * 24 + j
            issue_w(g + 2)
            wj = Wj[g % 3]
            res = []
            for br in range(2):
                cidx = br * 24 + j
                pb = bank()
                for kc in range(8):
                    _mm(k, pb[:, 0:ntok], wj[:, kc, br * 128:(br + 1) * 128], h2[:, kc, 0:ntok], kc == 0, kc == 7,
                        r=[wj, h2], w=[pb])
                if is_pre:
                    _ts(k, "dve", halo[:, cidx, :], pb[:, ntok - 2:ntok], smp[:, F_FLAG:F_FLAG + 1], None,
                        ALU.mult, None, r=[pb, smp], w=[halo])
                    continue
                rw = raw[ctr["raw"] % 4]
                ctr["raw"] += 1
                _act(k, rw[:, 2:2 + ntok], pb[:, 0:ntok], AF.Copy, r=[pb], w=[rw])
                _copy(k, "pool", rw[:, 0:2], halo[:, cidx, :], r=[halo], w=[rw])
                tc = tcv[ctr["t"] % 4]
                ctr["t"] += 1
                cw = F_CW + cidx * 3
                _act(k, tc[:], rw[:, 0:512], AF.Identity, r=[rw, smp], w=[tc], scale=smp[:, cw:cw + 1],
                     bias=smp[:, F_CB + cidx:F_CB + cidx + 1])
                for tap in (1, 2):
                    _stt(k, tc[:], rw[:, tap:tap + 512], smp[:, cw + tap:cw + tap + 1], tc[:], ALU.mult, ALU.add,
                         r=[rw, tc, smp], w=[tc])
                _copy(k, "pool", halo[:, cidx, :], rw[:, 512:514], r=[rw], w=[halo])
                res.append(tc)
            if is_pre:
                return
            pend.append((j, res))

        def up_tail():
            if not pend:
                return
            j, res = pend.pop(0)
            g_ = gg[ctr["g"] % 2]
            ctr["g"] += 1
            _act(k, g_[:], res[0][:], AF.Gelu_apprx_tanh, r=[res[0]], w=[g_])
            _tt(k, "dve" if j % 2 == 1 else "pool", actT[:, j, :], g_[:], res[1][:], ALU.mult, r=[g_, res[1]], w=[actT])

        def down(ti):
            ntok, orow0 = tiles[ti][1], tiles[ti][3]
            for sub in range(ntok // 128):
                x1 = x1t[ti % 2][sub]
                for half in range(2):
                    hs_ = slice(half * 512, (half + 1) * 512)
                    pd = bank()
                    for j in range(24):
                        _mm(k, pd[:], actT[:, j, sub * 128:(sub + 1) * 128], wd[:, j, hs_], j == 0, j == 23,
                            r=[actT, wd], w=[pd])
                    tp = tmp[ctr["tmp"] % 2]
                    ctr["tmp"] += 1
                    _tt(k, "dve", tp[:], pd[:], gtB[:, 1024 + half * 512:1024 + (half + 1) * 512], ALU.mult,
                        r=[pd, gtB], w=[tp])
                    _tt(k, "pool", x1[:, hs_], tp[:], x1[:, hs_], ALU.add, r=[tp, x1], w=[x1])
                r0 = orow0 + sub * 128
                k.dma("sp", s_xo[ti % 2][sub], out_d[r0:r0 + 128, :], x1[:], r=[x1], w=[k.dbuf((pfx + "out", r0))])

        if not pre:
            k.op("dve", lambda e: e.memset(halo[:], 0.0), w=[halo])
        nt_all = len(tiles)
        st1_load(0)
        for sub in range(tiles[0][1] // 128):
            st1_a(0, sub)
            st1_b(0, sub)
        st1_fin(0)
        for ti in range(nt_all):
            issue_w(ti * 24 + 1)
            sched = {}
            if ti + 1 < nt_all:
                st1_load(ti + 1)
                for sub in range(tiles[ti + 1][1] // 128):
                    j0 = 1 + 5 * sub
                    for dj, fn_ in enumerate((st1_a0, st1_a1, st1_a2, st1_a3, st1_b)):
                        sched.setdefault(j0 + dj, []).append(lambda ti=ti, sub=sub, fn_=fn_: fn_(ti + 1, sub))
                for q_ in range(3):
                    fcs = (range(0, 3), range(3, 6), range(6, 8))[q_]
                    sched.setdefault(21 + q_, []).append(lambda ti=ti, fcs=fcs: st1_fin(ti + 1, fcs))
            for j in range(24):
                up_chunk(ti, j)
                if j >= 1:
                    up_tail()
                for f_ in sched.get(j, []):
                    f_()
            up_tail()
            issue_w((ti + 1) * 24 + 1)
            if not tiles[ti][2]:
                down(ti)
        k.barrier()
    top.close()


def build_ffn(S2):
    nc = bass.Bass("TRN2", target_bir_lowering=False)
    io = {}

    def inp(name, shape, dt=F32):
        io[name] = nc.dram_tensor(name, list(shape), dt, kind="ExternalInput").ap()
    inp("xin", [128 + S2, D])
    inp("mixin", [D, 128 + S2])
    inp("ident", [128, 128])
    inp("smallp", [128, F_NP])
    inp("gtb", [2048])
    inp("adaw", [8, 128, 8, 512])
    inp("wout", [128, 8, 1024])
    inp("fup", [24, 128, 8, 256])
    inp("fdown", [128, 24, 1024])
    io["xout"] = nc.dram_tensor("xout", [S2, D], F32, kind="ExternalOutput").ap()
    with contextlib.ExitStack() as st:
        k = K(nc, st)
        emit_ffn(k, nc, S2, io)
        k.finish()
    return nc


def ffn_weights(l, b, flag, inp):
    d = {}
    ada_w, ada_b = inp["ada_w"][l], inp["ada_b"][l]
    smallp = np.zeros((128, F_NP), np.float32)
    smallp[:, F_CV:F_CV + 8] = _pk(inp["c"][b])
    smallp[:, F_BSH:F_BSH + 8] = _pk(ada_b[3072:4096])
    smallp[:, F_BSC:F_BSC + 8] = _pk(ada_b[4096:5120])
    smallp[:, F_G2:F_G2 + 8] = _pk(inp["norm2_g"][l])
    cw = inp["ffn_conv_w"][l]
    smallp[:, F_CW:F_CW + 144] = cw.reshape(3, 48, 128).transpose(2, 1, 0).reshape(128, 144)
    smallp[:, F_CB:F_CB + 48] = _pk(inp["ffn_conv_b"][l])
    smallp[:, F_FLAG] = flag
    d["smallp"] = smallp
    d["gtb"] = np.ascontiguousarray(np.concatenate([ada_b[2048:3072], ada_b[5120:6144]]))
    cols = np.concatenate([np.arange(3072, 5120), np.arange(2048, 3072), np.arange(5120, 6144)])
    aw = ada_w[:, cols].reshape(8, 128, 8, 512).transpose(2, 1, 0, 3)
    d["adaw"] = np.ascontiguousarray(aw)
    d["wout"] = np.ascontiguousarray(inp["w_out"][l].reshape(8, 128, 1024).transpose(1, 0, 2))
    fu = inp["ffn_up"][l].reshape(8, 128, 2, 24, 128)
    d["fup"] = np.ascontiguousarray(fu.transpose(3, 1, 0, 2, 4).reshape(24, 128, 8, 256))
    d["fdown"] = np.ascontiguousarray(inp["ffn_down"][l].reshape(24, 128, 1024).transpose(1, 0, 2))
    return d


def ffn_inputs(l, b, th, xb, mixTb, inp):
    S = xb.shape[0]
    S2 = S // 2
    t0 = th * S2
    p0 = t0 - 128 if th > 0 else 0
    d = dict(ident=np.eye(128, dtype=np.float32))
    d["xin"] = np.ascontiguousarray(np.concatenate([xb[p0:p0 + 128], xb[t0:t0 + S2]], axis=0), dtype=np.float32)
    d["mixin"] = np.ascontiguousarray(np.concatenate([mixTb[:, p0:p0 + 128], mixTb[:, t0:t0 + S2]], axis=1),
                                      dtype=np.float32)
    d.update(ffn_weights(l, b, 1.0 if th > 0 else 0.0, inp))
    return d


_MW = ("smallp", "rgwa", "rgwx", "adaw", "win")
_FW = ("smallp", "gtb", "adaw", "wout", "fup", "fdown")


def build_fused(S, depth=2, skip=()):
    nc = bass.Bass("TRN2", target_bir_lowering=False)
    ext = {}

    def inp(name, shape, dt=F32):
        ext[name] = nc.dram_tensor(name, list(shape), dt, kind="ExternalInput").ap()
    inp("x", [S, D])
    for n_ in ("ident", "tri", "trin", "ones"):
        inp(n_, [128, 128])
    inp("masks", [128, 2048])
    for l in range(depth):
        inp(f"m{l}_smallp", [128, M_NP])
        inp(f"m{l}_rgwa", [128, 8, 128])
        inp(f"m{l}_rgwx", [128, 8, 128])
        inp(f"m{l}_adaw", [4, 128, 8, 512])
        inp(f"m{l}_win", [8, 128, 8, 896])
        inp(f"f{l}_smallp", [128, F_NP])
        inp(f"f{l}_gtb", [2048])
        inp(f"f{l}_adaw", [8, 128, 8, 512])
        inp(f"f{l}_wout", [128, 8, 1024])
        inp(f"f{l}_fup", [24, 128, 8, 256])
        inp(f"f{l}_fdown", [128, 24, 1024])
    out = nc.dram_tensor("out", [S, D], F32, kind="ExternalOutput").ap()
    mixT_d = nc.dram_tensor("mixT_d", [D, S], BF16).ap()
    xmid = [nc.dram_tensor(f"xmid{l}", [S, D], F32).ap() for l in range(depth - 1)]
    fupb = [nc.dram_tensor(f"fupb{l}", [24, 128, 8, 256], BF16).ap() for l in range(depth)]
    with contextlib.ExitStack() as st:
        k = K(nc, st)
        P = [k.ps(f"ps{i}", [128, 512], F32) for i in range(8)]
        for l in range(depth):
            x_in = ext["x"] if l == 0 else xmid[l - 1]
            x_out = out if l == depth - 1 else xmid[l]
            io = dict(x=x_in, mixT=mixT_d, ident=ext["ident"], tri=ext["tri"], trin=ext["trin"], ones=ext["ones"],
                      masks=ext["masks"])
            for n_ in _MW:
                io[n_] = ext[f"m{l}_{n_}"]
            s_cast = k.slot("fcast")
            for j in range(24):
                k.dma("pool", s_cast, fupb[l][j], ext[f"f{l}_fup"][j], w=[k.dbuf(("fupb", l, j))])
            if f"m{l}" not in skip:
                emit_mixer(k, nc, S, io, pfx=f"m{l}", P=P, ncc=8, mix_bf16=True)
            io = dict(xin=x_in, mixin=mixT_d, xout=x_out, ident=ext["ident"], fup_bf=fupb[l])
            for n_ in _FW:
                io[n_] = ext[f"f{l}_{n_}"]
            if f"f{l}" not in skip:
                emit_ffn(k, nc, S, io, pfx=f"f{l}", P=P, pre=False, mix_bf16=True)
        k.finish()
    return nc


def fused_inputs(b, inp, depth):
    d = dict(_consts())
    d["x"] = np.ascontiguousarray(inp["x"][b], dtype=np.float32)
    for l in range(depth):
        for n_, v in mixer_weights(l, b, None, inp).items():
            d[f"m{l}_{n_}"] = v
        for n_, v in ffn_weights(l, b, 0.0, inp).items():
            d[f"f{l}_{n_}"] = v
    return d
```

```python
import contextlib
import re
import numpy as np
import concourse.bass as bass
import concourse.mybir as mybir
from concourse.bass_utils import run_bass_kernel_spmd

F32 = mybir.dt.float32
BF16 = mybir.dt.bfloat16
AF = mybir.ActivationFunctionType
ALU = mybir.AluOpType

D = 1024
NH = 8
EPS = 1e-6
DFF = 3072


class _Eng:
    def __init__(self, name, sem, is_pe=False):
        self.name = name
        self.sem = sem
        self.count = 0
        self.waited = {}
        self.prog = []
        self.is_pe = is_pe
        self.is_slot = False
        self.pending = []


class _Slot:
    def __init__(self, name, sem):
        self.name = name
        self.sem = sem
        self.count = 0
        self.is_slot = True


class Buf:
    __slots__ = ("last_w", "readers", "name")

    def __init__(self, name=""):
        self.last_w = None
        self.readers = {}
        self.name = name


class TT:
    def __init__(self, t, name):
        self.t = t
        self.buf = Buf(name)

    def __getitem__(self, key):
        return self.t[key]


class K:
    def __init__(self, nc, stack):
        self.nc = nc
        self.stack = stack
        self.engs = {}
        for name in ("pe", "act", "dve", "pool", "sp"):
            sem = stack.enter_context(nc.semaphore("sem_" + name))
            self.engs[name] = _Eng(name, sem, is_pe=(name == "pe"))
        self.slots = []
        self._slot_cache = {}
        self.dbufs = {}
        self.n_inst = 0

    def slot(self, name):
        key = re.sub(r"^([mf])\d", r"\1", name)
        if key in self._slot_cache:
            return self._slot_cache[key]
        sem = self.stack.enter_context(self.nc.semaphore("slot_" + key))
        s = _Slot(key, sem)
        self.slots.append(s)
        self._slot_cache[key] = s
        return s

    def sb(self, name, shape, dtype, stack=None):
        st = stack if stack is not None else self.stack
        t = st.enter_context(self.nc.sbuf_tensor(name, list(shape), dtype))
        return TT(t, name)

    def ps(self, name, shape, dtype, stack=None):
        st = stack if stack is not None else self.stack
        t = st.enter_context(self.nc.psum_tensor(name, list(shape), dtype))
        return TT(t, name)

    def dbuf(self, key):
        b = self.dbufs.get(key)
        if b is None:
            b = Buf(str(key))
            self.dbufs[key] = b
        return b

    @staticmethod
    def _b(x):
        return x.buf if isinstance(x, TT) else x

    def _waits(self, E, reads, writes):
        deps = {}

        def add(obj, val):
            if deps.get(obj, 0) < val:
                deps[obj] = val
        for b in reads:
            b = self._b(b)
            if b.last_w is not None:
                add(*b.last_w)
        for b in writes:
            b = self._b(b)
            if b.last_w is not None:
                add(*b.last_w)
            for o, v in b.readers.items():
                add(o, v)
        waits = []
        for obj, val in deps.items():
            if obj.is_slot:
                val = obj.count
            elif obj is E:
                if E.is_pe:
                    continue
                if E.count - val >= 2:
                    continue
            if E.waited.get(obj, 0) >= val:
                continue
            E.waited[obj] = val
            waits.append((obj.sem, val))
        return waits

    def barrier(self):
        for E in self.engs.values():
            for o in list(self.engs.values()) + self.slots:
                if o is E or o.count == 0:
                    continue
                if E.waited.get(o, 0) >= o.count:
                    continue
                E.waited[o] = o.count
                E.pending.append((o.sem, o.count))

    def op(self, eng, fn, r=(), w=()):
        E = self.engs[eng]
        waits = self._waits(E, r, w)
        waits = E.pending + waits
        E.pending = []
        E.count += 1
        idx = E.count
        E.prog.append((waits, fn, (E.sem, 1)))
        for b in r:
            b = self._b(b)
            if b.readers.get(E, 0) < idx:
                b.readers[E] = idx
        for b in w:
            b = self._b(b)
            b.last_w = (E, idx)
            b.readers = {}
        self.n_inst += 1

    def dma(self, q, slot, out, in_, r=(), w=()):
        E = self.engs[q]
        waits = self._waits(E, r, w)
        waits = E.pending + waits
        E.pending = []
        slot.count += 16
        val = slot.count
        E.prog.append((waits, lambda e, out=out, in_=in_: e.dma_start(out=out, in_=in_), (slot.sem, 16)))
        for b in r:
            b = self._b(b)
            b.readers[slot] = val
        for b in w:
            b = self._b(b)
            b.last_w = (slot, val)
            b.readers = {}
        self.n_inst += 1

    def finish(self):
        E = self.engs["sp"]
        final_waits = [(s.sem, s.count) for s in self.slots if s.count > 0]
        for n in ("pe", "act", "dve", "pool"):
            e2 = self.engs[n]
            if e2.count > 0:
                final_waits.append((e2.sem, e2.count))
        progs = {n: e.prog for n, e in self.engs.items()}
        nc = self.nc
        with nc.Block() as block:
            def run(e, prog, extra=()):
                for waits, fn, inc in prog:
                    for sem, val in waits:
                        e.wait_ge(sem, val)
                    ins = fn(e)
                    ins.then_inc(inc[0], inc[1])
                for sem, val in extra:
                    e.wait_ge(sem, val)

            @block.sync
            def _(e):
                run(e, progs["sp"], final_waits)

            @block.tensor
            def _(e):
                run(e, progs["pe"])

            @block.scalar
            def _(e):
                run(e, progs["act"])

            @block.vector
            def _(e):
                run(e, progs["dve"])

            @block.gpsimd
            def _(e):
                run(e, progs["pool"])


def _act(k, out, in_, func, r, w, **kw):
    k.op("act", lambda e: e.activation(out=out, in_=in_, func=func, **kw), r=r, w=w)


def _mm(k, out, lhsT, rhs, start, stop, r, w):
    k.op("pe", lambda e: e.matmul(out, lhsT=lhsT, rhs=rhs, start=start, stop=stop), r=r, w=w)


def _tt(k, eng, out, in0, in1, op, r, w):
    k.op(eng, lambda e: e.tensor_tensor(out=out, in0=in0, in1=in1, op=op), r=r, w=w)


def _ts(k, eng, out, in0, s1, s2, op0, op1, r, w):
    if op1 is None:
        k.op(eng, lambda e: e.tensor_scalar(out=out, in0=in0, scalar1=s1, scalar2=None, op0=op0), r=r, w=w)
    else:
        k.op(eng, lambda e: e.tensor_scalar(out=out, in0=in0, scalar1=s1, scalar2=s2, op0=op0, op1=op1), r=r, w=w)


def _stt(k, out, in0, scalar, in1, op0, op1, r, w):
    k.op("dve", lambda e: e.scalar_tensor_tensor(out=out, in0=in0, scalar=scalar, in1=in1, op0=op0, op1=op1),
         r=r, w=w)


def _copy(k, eng, out, in_, r, w):
    k.op(eng, lambda e: e.tensor_copy(out=out, in_=in_), r=r, w=w)


M_CV, M_BSH, M_BSC, M_G1, M_CW, M_CB, M_BA, M_BX, M_LAM, M_QG, M_KG, M_NP = 0, 8, 16, 24, 32, 64, 72, 80, 88, 96, 97, 98


def emit_mixer(k, nc, S, io, pfx="m", P=None, ncc=4, mix_bf16=False):
    NT = S // 512
    NB = S // 128
    x_d, mix_d = io["x"], io["mixT"]
    if P is None:
        P = [k.ps(f"{pfx}ps{i}", [128, 512], F32) for i in range(8)]
    top = contextlib.ExitStack()
    bank_ctr = [0]

    def bank():
        b = P[bank_ctr[0] % 8]
        bank_ctr[0] += 1
        return b

    hT_d = io["dbg_hT"] if "dbg_hT" in io else nc.dram_tensor(f"{pfx}_hT_d", [128, 8, S], BF16).ap()

    ident = k.sb(f"{pfx}ident", [128, 128], BF16, stack=top)
    tri = k.sb(f"{pfx}tri", [128, 128], BF16, stack=top)
    trin = k.sb(f"{pfx}trin", [128, 128], BF16, stack=top)
    ones = k.sb(f"{pfx}ones", [128, 128], BF16, stack=top)
    masks = k.sb(f"{pfx}masks", [128, 4 * 512], F32, stack=top)
    smp = k.sb(f"{pfx}smp", [128, M_NP], F32, stack=top)
    wa = k.sb(f"{pfx}wa", [128, ncc, 128], BF16, stack=top)
    wx = k.sb(f"{pfx}wx", [128, ncc, 128], BF16, stack=top)
    s_c = k.slot(f"{pfx}const")
    s_c2 = k.slot(f"{pfx}const2")
    k.dma("pool", s_c, ident[:], io["ident"], w=[ident])
    k.dma("pool", s_c, tri[:], io["tri"], w=[tri])
    k.dma("pool", s_c, trin[:], io["trin"], w=[trin])
    k.dma("pool", s_c, ones[:], io["ones"], w=[ones])
    k.dma("pool", s_c, wa[:], io["rgwa"], w=[wa])
    k.dma("pool", s_c, wx[:], io["rgwx"], w=[wx])
    k.dma("sp", s_c2, masks[:], io["masks"], w=[masks])
    k.dma("sp", s_c2, smp[:], io["smallp"], w=[smp])

    modt = k.sb(f"{pfx}modt", [128, 16], F32, stack=top)
    A1 = k.sb(f"{pfx}A1", [128, 8], F32, stack=top)
    negc = k.sb(f"{pfx}negc", [128, 8], F32, stack=top)
    negc2 = k.sb(f"{pfx}negc2", [128, 8], F32, stack=top)
    t4 = k.sb(f"{pfx}t4", [128, 8], F32, stack=top)
    gqs = k.sb(f"{pfx}gqs", [128, 1], F32, stack=top)

    with contextlib.ExitStack() as ps:
        adaw_t = [k.sb(f"{pfx}adaw{i}", [128, 8, 512], F32, stack=ps) for i in range(2)]
        s_aw = [k.slot(f"{pfx}adaw{i}") for i in range(2)]
        pm = bank()
        for piece in range(4):
            at = adaw_t[piece % 2]
            k.dma("sp", s_aw[piece % 2], at[:], io["adaw"][piece], w=[at])
            for jj in range(4):
                j = piece * 4 + jj
                for kc in range(8):
                    _mm(k, pm[:, j:j + 1], at[:, kc, jj * 128:(jj + 1) * 128], smp[:, M_CV + kc:M_CV + kc + 1],
                        kc == 0, kc == 7, r=[at, smp], w=[pm])
        _tt(k, "dve", modt[:], pm[:, 0:16], smp[:, M_BSH:M_BSH + 16], ALU.add, r=[pm, smp], w=[modt])
        _ts(k, "dve", A1[:], modt[:, 8:16], 1.0, None, ALU.add, None, r=[modt], w=[A1])
        _tt(k, "dve", A1[:], A1[:], smp[:, M_G1:M_G1 + 8], ALU.mult, r=[A1, smp], w=[A1])
        _act(k, t4[:], smp[:, M_LAM:M_LAM + 8], AF.Exp, r=[smp], w=[t4], scale=-1.0)
        _act(k, t4[:], t4[:], AF.Ln, r=[t4], w=[t4], bias=1.0)
        _ts(k, "dve", negc[:], t4[:], -8.0, None, ALU.mult, None, r=[t4], w=[negc])
        _ts(k, "dve", negc2[:], t4[:], -16.0, None, ALU.mult, None, r=[t4], w=[negc2])
        _ts(k, "dve", gqs[:], smp[:, M_QG:M_QG + 1], 1.0 / float(np.sqrt(128.0)), None, ALU.mult, None,
            r=[smp], w=[gqs])
        k.barrier()

    with contextlib.ExitStack() as ps:
        xb = [k.sb(f"{pfx}xb{i}", [128, 1024], F32, stack=ps) for i in range(3)]
        s_x = [k.slot(f"{pfx}x{i}") for i in range(3)]
        xh = [k.sb(f"{pfx}xh{i}", [128, 1024], BF16, stack=ps) for i in range(2)]
        junk = k.sb(f"{pfx}junk", [128, 1024], BF16, stack=ps)
        ssq = [k.sb(f"{pfx}ssq{i}", [128, 1], F32, stack=ps) for i in range(2)]
        hTo = [k.sb(f"{pfx}hTo{i}", [128, 8, 512], BF16, stack=ps) for i in range(2)]
        s_ho = [k.slot(f"{pfx}ho{i}") for i in range(2)]
        for tt in range(NT):
            banks = [P[(tt % 2) * 4 + i] for i in range(4)]
            bviews = [b[:].bitcast(BF16) for b in banks]
            for sub in range(4):
                blk = tt * 4 + sub
                xt = xb[blk % 3]
                k.dma("sp", s_x[blk % 3], xt[:], x_d[blk * 128:(blk + 1) * 128, :], w=[xt])
                sq = ssq[blk % 2]
                _act(k, junk[:], xt[:], AF.Square, r=[xt], w=[junk, sq], accum_out=sq[:])
                _act(k, sq[:], sq[:], AF.Sqrt, r=[sq], w=[sq], scale=1.0 / D, bias=EPS)
                k.op("dve", lambda e, sq=sq: e.reciprocal(out=sq[:], in_=sq[:]), r=[sq], w=[sq])
                xht = xh[blk % 2]
                _ts(k, "dve", xht[:, 0:512], xt[:, 0:512], sq[:, 0:1], None, ALU.mult, None, r=[xt, sq], w=[xht])
                _ts(k, "pool", xht[:, 512:1024], xt[:, 512:1024], sq[:, 0:1], None, ALU.mult, None, r=[xt, sq], w=[xht])
                for fc in range(8):
                    bi = fc // 2
                    c0 = (fc % 2) * 512 + sub * 128
                    k.op("pe", lambda e, o=bviews[bi][:, c0:c0 + 128], i_=xht[:, fc * 128:(fc + 1) * 128]:
                         e.transpose(o, i_, ident[:]), r=[xht, ident], w=[banks[bi]])
            ho = hTo[tt % 2]
            for fc in range(8):
                bi = fc // 2
                c0 = (fc % 2) * 512
                _act(k, ho[:, fc, :], bviews[bi][:, c0:c0 + 512], AF.Identity, r=[banks[bi], A1, modt], w=[ho],
                     scale=A1[:, fc:fc + 1], bias=modt[:, fc:fc + 1])
            k.dma("sp", s_ho[tt % 2], hT_d[:, :, tt * 512:(tt + 1) * 512], ho[:], r=[ho],
                  w=[k.dbuf((pfx + "hT", tt))])
        k.barrier()

    with contextlib.ExitStack() as ps:
        W = k.sb(f"{pfx}W", [128, 8, 896], BF16, stack=ps)
        s_w = k.slot(f"{pfx}w")
        hT = [k.sb(f"{pfx}hT{i}", [128, 8, 512], BF16, stack=ps) for i in range(2)]
        s_h = [k.slot(f"{pfx}h{i}") for i in range(2)]
        qT = k.sb(f"{pfx}qT", [128, S], BF16, stack=ps)
        kT = k.sb(f"{pfx}kT", [128, S], BF16, stack=ps)
        vA = k.sb(f"{pfx}vA", [128, NB * 128], BF16, stack=ps)
        mixA = k.sb(f"{pfx}mixA", [128, S], BF16, stack=ps)
        sgb = k.sb(f"{pfx}sgb", [128, S], BF16, stack=ps)
        xraw = k.sb(f"{pfx}xraw", [128, 515], F32, stack=ps)
        state = k.sb(f"{pfx}state", [128, 1], F32, stack=ps)

        def f32t(n):
            return k.sb(f"{pfx}{n}", [128, 512], F32, stack=ps)

        def bf16t(n):
            return k.sb(f"{pfx}{n}", [128, 512], BF16, stack=ps)
        names32 = ("xc", "r", "ig", "a", "a2", "u", "hs", "gl", "ya", "sga", "rq", "rk")
        dbl32 = [[f32t(f"{n}{i}") for n in names32] for i in range(2)]
        dbl16 = [[bf16t(f"{n}{i}") for n in ("xcb", "sqq", "sqk")] for i in range(2)]
        EB = [f32t(f"e{i}") for i in range(4)]
        LB = [bf16t(f"L{i}") for i in range(4)]
        XB = [f32t(f"X{i}") for i in range(3)]
        ATB = [bf16t(f"at{i}") for i in range(4)]
        tmpo = [f32t(f"tmpo{i}") for i in range(2)]
        mixo = [(bf16t if mix_bf16 else f32t)(f"mixo{i}") for i in range(2)]
        s_mo = [k.slot(f"{pfx}mo{i}") for i in range(2)]
        mo_ctr = [0]

        for cc in range(ncc):
            k.dma("pool", s_w, W[:], io["win"][cc], w=[W])
            k.op("dve", lambda e: e.memset(state[:], 0.0), w=[state])
            k.op("dve", lambda e: e.memset(xraw[:, 0:3], 0.0), w=[xraw])
            def load_h(t_):
                k.dma("sp", s_h[t_ % 2], hT[t_ % 2][:], hT_d[:, :, t_ * 512:(t_ + 1) * 512],
                      r=[k.dbuf((pfx + "hT", t_))], w=[hT[t_ % 2]])
            load_h(0)
            for tt in range(NT):
                tsl = slice(tt * 512, (tt + 1) * 512)
                h = hT[tt % 2]
                xc, r_t, ig, a_t, a2, u_t, hs, gl, ya, sga, rq, rk = dbl32[tt % 2]
                xcb, sqq, sqk = dbl16[tt % 2]

                def proj(pb, s_idx):
                    for kc in range(8):
                        _mm(k, pb[:], W[:, kc, s_idx * 128:(s_idx + 1) * 128], h[:, kc, :], kc == 0, kc == 7,
                            r=[W, h], w=[pb])
                pbq = bank()
                proj(pbq, 2)
                _act(k, sqq[:], pbq[:], AF.Square, r=[pbq], w=[sqq])
                pbk = bank()
                proj(pbk, 3)
                _act(k, sqk[:], pbk[:], AF.Square, r=[pbk], w=[sqk])
                pb = bank()
                proj(pb, 0)
                _act(k, xraw[:, 3:515], pb[:], AF.Copy, r=[pb], w=[xraw])
                pb = bank()
                proj(pb, 1)
                _act(k, gl[:], pb[:], AF.Gelu_apprx_tanh, r=[pb], w=[gl])
                cw = M_CW + cc * 4
                _ts(k, "dve", xc[:], xraw[:, 0:512], smp[:, cw:cw + 1], smp[:, M_CB + cc:M_CB + cc + 1],
                    ALU.mult, ALU.add, r=[xraw, smp], w=[xc])
                for tap in range(1, 4):
                    _stt(k, xc[:], xraw[:, tap:tap + 512], smp[:, cw + tap:cw + tap + 1], xc[:], ALU.mult, ALU.add,
                         r=[xraw, xc, smp], w=[xc])
                _copy(k, "pool", xcb[:], xc[:], r=[xc], w=[xcb])
                _copy(k, "pool", xraw[:, 0:3], xraw[:, 512:515], r=[xraw], w=[xraw])
                for (pbx, sqx, rx, gsc, dst) in ((pbq, sqq, rq, gqs[:, 0:1], qT),
                                                 (pbk, sqk, rk, smp[:, M_KG:M_KG + 1], kT)):
                    pbs = bank()
                    _mm(k, pbs[:], ones[:], sqx[:], True, True, r=[ones, sqx], w=[pbs])
                    _act(k, rx[:], pbs[:], AF.Sqrt, r=[pbs], w=[rx], scale=1.0 / 128.0, bias=EPS)
                    k.op("dve", lambda e, rx=rx: e.reciprocal(out=rx[:], in_=rx[:]), r=[rx], w=[rx])
                    _stt(k, dst[:, tsl], pbx[:], gsc, rx[:], ALU.mult, ALU.mult, r=[pbx, rx, gqs, smp], w=[dst])
                pb = bank()
                proj(pb, 5)
                _act(k, sga[:], pb[:], AF.Sigmoid, r=[pb], w=[sga])
                pb = bank()
                proj(pb, 6)
                _act(k, sgb[:, tsl], pb[:], AF.Sigmoid, r=[pb], w=[sgb])
                pbv = bank()
                for sub in range(4):
                    for kc in range(8):
                        _mm(k, pbv[:, sub * 128:(sub + 1) * 128], h[:, kc, sub * 128:(sub + 1) * 128],
                            W[:, kc, 4 * 128:5 * 128], kc == 0, kc == 7, r=[W, h], w=[pbv])
                _copy(k, "dve", vA[:, tt * 512:(tt + 1) * 512], pbv[:], r=[pbv], w=[vA])
                if tt + 1 < NT:
                    load_h(tt + 1)
                pbr = bank()
                _mm(k, pbr[:], wa[:, cc, :], xcb[:], True, True, r=[wa, xcb], w=[pbr])
                pbi = bank()
                _mm(k, pbi[:], wx[:, cc, :], xcb[:], True, True, r=[wx, xcb], w=[pbi])
                _act(k, r_t[:], pbr[:], AF.Sigmoid, r=[pbr, smp], w=[r_t], bias=smp[:, M_BA + cc:M_BA + cc + 1])
                _act(k, ig[:], pbi[:], AF.Sigmoid, r=[pbi, smp], w=[ig], bias=smp[:, M_BX + cc:M_BX + cc + 1])
                _act(k, a_t[:], r_t[:], AF.Exp, r=[r_t, negc], w=[a_t], scale=negc[:, cc:cc + 1])
                _act(k, a2[:], r_t[:], AF.Exp, r=[r_t, negc2], w=[a2], scale=negc2[:, cc:cc + 1])
                _act(k, a2[:], a2[:], AF.Sqrt, r=[a2], w=[a2], scale=-1.0, bias=1.0)
                _tt(k, "pool", u_t[:], ig[:], xc[:], ALU.mult, r=[ig, xc], w=[u_t])
                _tt(k, "pool", u_t[:], u_t[:], a2[:], ALU.mult, r=[u_t, a2], w=[u_t])
                k.op("dve", lambda e, hs=hs, a_t=a_t, u_t=u_t: e.tensor_tensor_scan(
                    out=hs[:], data0=a_t[:], data1=u_t[:], initial=state[:, 0:1], op0=ALU.mult, op1=ALU.add),
                     r=[a_t, u_t, state], w=[hs])
                _copy(k, "dve", state[:], hs[:, 511:512], r=[hs], w=[state])
                _tt(k, "pool", ya[:], hs[:], gl[:], ALU.mult, r=[hs, gl], w=[ya])
                _tt(k, "pool", mixA[:, tsl], ya[:], sga[:], ALU.mult, r=[ya, sga], w=[mixA])

            steps = []
            for qt in range(NT):
                topkb = 4 * qt + 3
                for kb in range(topkb, -1, -1):
                    steps.append(dict(i=len(steps), qt=qt, kb=kb, first=(kb == topkb), last=(kb == 0),
                                      diag=(kb - 4 * qt) if kb >= 4 * qt else None))
            ZP = [P[0], P[1]]
            ACC = [P[2], P[3]]
            OB = [P[4], P[5], P[6]]

            def stA(s):
                zp = ZP[s["i"] % 2]
                kb, qt = s["kb"], s["qt"]
                _mm(k, zp[:], kT[:, kb * 128:(kb + 1) * 128], qT[:, qt * 512:(qt + 1) * 512], True, True,
                    r=[kT, qT], w=[zp])

            def stB(s):
                zp = ZP[s["i"] % 2]
                e_ = EB[s["i"] % 4]
                L = LB[s["i"] % 4]
                _act(k, e_[:], zp[:], AF.Exp, r=[zp], w=[e_])
                if s["diag"] is not None:
                    dg = s["diag"]
                    _tt(k, "dve", e_[:], e_[:], masks[:, dg * 512:(dg + 1) * 512], ALU.mult, r=[e_, masks], w=[e_])
                _act(k, L[:], e_[:], AF.Ln, r=[e_], w=[L], bias=1.0)

            def stC(s):
                acc = ACC[s["qt"] % 2]
                L = LB[s["i"] % 4]
                _mm(k, acc[:], tri[:], L[:], s["first"], s["last"], r=[tri, L], w=[acc])

            def stD(s):
                acc = ACC[s["qt"] % 2]
                X = XB[s["i"] % 3]
                _act(k, X[:], acc[:], AF.Exp, r=[acc], w=[X], scale=-1.0)

            def stE(s):
                if s["last"]:
                    return
                acc = ACC[s["qt"] % 2]
                L = LB[s["i"] % 4]
                _mm(k, acc[:], trin[:], L[:], False, False, r=[trin, L], w=[acc])

            def stF(s):
                at = ATB[s["i"] % 4]
                _tt(k, "dve", at[:], EB[s["i"] % 4][:], XB[s["i"] % 3][:], ALU.mult,
                    r=[EB[s["i"] % 4], XB[s["i"] % 3]], w=[at])

            def stG(s):
                o = OB[s["qt"] % 3]
                at = ATB[s["i"] % 4]
                kb, qt = s["kb"], s["qt"]
                _mm(k, o[:], vA[:, kb * 128:(kb + 1) * 128], at[:], s["first"], s["last"], r=[vA, at], w=[o])
                if s["last"]:
                    qsl = slice(qt * 512, (qt + 1) * 512)
                    j = mo_ctr[0] % 2
                    mo_ctr[0] += 1
                    _tt(k, "dve", tmpo[j][:], o[:], sgb[:, qsl], ALU.mult, r=[o, sgb], w=[tmpo[j]])
                    _tt(k, "pool", mixo[j][:], tmpo[j][:], mixA[:, qsl], ALU.add, r=[tmpo[j], mixA], w=[mixo[j]])
                    k.dma("sp", s_mo[j], mix_d[cc * 128:(cc + 1) * 128, qsl], mixo[j][:], r=[mixo[j]],
                          w=[k.dbuf((pfx + "mix", cc, qt))])

            if "dbg_q" in io and cc == ncc - 1:
                s_dbg = k.slot(f"{pfx}dbg")
                k.dma("sp", s_dbg, io["dbg_small"][:, 0:16], modt[:], r=[modt])
                k.dma("sp", s_dbg, io["dbg_small"][:, 16:24], A1[:], r=[A1])
                for nm, t in (("dbg_q", qT), ("dbg_k", kT), ("dbg_v", vA), ("dbg_mixA", mixA), ("dbg_sgb", sgb)):
                    k.dma("sp", s_dbg, io[nm], t[:], r=[t])
            n = len(steps)
            for it in range(n + 3):
                if it < n:
                    stA(steps[it])
                    stB(steps[it])
                if 0 <= it - 2 < n:
                    stE(steps[it - 2])
                if 0 <= it - 1 < n:
                    stC(steps[it - 1])
                    stD(steps[it - 1])
                if 0 <= it - 2 < n:
                    stF(steps[it - 2])
                if 0 <= it - 3 < n:
                    stG(steps[it - 3])
        k.barrier()
    top.close()


def build_mixer(S, dbg=False):
    nc = bass.Bass("TRN2", target_bir_lowering=False)
    io = {}

    def inp(name, shape, dt=F32):
        io[name] = nc.dram_tensor(name, list(shape), dt, kind="ExternalInput").ap()
    inp("x", [S, D])
    inp("ident", [128, 128])
    inp("tri", [128, 128])
    inp("trin", [128, 128])
    inp("ones", [128, 128])
    inp("masks", [128, 2048])
    inp("smallp", [128, M_NP])
    inp("rgwa", [128, 4, 128])
    inp("rgwx", [128, 4, 128])
    inp("adaw", [4, 128, 8, 512])
    inp("win", [4, 128, 8, 896])
    io["mixT"] = nc.dram_tensor("mixT", [512, S], F32, kind="ExternalOutput").ap()
    if dbg:
        io["dbg_hT"] = nc.dram_tensor("dbg_hT", [128, 8, S], BF16, kind="ExternalOutput").ap()
        io["dbg_small"] = nc.dram_tensor("dbg_small", [128, 24], F32, kind="ExternalOutput").ap()
        for nm in ("dbg_q", "dbg_k", "dbg_v", "dbg_mixA", "dbg_sgb"):
            io[nm] = nc.dram_tensor(nm, [128, S], BF16, kind="ExternalOutput").ap()
    with contextlib.ExitStack() as st:
        k = K(nc, st)
        emit_mixer(k, nc, S, io)
        k.finish()
    return nc


def _pk(v):
    v = np.asarray(v, np.float32)
    return np.ascontiguousarray(v.reshape(-1, 128).T)


def _consts():
    p = np.arange(128)[:, None]
    c = np.arange(128)[None, :]
    tri = (p >= c).astype(np.float32)
    trin = (p < c).astype(np.float32)
    cc = np.arange(512)[None, :]
    masks = np.concatenate([((128 * i + p) < cc).astype(np.float32) for i in range(4)], axis=1)
    return dict(ident=np.eye(128, dtype=np.float32), tri=tri, trin=trin, ones=np.ones((128, 128), np.float32),
                masks=np.ascontiguousarray(masks))


def mixer_weights(l, b, hh, inp):
    if hh is None:
        ch, h0, ncc = slice(0, 1024), 0, 8
    else:
        ch, h0, ncc = slice(hh * 512, (hh + 1) * 512), hh * 4, 4
    d = {}
    ada_w, ada_b = inp["ada_w"][l], inp["ada_b"][l]
    smallp = np.zeros((128, M_NP), np.float32)
    smallp[:, M_CV:M_CV + 8] = _pk(inp["c"][b])
    smallp[:, M_BSH:M_BSH + 8] = _pk(ada_b[0:1024])
    smallp[:, M_BSC:M_BSC + 8] = _pk(ada_b[1024:2048])
    smallp[:, M_G1:M_G1 + 8] = _pk(inp["norm1_g"][l])
    cw = inp["conv_w"][l][:, ch]
    for cc in range(ncc):
        smallp[:, M_CW + cc * 4:M_CW + cc * 4 + 4] = cw[:, cc * 128:(cc + 1) * 128].T
    smallp[:, M_CB:M_CB + ncc] = _pk(inp["conv_b"][l][ch])
    smallp[:, M_BA:M_BA + ncc] = _pk(inp["rg_ba"][l][ch])
    smallp[:, M_BX:M_BX + ncc] = _pk(inp["rg_bx"][l][ch])
    smallp[:, M_LAM:M_LAM + ncc] = _pk(inp["rg_lambda"][l][ch])
    smallp[:, M_QG] = inp["q_norm_g"][l]
    smallp[:, M_KG] = inp["k_norm_g"][l]
    d["smallp"] = smallp
    d["rgwa"] = np.ascontiguousarray(inp["rg_wa"][l][h0:h0 + ncc].transpose(1, 0, 2))
    d["rgwx"] = np.ascontiguousarray(inp["rg_wx"][l][h0:h0 + ncc].transpose(1, 0, 2))
    aw = ada_w[:, 0:2048].reshape(8, 128, 4, 512).transpose(2, 1, 0, 3)
    d["adaw"] = np.ascontiguousarray(aw)
    w_in = inp["w_in"][l]
    w7 = w_in.reshape(8, 128, 7, 8, 128)[:, :, :, h0:h0 + ncc, :]
    d["win"] = np.ascontiguousarray(w7.transpose(3, 1, 0, 2, 4).reshape(ncc, 128, 8, 896))
    return d


def mixer_inputs(l, b, hh, xb, inp):
    d = dict(_consts())
    d["x"] = np.ascontiguousarray(xb, dtype=np.float32)
    d.update(mixer_weights(l, b, hh, inp))
    return d


_PROGS = {}


def _prog(kind, n):
    key = (kind, n)
    if key not in _PROGS:
        _PROGS[key] = build_mixer(n) if kind == "mixer" else build_ffn(n)
    return _PROGS[key]


FUSED = True


def kernel(**inputs):
    inp = {k_: np.asarray(v, dtype=np.float32) for k_, v in inputs.items()}
    x = inp["x"]
    B, S, _ = x.shape
    depth = inp["w_in"].shape[0]
    cores = list(range(8))
    if FUSED:
        key = ("fused", S, depth)
        if key not in _PROGS:
            _PROGS[key] = build_fused(S, depth)
        per_b = [fused_inputs(b, inp, depth) for b in range(B)]
        res = run_bass_kernel_spmd(_PROGS[key], [per_b[c // 2] for c in cores], core_ids=cores)
        half = S // 2
        return np.stack([np.concatenate([np.asarray(res.results[2 * b]["out"])[:half],
                                         np.asarray(res.results[2 * b + 1]["out"])[half:]], axis=0)
                         for b in range(B)], axis=0).astype(np.float32)
    for l in range(depth):
        nc = _prog("mixer", S)
        in_maps = [mixer_inputs(l, c // 2, c % 2, x[c // 2], inp) for c in cores]
        res = run_bass_kernel_spmd(nc, in_maps, core_ids=cores)
        mixT = [np.concatenate([np.asarray(res.results[2 * b]["mixT"]), np.asarray(res.results[2 * b + 1]["mixT"])],
                               axis=0) for b in range(B)]
        nc = _prog("ffn", S // 2)
        in_maps = [ffn_inputs(l, c // 2, c % 2, x[c // 2], mixT[c // 2], inp) for c in cores]
        res = run_bass_kernel_spmd(nc, in_maps, core_ids=cores)
        x = np.stack([np.concatenate([np.asarray(res.results[2 * b]["xout"]), np.asarray(res.results[2 * b + 1]["xout"])],
                                     axis=0) for b in range(B)], axis=0).astype(np.float32)
    return x


F_CV, F_BSH, F_BSC, F_G2, F_CW, F_CB, F_FLAG, F_NP = 0, 8, 16, 24, 32, 176, 224, 225


def emit_ffn(k, nc, S2, io, pfx="f", P=None, pre=True, mix_bf16=False):
    NT = S2 // 512
    x_d, mix_d, out_d = io["xin"], io["mixin"], io["xout"]
    if P is None:
        P = [k.ps(f"{pfx}ps{i}", [128, 512], F32) for i in range(8)]
    top = contextlib.ExitStack()
    bank_ctr = [0]

    def bank():
        b = P[4 + bank_ctr[0] % 4]
        bank_ctr[0] += 1
        return b

    ident = k.sb(f"{pfx}ident", [128, 128], BF16, stack=top)
    smp = k.sb(f"{pfx}smp", [128, F_NP], F32, stack=top)
    wo = k.sb(f"{pfx}wo", [128, 8, 1024], BF16, stack=top)
    wd = k.sb(f"{pfx}wd", [128, 24, 1024], BF16, stack=top)
    gtB = k.sb(f"{pfx}gtB", [128, 2048], F32, stack=top)
    modt = k.sb(f"{pfx}modt", [128, 16], F32, stack=top)
    A2 = k.sb(f"{pfx}A2", [128, 8], F32, stack=top)
    halo = k.sb(f"{pfx}halo", [128, 48, 2], F32, stack=top)
    s_c = k.slot(f"{pfx}const")
    s_c2 = k.slot(f"{pfx}const2")
    k.dma("sp", s_c2, smp[:], io["smallp"], w=[smp])
    k.dma("pool", s_c, ident[:], io["ident"], w=[ident])
    k.dma("pool", s_c, wo[:], io["wout"], w=[wo])
    for q in range(4):
        k.dma("pool", s_c, wd[:, q * 6:(q + 1) * 6, :], io["fdown"][:, q * 6:(q + 1) * 6, :], w=[wd])

    with contextlib.ExitStack() as ps:
        adaw_t = [k.sb(f"{pfx}adaw{i}", [128, 8, 512], F32, stack=ps) for i in range(2)]
        s_aw = [k.slot(f"{pfx}adaw{i}") for i in range(2)]
        cB = k.sb(f"{pfx}cB", [128, 8, 128], F32, stack=ps)
        onesf = k.sb(f"{pfx}onesf", [128, 128], F32, stack=ps)
        gtb_t = k.sb(f"{pfx}gtb_t", [128, 2048], F32, stack=ps)
        k.dma("sp", s_c2, gtb_t[:], io["gtb"].partition_broadcast(128), w=[gtb_t])
        k.op("dve", lambda e: e.memset(onesf[:], 1.0), w=[onesf])
        for kc in range(8):
            _ts(k, "dve", cB[:, kc, :], onesf[:], smp[:, F_CV + kc:F_CV + kc + 1], None, ALU.mult, None,
                r=[onesf, smp], w=[cB])
        pm = bank()
        for piece in range(4):
            at = adaw_t[piece % 2]
            k.dma("sp", s_aw[piece % 2], at[:], io["adaw"][piece], w=[at])
            for jj in range(4):
                j = piece * 4 + jj
                for kc in range(8):
                    _mm(k, pm[:, j:j + 1], at[:, kc, jj * 128:(jj + 1) * 128], smp[:, F_CV + kc:F_CV + kc + 1],
                        kc == 0, kc == 7, r=[at, smp], w=[pm])
        _tt(k, "dve", modt[:], pm[:, 0:16], smp[:, F_BSH:F_BSH + 16], ALU.add, r=[pm, smp], w=[modt])
        _ts(k, "dve", A2[:], modt[:, 8:16], 1.0, None, ALU.add, None, r=[modt], w=[A2])
        _tt(k, "dve", A2[:], A2[:], smp[:, F_G2:F_G2 + 8], ALU.mult, r=[A2, smp], w=[A2])
        for piece in range(4, 8):
            at = adaw_t[piece % 2]
            k.dma("sp", s_aw[piece % 2], at[:], io["adaw"][piece], w=[at])
            pb = bank()
            for kc in range(8):
                _mm(k, pb[:], cB[:, kc, :], at[:, kc, :], kc == 0, kc == 7, r=[cB, at], w=[pb])
            c0 = (piece - 4) * 512
            _tt(k, "dve", gtB[:, c0:c0 + 512], pb[:], gtb_t[:, c0:c0 + 512], ALU.add, r=[pb, gtb_t], w=[gtB])
        for kc in range(8):
            _tt(k, "dve" if kc % 2 == 0 else "pool", wo[:, kc, :], wo[:, kc, :], gtB[:, 0:1024], ALU.mult,
                r=[wo, gtB], w=[wo])
        for j in range(24):
            _tt(k, "dve" if j % 2 == 0 else "pool", wd[:, j, :], wd[:, j, :], gtB[:, 1024:2048], ALU.mult,
                r=[wd, gtB], w=[wd])
        k.barrier()

    with contextlib.ExitStack() as ps:
        mx = k.sb(f"{pfx}mx", [128, 8, 512], BF16, stack=ps)
        s_mx = k.slot(f"{pfx}mx0")
        xt = [k.sb(f"{pfx}xt{i}", [128, 1024], F32, stack=ps) for i in range(2)]
        s_xt = [k.slot(f"{pfx}xt{i}") for i in range(2)]
        x1t = [[k.sb(f"{pfx}x1t{p_}{i}", [128, 1024], F32, stack=ps) for i in range(4)] for p_ in range(2)]
        s_xo = [[k.slot(f"{pfx}xo{p_}{i}") for i in range(4)] for p_ in range(2)]
        xh = [k.sb(f"{pfx}xh{i}", [128, 1024], BF16, stack=ps) for i in range(2)]
        ssq = [k.sb(f"{pfx}ssq{i}", [128, 1], F32, stack=ps) for i in range(2)]
        h2T = [k.sb(f"{pfx}h2T{i}", [128, 8, 512], BF16, stack=ps) for i in range(2)]
        Wj = [k.sb(f"{pfx}Wj{i}", [128, 8, 256], BF16, stack=ps) for i in range(3)]
        s_wj = [k.slot(f"{pfx}wj{i}") for i in range(3)]
        raw = [k.sb(f"{pfx}raw{i}", [128, 514], F32, stack=ps) for i in range(4)]
        tcv = [k.sb(f"{pfx}tcv{i}", [128, 512], F32, stack=ps) for i in range(4)]
        gg = [k.sb(f"{pfx}gg{i}", [128, 512], F32, stack=ps) for i in range(2)]
        actT = k.sb(f"{pfx}actT", [128, 24, 512], BF16, stack=ps)
        tmp = [k.sb(f"{pfx}tmp{i}", [128, 512], F32, stack=ps) for i in range(2)]
        ctr = dict(x=0, raw=0, t=0, g=0, tmp=0, tmp1=0)
        mixv = mix_d.rearrange("(kc p) t -> p kc t", p=128)
        tiles = ([(0, 128, True, 0)] if pre else []) + \
            [((128 if pre else 0) + tt * 512, 512, False, tt * 512) for tt in range(NT)]
        wst = dict(issued=0, total=24 * len(tiles))
        tb = [P[i] for i in range(4)]
        tbv = [b_[:].bitcast(BF16) for b_ in tb]

        def issue_w(upto):
            while wst["issued"] <= upto and wst["issued"] < wst["total"]:
                g = wst["issued"]
                if "fup_bf" in io:
                    k.dma("sp", s_wj[g % 3], Wj[g % 3][:], io["fup_bf"][g % 24], w=[Wj[g % 3]])
                else:
                    k.dma("pool", s_wj[g % 3], Wj[g % 3][:], io["fup"][g % 24], w=[Wj[g % 3]])
                wst["issued"] += 1

        def st1_load(ti):
            row0, ntok = tiles[ti][0], tiles[ti][1]
            k.dma("sp" if mix_bf16 else "pool", s_mx, mx[:, :, 0:ntok], mixv[:, :, row0:row0 + ntok], w=[mx])

        st1x = {}

        def st1_a0(ti, sub):
            row0 = tiles[ti][0]
            xi = ctr["x"] % 2
            ctr["x"] += 1
            xx = xt[xi]
            k.dma("sp", s_xt[xi], xx[:], x_d[row0 + sub * 128:row0 + (sub + 1) * 128, :], w=[xx])
            tps = []
            for half in range(2):
                hs_ = slice(half * 512, (half + 1) * 512)
                po = bank()
                for kc in range(8):
                    _mm(k, po[:], mx[:, kc, sub * 128:(sub + 1) * 128], wo[:, kc, hs_], kc == 0, kc == 7,
                        r=[mx, wo], w=[po])
                _tt(k, "dve", x1t[ti % 2][sub][:, hs_], po[:], xx[:, hs_], ALU.add, r=[po, xx],
                    w=[x1t[ti % 2][sub]])

        def st1_a1(ti, sub):
            pass

        def st1_a2(ti, sub):
            x1 = x1t[ti % 2][sub]
            sq = ssq[sub % 2]
            xht = xh[sub % 2]
            _act(k, xht[:], x1[:], AF.Square, r=[x1], w=[xht, sq], accum_out=sq[:])
            _act(k, sq[:], sq[:], AF.Sqrt, r=[sq], w=[sq], scale=1.0 / D, bias=EPS)

        def st1_a3(ti, sub):
            x1 = x1t[ti % 2][sub]
            sq = ssq[sub % 2]
            xht = xh[sub % 2]
            k.op("dve", lambda e, sq=sq: e.reciprocal(out=sq[:], in_=sq[:]), r=[sq], w=[sq])
            _ts(k, "dve", xht[:, 0:512], x1[:, 0:512], sq[:, 0:1], None, ALU.mult, None, r=[x1, sq], w=[xht])
            _ts(k, "pool", xht[:, 512:1024], x1[:, 512:1024], sq[:, 0:1], None, ALU.mult, None, r=[x1, sq], w=[xht])

        def st1_a(ti, sub):
            st1_a0(ti, sub)
            st1_a1(ti, sub)
            st1_a2(ti, sub)
            st1_a3(ti, sub)

        def st1_b(ti, sub):
            xht = xh[sub % 2]
            for fc in range(8):
                bi = fc // 2
                c0 = (fc % 2) * 512 + sub * 128
                k.op("pe", lambda e, o=tbv[bi][:, c0:c0 + 128], i_=xht[:, fc * 128:(fc + 1) * 128]:
                     e.transpose(o, i_, ident[:]), r=[xht, ident], w=[tb[bi]])

        def st1_fin(ti, fcs=range(8)):
            ntok = tiles[ti][1]
            h2 = h2T[ti % 2]
            for fc in fcs:
                bi = fc // 2
                c0 = (fc % 2) * 512
                _act(k, h2[:, fc, 0:ntok], tbv[bi][:, c0:c0 + ntok], AF.Identity, r=[tb[bi], A2, modt], w=[h2],
                     scale=A2[:, fc:fc + 1], bias=modt[:, fc:fc + 1])

        pend = []

        def up_chunk(ti, j):
            ntok, is_pre = tiles[ti][1], tiles[ti][2]
            h2 = h2T[ti % 2]
            g = ti * 24 + j
            issue_w(g + 2)
            wj = Wj[g % 3]
            res = []
            for br in range(2):
                cidx = br * 24 + j
                pb = bank()
                for kc in range(8):
                    _mm(k, pb[:, 0:ntok], wj[:, kc, br * 128:(br + 1) * 128], h2[:, kc, 0:ntok], kc == 0, kc == 7,
                        r=[wj, h2], w=[pb])
                if is_pre:
                    _ts(k, "dve", halo[:, cidx, :], pb[:, ntok - 2:ntok], smp[:, F_FLAG:F_FLAG + 1], None,
                        ALU.mult, None, r=[pb, smp], w=[halo])
                    continue
                rw = raw[ctr["raw"] % 4]
                ctr["raw"] += 1
                _act(k, rw[:, 2:2 + ntok], pb[:, 0:ntok], AF.Copy, r=[pb], w=[rw])
                _copy(k, "pool", rw[:, 0:2], halo[:, cidx, :], r=[halo], w=[rw])
                tc = tcv[ctr["t"] % 4]
                ctr["t"] += 1
                cw = F_CW + cidx * 3
                _act(k, tc[:], rw[:, 0:512], AF.Identity, r=[rw, smp], w=[tc], scale=smp[:, cw:cw + 1],
                     bias=smp[:, F_CB + cidx:F_CB + cidx + 1])
                for tap in (1, 2):
                    _stt(k, tc[:], rw[:, tap:tap + 512], smp[:, cw + tap:cw + tap + 1], tc[:], ALU.mult, ALU.add,
                         r=[rw, tc, smp], w=[tc])
                _copy(k, "pool", halo[:, cidx, :], rw[:, 512:514], r=[rw], w=[halo])
                res.append(tc)
            if is_pre:
                return
            pend.append((j, res))

        def up_tail():
            if not pend:
                return
            j, res = pend.pop(0)
            g_ = gg[ctr["g"] % 2]
            ctr["g"] += 1
            _act(k, g_[:], res[0][:], AF.Gelu_apprx_tanh, r=[res[0]], w=[g_])
            _tt(k, "dve" if j % 2 == 1 else "pool", actT[:, j, :], g_[:], res[1][:], ALU.mult, r=[g_, res[1]], w=[actT])

        def down(ti):
            ntok, orow0 = tiles[ti][1], tiles[ti][3]
            for sub in range(ntok // 128):
                x1 = x1t[ti % 2][sub]
                for half in range(2):
                    hs_ = slice(half * 512, (half + 1) * 512)
                    pd = bank()
                    for j in range(24):
                        _mm(k, pd[:], actT[:, j, sub * 128:(sub + 1) * 128], wd[:, j, hs_], j == 0, j == 23,
                            r=[actT, wd], w=[pd])
                    _tt(k, "dve", x1[:, hs_], pd[:], x1[:, hs_], ALU.add, r=[pd, x1], w=[x1])
                r0 = orow0 + sub * 128
                k.dma("sp", s_xo[ti % 2][sub], out_d[r0:r0 + 128, :], x1[:], r=[x1], w=[k.dbuf((pfx + "out", r0))])

        if not pre:
            k.op("dve", lambda e: e.memset(halo[:], 0.0), w=[halo])
        nt_all = len(tiles)
        st1_load(0)
        for sub in range(tiles[0][1] // 128):
            st1_a(0, sub)
            st1_b(0, sub)
        st1_fin(0)
        for ti in range(nt_all):
            issue_w(ti * 24 + 1)
            sched = {}
            if ti + 1 < nt_all:
                st1_load(ti + 1)
                for sub in range(tiles[ti + 1][1] // 128):
                    j0 = 5 * sub
                    for dj, fn_ in ((0, st1_a0), (2, st1_a2), (3, st1_a3), (6, st1_b)):
                        sched.setdefault(j0 + dj, []).append(lambda ti=ti, sub=sub, fn_=fn_: fn_(ti + 1, sub))
                sched.setdefault(22, []).append(lambda ti=ti: st1_fin(ti + 1, range(0, 4)))
                sched.setdefault(23, []).append(lambda ti=ti: st1_fin(ti + 1, range(4, 8)))
            for j in range(24):
                up_chunk(ti, j)
                if j >= 1:
                    up_tail()
                for f_ in sched.get(j, []):
                    f_()
            up_tail()
            issue_w((ti + 1) * 24 + 1)
            if not tiles[ti][2]:
                down(ti)
        k.barrier()
    top.close()


def build_ffn(S2):
    nc = bass.Bass("TRN2", target_bir_lowering=False)
    io = {}

    def inp(name, shape, dt=F32):
        io[name] = nc.dram_tensor(name, list(shape), dt, kind="ExternalInput").ap()
    inp("xin", [128 + S2, D])
    inp("mixin", [D, 128 + S2])
    inp("ident", [128, 128])
    inp("smallp", [128, F_NP])
    inp("gtb", [2048])
    inp("adaw", [8, 128, 8, 512])
    inp("wout", [128, 8, 1024])
    inp("fup", [24, 128, 8, 256])
    inp("fdown", [128, 24, 1024])
    io["xout"] = nc.dram_tensor("xout", [S2, D], F32, kind="ExternalOutput").ap()
    with contextlib.ExitStack() as st:
        k = K(nc, st)
        emit_ffn(k, nc, S2, io)
        k.finish()
    return nc


def ffn_weights(l, b, flag, inp):
    d = {}
    ada_w, ada_b = inp["ada_w"][l], inp["ada_b"][l]
    smallp = np.zeros((128, F_NP), np.float32)
    smallp[:, F_CV:F_CV + 8] = _pk(inp["c"][b])
    smallp[:, F_BSH:F_BSH + 8] = _pk(ada_b[3072:4096])
    smallp[:, F_BSC:F_BSC + 8] = _pk(ada_b[4096:5120])
    smallp[:, F_G2:F_G2 + 8] = _pk(inp["norm2_g"][l])
    cw = inp["ffn_conv_w"][l]
    smallp[:, F_CW:F_CW + 144] = cw.reshape(3, 48, 128).transpose(2, 1, 0).reshape(128, 144)
    smallp[:, F_CB:F_CB + 48] = _pk(inp["ffn_conv_b"][l])
    smallp[:, F_FLAG] = flag
    d["smallp"] = smallp
    d["gtb"] = np.ascontiguousarray(np.concatenate([ada_b[2048:3072], ada_b[5120:6144]]))
    cols = np.concatenate([np.arange(3072, 5120), np.arange(2048, 3072), np.arange(5120, 6144)])
    aw = ada_w[:, cols].reshape(8, 128, 8, 512).transpose(2, 1, 0, 3)
    d["adaw"] = np.ascontiguousarray(aw)
    d["wout"] = np.ascontiguousarray(inp["w_out"][l].reshape(8, 128, 1024).transpose(1, 0, 2))
    fu = inp["ffn_up"][l].reshape(8, 128, 2, 24, 128)
    d["fup"] = np.ascontiguousarray(fu.transpose(3, 1, 0, 2, 4).reshape(24, 128, 8, 256))
    d["fdown"] = np.ascontiguousarray(inp["ffn_down"][l].reshape(24, 128, 1024).transpose(1, 0, 2))
    return d


def ffn_inputs(l, b, th, xb, mixTb, inp):
    S = xb.shape[0]
    S2 = S // 2
    t0 = th * S2
    p0 = t0 - 128 if th > 0 else 0
    d = dict(ident=np.eye(128, dtype=np.float32))
    d["xin"] = np.ascontiguousarray(np.concatenate([xb[p0:p0 + 128], xb[t0:t0 + S2]], axis=0), dtype=np.float32)
    d["mixin"] = np.ascontiguousarray(np.concatenate([mixTb[:, p0:p0 + 128], mixTb[:, t0:t0 + S2]], axis=1),
                                      dtype=np.float32)
    d.update(ffn_weights(l, b, 1.0 if th > 0 else 0.0, inp))
    return d


_MW = ("smallp", "rgwa", "rgwx", "adaw", "win")
_FW = ("smallp", "gtb", "adaw", "wout", "fup", "fdown")


def build_fused(S, depth=2, skip=()):
    nc = bass.Bass("TRN2", target_bir_lowering=False)
    ext = {}

    def inp(name, shape, dt=F32):
        ext[name] = nc.dram_tensor(name, list(shape), dt, kind="ExternalInput").ap()
    inp("x", [S, D])
    for n_ in ("ident", "tri", "trin", "ones"):
        inp(n_, [128, 128])
    inp("masks", [128, 2048])
    for l in range(depth):
        inp(f"m{l}_smallp", [128, M_NP])
        inp(f"m{l}_rgwa", [128, 8, 128])
        inp(f"m{l}_rgwx", [128, 8, 128])
        inp(f"m{l}_adaw", [4, 128, 8, 512])
        inp(f"m{l}_win", [8, 128, 8, 896])
        inp(f"f{l}_smallp", [128, F_NP])
        inp(f"f{l}_gtb", [2048])
        inp(f"f{l}_adaw", [8, 128, 8, 512])
        inp(f"f{l}_wout", [128, 8, 1024])
        inp(f"f{l}_fup", [24, 128, 8, 256])
        inp(f"f{l}_fdown", [128, 24, 1024])
    out = nc.dram_tensor("out", [S, D], F32, kind="ExternalOutput").ap()
    mixT_d = nc.dram_tensor("mixT_d", [D, S], BF16).ap()
    xmid = [nc.dram_tensor(f"xmid{l}", [S, D], F32).ap() for l in range(depth - 1)]
    fupb = [nc.dram_tensor(f"fupb{l}", [24, 128, 8, 256], BF16).ap() for l in range(depth)]
    with contextlib.ExitStack() as st:
        k = K(nc, st)
        P = [k.ps(f"ps{i}", [128, 512], F32) for i in range(8)]
        for l in range(depth):
            x_in = ext["x"] if l == 0 else xmid[l - 1]
            x_out = out if l == depth - 1 else xmid[l]
            io = dict(x=x_in, mixT=mixT_d, ident=ext["ident"], tri=ext["tri"], trin=ext["trin"], ones=ext["ones"],
                      masks=ext["masks"])
            for n_ in _MW:
                io[n_] = ext[f"m{l}_{n_}"]
            s_cast = k.slot("fcast")
            for j in range(24):
                k.dma("pool", s_cast, fupb[l][j], ext[f"f{l}_fup"][j], w=[k.dbuf(("fupb", l, j))])
            if f"m{l}" not in skip:
                emit_mixer(k, nc, S, io, pfx=f"m{l}", P=P, ncc=8, mix_bf16=True)
            io = dict(xin=x_in, mixin=mixT_d, xout=x_out, ident=ext["ident"], fup_bf=fupb[l])
            for n_ in _FW:
                io[n_] = ext[f"f{l}_{n_}"]
            if f"f{l}" not in skip:
                emit_ffn(k, nc, S, io, pfx=f"f{l}", P=P, pre=False, mix_bf16=True)
        k.finish()
    return nc


def fused_inputs(b, inp, depth):
    d = dict(_consts())
    d["x"] = np.ascontiguousarray(inp["x"][b], dtype=np.float32)
    for l in range(depth):
        for n_, v in mixer_weights(l, b, None, inp).items():
            d[f"m{l}_{n_}"] = v
        for n_, v in ffn_weights(l, b, 0.0, inp).items():
            d[f"f{l}_{n_}"] = v
    return d
```

```python
import contextlib
import re
import numpy as np
import concourse.bass as bass
import concourse.mybir as mybir
from concourse.bass_utils import run_bass_kernel_spmd

F32 = mybir.dt.float32
BF16 = mybir.dt.bfloat16
AF = mybir.ActivationFunctionType
ALU = mybir.AluOpType

D = 1024
NH = 8
EPS = 1e-6
DFF = 3072


class _Eng:
    def __init__(self, name, sem, is_pe=False):
        self.name = name
        self.sem = sem
        self.count = 0
        self.waited = {}
        self.prog = []
        self.is_pe = is_pe
        self.is_slot = False
        self.pending = []


class _Slot:
    def __init__(self, name, sem):
        self.name = name
        self.sem = sem
        self.count = 0
        self.is_slot = True


class Buf:
    __slots__ = ("last_w", "readers", "name")

    def __init__(self, name=""):
        self.last_w = None
        self.readers = {}
        self.name = name


class TT:
    def __init__(self, t, name):
        self.t = t
        self.buf = Buf(name)

    def __getitem__(self, key):
        return self.t[key]


class K:
    def __init__(self, nc, stack):
        self.nc = nc
        self.stack = stack
        self.engs = {}
        for name in ("pe", "act", "dve", "pool", "sp"):
            sem = stack.enter_context(nc.semaphore("sem_" + name))
            self.engs[name] = _Eng(name, sem, is_pe=(name == "pe"))
        self.slots = []
        self._slot_cache = {}
        self.dbufs = {}
        self.n_inst = 0

    def slot(self, name):
        key = re.sub(r"^([mf])\d", r"\1", name)
        if key in self._slot_cache:
            return self._slot_cache[key]
        sem = self.stack.enter_context(self.nc.semaphore("slot_" + key))
        s = _Slot(key, sem)
        self.slots.append(s)
        self._slot_cache[key] = s
        return s

    def sb(self, name, shape, dtype, stack=None):
        st = stack if stack is not None else self.stack
        t = st.enter_context(self.nc.sbuf_tensor(name, list(shape), dtype))
        return TT(t, name)

    def ps(self, name, shape, dtype, stack=None):
        st = stack if stack is not None else self.stack
        t = st.enter_context(self.nc.psum_tensor(name, list(shape), dtype))
        return TT(t, name)

    def dbuf(self, key):
        b = self.dbufs.get(key)
        if b is None:
            b = Buf(str(key))
            self.dbufs[key] = b
        return b

    @staticmethod
    def _b(x):
        return x.buf if isinstance(x, TT) else x

    def _waits(self, E, reads, writes):
        deps = {}

        def add(obj, val):
            if deps.get(obj, 0) < val:
                deps[obj] = val
        for b in reads:
            b = self._b(b)
            if b.last_w is not None:
                add(*b.last_w)
        for b in writes:
            b = self._b(b)
            if b.last_w is not None:
                add(*b.last_w)
            for o, v in b.readers.items():
                add(o, v)
        waits = []
        for obj, val in deps.items():
            if obj.is_slot:
                val = obj.count
            elif obj is E:
                if E.is_pe:
                    continue
                if E.count - val >= 2:
                    continue
            if E.waited.get(obj, 0) >= val:
                continue
            E.waited[obj] = val
            waits.append((obj.sem, val))
        return waits

    def barrier(self):
        for E in self.engs.values():
            for o in list(self.engs.values()) + self.slots:
                if o is E or o.count == 0:
                    continue
                if E.waited.get(o, 0) >= o.count:
                    continue
                E.waited[o] = o.count
                E.pending.append((o.sem, o.count))

    def op(self, eng, fn, r=(), w=()):
        E = self.engs[eng]
        waits = self._waits(E, r, w)
        waits = E.pending + waits
        E.pending = []
        E.count += 1
        idx = E.count
        E.prog.append((waits, fn, (E.sem, 1)))
        for b in r:
            b = self._b(b)
            if b.readers.get(E, 0) < idx:
                b.readers[E] = idx
        for b in w:
            b = self._b(b)
            b.last_w = (E, idx)
            b.readers = {}
        self.n_inst += 1

    def dma(self, q, slot, out, in_, r=(), w=()):
        E = self.engs[q]
        waits = self._waits(E, r, w)
        waits = E.pending + waits
        E.pending = []
        slot.count += 16
        val = slot.count
        E.prog.append((waits, lambda e, out=out, in_=in_: e.dma_start(out=out, in_=in_), (slot.sem, 16)))
        for b in r:
            b = self._b(b)
            b.readers[slot] = val
        for b in w:
            b = self._b(b)
            b.last_w = (slot, val)
            b.readers = {}
        self.n_inst += 1

    def finish(self):
        E = self.engs["sp"]
        final_waits = [(s.sem, s.count) for s in self.slots if s.count > 0]
        for n in ("pe", "act", "dve", "pool"):
            e2 = self.engs[n]
            if e2.count > 0:
                final_waits.append((e2.sem, e2.count))
        progs = {n: e.prog for n, e in self.engs.items()}
        nc = self.nc
        with nc.Block() as block:
            def run(e, prog, extra=()):
                for waits, fn, inc in prog:
                    for sem, val in waits:
                        e.wait_ge(sem, val)
                    ins = fn(e)
                    ins.then_inc(inc[0], inc[1])
                for sem, val in extra:
                    e.wait_ge(sem, val)

            @block.sync
            def _(e):
                run(e, progs["sp"], final_waits)

            @block.tensor
            def _(e):
                run(e, progs["pe"])

            @block.scalar
            def _(e):
                run(e, progs["act"])

            @block.vector
            def _(e):
                run(e, progs["dve"])

            @block.gpsimd
            def _(e):
                run(e, progs["pool"])


def _act(k, out, in_, func, r, w, **kw):
    k.op("act", lambda e: e.activation(out=out, in_=in_, func=func, **kw), r=r, w=w)


def _mm(k, out, lhsT, rhs, start, stop, r, w):
    k.op("pe", lambda e: e.matmul(out, lhsT=lhsT, rhs=rhs, start=start, stop=stop), r=r, w=w)


def _tt(k, eng, out, in0, in1, op, r, w):
    k.op(eng, lambda e: e.tensor_tensor(out=out, in0=in0, in1=in1, op=op), r=r, w=w)


def _ts(k, eng, out, in0, s1, s2, op0, op1, r, w):
    if op1 is None:
        k.op(eng, lambda e: e.tensor_scalar(out=out, in0=in0, scalar1=s1, scalar2=None, op0=op0), r=r, w=w)
    else:
        k.op(eng, lambda e: e.tensor_scalar(out=out, in0=in0, scalar1=s1, scalar2=s2, op0=op0, op1=op1), r=r, w=w)


def _stt(k, out, in0, scalar, in1, op0, op1, r, w):
    k.op("dve", lambda e: e.scalar_tensor_tensor(out=out, in0=in0, scalar=scalar, in1=in1, op0=op0, op1=op1),
         r=r, w=w)


def _copy(k, eng, out, in_, r, w):
    k.op(eng, lambda e: e.tensor_copy(out=out, in_=in_), r=r, w=w)


M_CV, M_BSH, M_BSC, M_G1, M_CW, M_CB, M_BA, M_BX, M_LAM, M_QG, M_KG, M_NP = 0, 8, 16, 24, 32, 64, 72, 80, 88, 96, 97, 98


def emit_mixer(k, nc, S, io, pfx="m", P=None, ncc=4, mix_bf16=False):
    NT = S // 512
    NB = S // 128
    x_d, mix_d = io["x"], io["mixT"]
    if P is None:
        P = [k.ps(f"{pfx}ps{i}", [128, 512], F32) for i in range(8)]
    top = contextlib.ExitStack()
    bank_ctr = [0]

    def bank():
        b = P[bank_ctr[0] % 8]
        bank_ctr[0] += 1
        return b

    hT_d = io["dbg_hT"] if "dbg_hT" in io else nc.dram_tensor(f"{pfx}_hT_d", [128, 8, S], BF16).ap()

    ident = k.sb(f"{pfx}ident", [128, 128], BF16, stack=top)
    tri = k.sb(f"{pfx}tri", [128, 128], BF16, stack=top)
    trin = k.sb(f"{pfx}trin", [128, 128], BF16, stack=top)
    ones = k.sb(f"{pfx}ones", [128, 128], BF16, stack=top)
    masks = k.sb(f"{pfx}masks", [128, 4 * 512], F32, stack=top)
    smp = k.sb(f"{pfx}smp", [128, M_NP], F32, stack=top)
    wa = k.sb(f"{pfx}wa", [128, ncc, 128], BF16, stack=top)
    wx = k.sb(f"{pfx}wx", [128, ncc, 128], BF16, stack=top)
    s_c = k.slot(f"{pfx}const")
    s_c2 = k.slot(f"{pfx}const2")
    k.dma("pool", s_c, ident[:], io["ident"], w=[ident])
    k.dma("pool", s_c, tri[:], io["tri"], w=[tri])
    k.dma("pool", s_c, trin[:], io["trin"], w=[trin])
    k.dma("pool", s_c, ones[:], io["ones"], w=[ones])
    k.dma("pool", s_c, wa[:], io["rgwa"], w=[wa])
    k.dma("pool", s_c, wx[:], io["rgwx"], w=[wx])
    k.dma("sp", s_c2, masks[:], io["masks"], w=[masks])
    k.dma("sp", s_c2, smp[:], io["smallp"], w=[smp])

    modt = k.sb(f"{pfx}modt", [128, 16], F32, stack=top)
    A1 = k.sb(f"{pfx}A1", [128, 8], F32, stack=top)
    negc = k.sb(f"{pfx}negc", [128, 8], F32, stack=top)
    negc2 = k.sb(f"{pfx}negc2", [128, 8], F32, stack=top)
    t4 = k.sb(f"{pfx}t4", [128, 8], F32, stack=top)
    gqs = k.sb(f"{pfx}gqs", [128, 1], F32, stack=top)

    with contextlib.ExitStack() as ps:
        adaw_t = [k.sb(f"{pfx}adaw{i}", [128, 8, 512], F32, stack=ps) for i in range(2)]
        s_aw = [k.slot(f"{pfx}adaw{i}") for i in range(2)]
        pm = bank()
        for piece in range(4):
            at = adaw_t[piece % 2]
            k.dma("sp", s_aw[piece % 2], at[:], io["adaw"][piece], w=[at])
            for jj in range(4):
                j = piece * 4 + jj
                for kc in range(8):
                    _mm(k, pm[:, j:j + 1], at[:, kc, jj * 128:(jj + 1) * 128], smp[:, M_CV + kc:M_CV + kc + 1],
                        kc == 0, kc == 7, r=[at, smp], w=[pm])
        _tt(k, "dve", modt[:], pm[:, 0:16], smp[:, M_BSH:M_BSH + 16], ALU.add, r=[pm, smp], w=[modt])
        _ts(k, "dve", A1[:], modt[:, 8:16], 1.0, None, ALU.add, None, r=[modt], w=[A1])
        _tt(k, "dve", A1[:], A1[:], smp[:, M_G1:M_G1 + 8], ALU.mult, r=[A1, smp], w=[A1])
        _act(k, t4[:], smp[:, M_LAM:M_LAM + 8], AF.Exp, r=[smp], w=[t4], scale=-1.0)
        _act(k, t4[:], t4[:], AF.Ln, r=[t4], w=[t4], bias=1.0)
        _ts(k, "dve", negc[:], t4[:], -8.0, None, ALU.mult, None, r=[t4], w=[negc])
        _ts(k, "dve", negc2[:], t4[:], -16.0, None, ALU.mult, None, r=[t4], w=[negc2])
        _ts(k, "dve", gqs[:], smp[:, M_QG:M_QG + 1], 1.0 / float(np.sqrt(128.0)), None, ALU.mult, None,
            r=[smp], w=[gqs])
        k.barrier()

    with contextlib.ExitStack() as ps:
        xb = [k.sb(f"{pfx}xb{i}", [128, 1024], F32, stack=ps) for i in range(3)]
        s_x = [k.slot(f"{pfx}x{i}") for i in range(3)]
        xh = [k.sb(f"{pfx}xh{i}", [128, 1024], BF16, stack=ps) for i in range(2)]
        junk = k.sb(f"{pfx}junk", [128, 1024], BF16, stack=ps)
        ssq = [k.sb(f"{pfx}ssq{i}", [128, 1], F32, stack=ps) for i in range(2)]
        hTo = [k.sb(f"{pfx}hTo{i}", [128, 8, 512], BF16, stack=ps) for i in range(2)]
        s_ho = [k.slot(f"{pfx}ho{i}") for i in range(2)]
        for tt in range(NT):
            banks = [P[(tt % 2) * 4 + i] for i in range(4)]
            bviews = [b[:].bitcast(BF16) for b in banks]
            for sub in range(4):
                blk = tt * 4 + sub
                xt = xb[blk % 3]
                k.dma("sp", s_x[blk % 3], xt[:], x_d[blk * 128:(blk + 1) * 128, :], w=[xt])
                sq = ssq[blk % 2]
                _act(k, junk[:], xt[:], AF.Square, r=[xt], w=[junk, sq], accum_out=sq[:])
                _act(k, sq[:], sq[:], AF.Sqrt, r=[sq], w=[sq], scale=1.0 / D, bias=EPS)
                k.op("dve", lambda e, sq=sq: e.reciprocal(out=sq[:], in_=sq[:]), r=[sq], w=[sq])
                xht = xh[blk % 2]
                _ts(k, "dve", xht[:, 0:512], xt[:, 0:512], sq[:, 0:1], None, ALU.mult, None, r=[xt, sq], w=[xht])
                _ts(k, "pool", xht[:, 512:1024], xt[:, 512:1024], sq[:, 0:1], None, ALU.mult, None, r=[xt, sq], w=[xht])
                for fc in range(8):
                    bi = fc // 2
                    c0 = (fc % 2) * 512 + sub * 128
                    k.op("pe", lambda e, o=bviews[bi][:, c0:c0 + 128], i_=xht[:, fc * 128:(fc + 1) * 128]:
                         e.transpose(o, i_, ident[:]), r=[xht, ident], w=[banks[bi]])
            ho = hTo[tt % 2]
            for fc in range(8):
                bi = fc // 2
                c0 = (fc % 2) * 512
                _act(k, ho[:, fc, :], bviews[bi][:, c0:c0 + 512], AF.Identity, r=[banks[bi], A1, modt], w=[ho],
                     scale=A1[:, fc:fc + 1], bias=modt[:, fc:fc + 1])
            k.dma("sp", s_ho[tt % 2], hT_d[:, :, tt * 512:(tt + 1) * 512], ho[:], r=[ho],
                  w=[k.dbuf((pfx + "hT", tt))])
        k.barrier()

    with contextlib.ExitStack() as ps:
        W = k.sb(f"{pfx}W", [128, 8, 896], BF16, stack=ps)
        s_w = k.slot(f"{pfx}w")
        hT = [k.sb(f"{pfx}hT{i}", [128, 8, 512], BF16, stack=ps) for i in range(2)]
        s_h = [k.slot(f"{pfx}h{i}") for i in range(2)]
        qT = k.sb(f"{pfx}qT", [128, S], BF16, stack=ps)
        kT = k.sb(f"{pfx}kT", [128, S], BF16, stack=ps)
        vA = k.sb(f"{pfx}vA", [128, NB * 128], BF16, stack=ps)
        mixA = k.sb(f"{pfx}mixA", [128, S], BF16, stack=ps)
        sgb = k.sb(f"{pfx}sgb", [128, S], BF16, stack=ps)
        xraw = k.sb(f"{pfx}xraw", [128, 515], F32, stack=ps)
        state = k.sb(f"{pfx}state", [128, 1], F32, stack=ps)

        def f32t(n):
            return k.sb(f"{pfx}{n}", [128, 512], F32, stack=ps)

        def bf16t(n):
            return k.sb(f"{pfx}{n}", [128, 512], BF16, stack=ps)
        names32 = ("xc", "r", "ig", "a", "a2", "u", "hs", "gl", "ya", "sga", "rq", "rk")
        dbl32 = [[f32t(f"{n}{i}") for n in names32] for i in range(2)]
        dbl16 = [[bf16t(f"{n}{i}") for n in ("xcb", "sqq", "sqk")] for i in range(2)]
        EB = [f32t(f"e{i}") for i in range(4)]
        LB = [bf16t(f"L{i}") for i in range(4)]
        XB = [f32t(f"X{i}") for i in range(3)]
        ATB = [bf16t(f"at{i}") for i in range(4)]
        tmpo = [f32t(f"tmpo{i}") for i in range(2)]
        mixo = [(bf16t if mix_bf16 else f32t)(f"mixo{i}") for i in range(2)]
        s_mo = [k.slot(f"{pfx}mo{i}") for i in range(2)]
        mo_ctr = [0]

        for cc in range(ncc):
            k.dma("pool", s_w, W[:], io["win"][cc], w=[W])
            k.op("dve", lambda e: e.memset(state[:], 0.0), w=[state])
            k.op("dve", lambda e: e.memset(xraw[:, 0:3], 0.0), w=[xraw])
            def load_h(t_):
                k.dma("sp", s_h[t_ % 2], hT[t_ % 2][:], hT_d[:, :, t_ * 512:(t_ + 1) * 512],
                      r=[k.dbuf((pfx + "hT", t_))], w=[hT[t_ % 2]])
            load_h(0)
            for tt in range(NT):
                tsl = slice(tt * 512, (tt + 1) * 512)
                h = hT[tt % 2]
                xc, r_t, ig, a_t, a2, u_t, hs, gl, ya, sga, rq, rk = dbl32[tt % 2]
                xcb, sqq, sqk = dbl16[tt % 2]

                def proj(pb, s_idx):
                    for kc in range(8):
                        _mm(k, pb[:], W[:, kc, s_idx * 128:(s_idx + 1) * 128], h[:, kc, :], kc == 0, kc == 7,
                            r=[W, h], w=[pb])
                pbq = bank()
                proj(pbq, 2)
                _act(k, sqq[:], pbq[:], AF.Square, r=[pbq], w=[sqq])
                pbk = bank()
                proj(pbk, 3)
                _act(k, sqk[:], pbk[:], AF.Square, r=[pbk], w=[sqk])
                pb = bank()
                proj(pb, 0)
                _act(k, xraw[:, 3:515], pb[:], AF.Copy, r=[pb], w=[xraw])
                pb = bank()
                proj(pb, 1)
                _act(k, gl[:], pb[:], AF.Gelu_apprx_tanh, r=[pb], w=[gl])
                cw = M_CW + cc * 4
                _ts(k, "dve", xc[:], xraw[:, 0:512], smp[:, cw:cw + 1], smp[:, M_CB + cc:M_CB + cc + 1],
                    ALU.mult, ALU.add, r=[xraw, smp], w=[xc])
                for tap in range(1, 4):
                    _stt(k, xc[:], xraw[:, tap:tap + 512], smp[:, cw + tap:cw + tap + 1], xc[:], ALU.mult, ALU.add,
                         r=[xraw, xc, smp], w=[xc])
                _copy(k, "pool", xcb[:], xc[:], r=[xc], w=[xcb])
                _copy(k, "pool", xraw[:, 0:3], xraw[:, 512:515], r=[xraw], w=[xraw])
                for (pbx, sqx, rx, gsc, dst) in ((pbq, sqq, rq, gqs[:, 0:1], qT),
                                                 (pbk, sqk, rk, smp[:, M_KG:M_KG + 1], kT)):
                    pbs = bank()
                    _mm(k, pbs[:], ones[:], sqx[:], True, True, r=[ones, sqx], w=[pbs])
                    _act(k, rx[:], pbs[:], AF.Sqrt, r=[pbs], w=[rx], scale=1.0 / 128.0, bias=EPS)
                    k.op("dve", lambda e, rx=rx: e.reciprocal(out=rx[:], in_=rx[:]), r=[rx], w=[rx])
                    _stt(k, dst[:, tsl], pbx[:], gsc, rx[:], ALU.mult, ALU.mult, r=[pbx, rx, gqs, smp], w=[dst])
                pb = bank()
                proj(pb, 5)
                _act(k, sga[:], pb[:], AF.Sigmoid, r=[pb], w=[sga])
                pb = bank()
                proj(pb, 6)
                _act(k, sgb[:, tsl], pb[:], AF.Sigmoid, r=[pb], w=[sgb])
                pbv = bank()
                for sub in range(4):
                    for kc in range(8):
                        _mm(k, pbv[:, sub * 128:(sub + 1) * 128], h[:, kc, sub * 128:(sub + 1) * 128],
                            W[:, kc, 4 * 128:5 * 128], kc == 0, kc == 7, r=[W, h], w=[pbv])
                _copy(k, "dve", vA[:, tt * 512:(tt + 1) * 512], pbv[:], r=[pbv], w=[vA])
                if tt + 1 < NT:
                    load_h(tt + 1)
                pbr = bank()
                _mm(k, pbr[:], wa[:, cc, :], xcb[:], True, True, r=[wa, xcb], w=[pbr])
                pbi = bank()
                _mm(k, pbi[:], wx[:, cc, :], xcb[:], True, True, r=[wx, xcb], w=[pbi])
                _act(k, r_t[:], pbr[:], AF.Sigmoid, r=[pbr, smp], w=[r_t], bias=smp[:, M_BA + cc:M_BA + cc + 1])
                _act(k, ig[:], pbi[:], AF.Sigmoid, r=[pbi, smp], w=[ig], bias=smp[:, M_BX + cc:M_BX + cc + 1])
                _act(k, a_t[:], r_t[:], AF.Exp, r=[r_t, negc], w=[a_t], scale=negc[:, cc:cc + 1])
                _act(k, a2[:], r_t[:], AF.Exp, r=[r_t, negc2], w=[a2], scale=negc2[:, cc:cc + 1])
                _act(k, a2[:], a2[:], AF.Sqrt, r=[a2], w=[a2], scale=-1.0, bias=1.0)
                _tt(k, "pool", u_t[:], ig[:], xc[:], ALU.mult, r=[ig, xc], w=[u_t])
                _tt(k, "pool", u_t[:], u_t[:], a2[:], ALU.mult, r=[u_t, a2], w=[u_t])
                k.op("dve", lambda e, hs=hs, a_t=a_t, u_t=u_t: e.tensor_tensor_scan(
                    out=hs[:], data0=a_t[:], data1=u_t[:], initial=state[:, 0:1], op0=ALU.mult, op1=ALU.add),
                     r=[a_t, u_t, state], w=[hs])
                _copy(k, "dve", state[:], hs[:, 511:512], r=[hs], w=[state])
                _tt(k, "pool", ya[:], hs[:], gl[:], ALU.mult, r=[hs, gl], w=[ya])
                _tt(k, "pool", mixA[:, tsl], ya[:], sga[:], ALU.mult, r=[ya, sga], w=[mixA])

            steps = []
            for qt in range(NT):
                topkb = 4 * qt + 3
                for kb in range(topkb, -1, -1):
                    steps.append(dict(i=len(steps), qt=qt, kb=kb, first=(kb == topkb), last=(kb == 0),
                                      diag=(kb - 4 * qt) if kb >= 4 * qt else None))
            ZP = [P[0], P[1]]
            ACC = [P[2], P[3]]
            OB = [P[4], P[5], P[6]]

            def stA(s):
                zp = ZP[s["i"] % 2]
                kb, qt = s["kb"], s["qt"]
                _mm(k, zp[:], kT[:, kb * 128:(kb + 1) * 128], qT[:, qt * 512:(qt + 1) * 512], True, True,
                    r=[kT, qT], w=[zp])

            def stB(s):
                zp = ZP[s["i"] % 2]
                e_ = EB[s["i"] % 4]
                L = LB[s["i"] % 4]
                _act(k, e_[:], zp[:], AF.Exp, r=[zp], w=[e_])
                if s["diag"] is not None:
                    dg = s["diag"]
                    _tt(k, "dve", e_[:], e_[:], masks[:, dg * 512:(dg + 1) * 512], ALU.mult, r=[e_, masks], w=[e_])
                _act(k, L[:], e_[:], AF.Ln, r=[e_], w=[L], bias=1.0)

            def stC(s):
                acc = ACC[s["qt"] % 2]
                L = LB[s["i"] % 4]
                _mm(k, acc[:], tri[:], L[:], s["first"], s["last"], r=[tri, L], w=[acc])

            def stD(s):
                acc = ACC[s["qt"] % 2]
                X = XB[s["i"] % 3]
                _act(k, X[:], acc[:], AF.Exp, r=[acc], w=[X], scale=-1.0)

            def stE(s):
                if s["last"]:
                    return
                acc = ACC[s["qt"] % 2]
                L = LB[s["i"] % 4]
                _mm(k, acc[:], trin[:], L[:], False, False, r=[trin, L], w=[acc])

            def stF(s):
                at = ATB[s["i"] % 4]
                _tt(k, "dve", at[:], EB[s["i"] % 4][:], XB[s["i"] % 3][:], ALU.mult,
                    r=[EB[s["i"] % 4], XB[s["i"] % 3]], w=[at])

            def stG(s):
                o = OB[s["qt"] % 3]
                at = ATB[s["i"] % 4]
                kb, qt = s["kb"], s["qt"]
                _mm(k, o[:], vA[:, kb * 128:(kb + 1) * 128], at[:], s["first"], s["last"], r=[vA, at], w=[o])
                if s["last"]:
                    qsl = slice(qt * 512, (qt + 1) * 512)
                    j = mo_ctr[0] % 2
                    mo_ctr[0] += 1
                    _tt(k, "dve", tmpo[j][:], o[:], sgb[:, qsl], ALU.mult, r=[o, sgb], w=[tmpo[j]])
                    _tt(k, "pool", mixo[j][:], tmpo[j][:], mixA[:, qsl], ALU.add, r=[tmpo[j], mixA], w=[mixo[j]])
                    k.dma("sp", s_mo[j], mix_d[cc * 128:(cc + 1) * 128, qsl], mixo[j][:], r=[mixo[j]],
                          w=[k.dbuf((pfx + "mix", cc, qt))])

            if "dbg_q" in io and cc == ncc - 1:
                s_dbg = k.slot(f"{pfx}dbg")
                k.dma("sp", s_dbg, io["dbg_small"][:, 0:16], modt[:], r=[modt])
                k.dma("sp", s_dbg, io["dbg_small"][:, 16:24], A1[:], r=[A1])
                for nm, t in (("dbg_q", qT), ("dbg_k", kT), ("dbg_v", vA), ("dbg_mixA", mixA), ("dbg_sgb", sgb)):
                    k.dma("sp", s_dbg, io[nm], t[:], r=[t])
            n = len(steps)
            for it in range(n + 3):
                if it < n:
                    stA(steps[it])
                    stB(steps[it])
                if 0 <= it - 2 < n:
                    stE(steps[it - 2])
                if 0 <= it - 1 < n:
                    stC(steps[it - 1])
                    stD(steps[it - 1])
                if 0 <= it - 2 < n:
                    stF(steps[it - 2])
                if 0 <= it - 3 < n:
                    stG(steps[it - 3])
        k.barrier()
    top.close()


def build_mixer(S, dbg=False):
    nc = bass.Bass("TRN2", target_bir_lowering=False)
    io = {}

    def inp(name, shape, dt=F32):
        io[name] = nc.dram_tensor(name, list(shape), dt, kind="ExternalInput").ap()
    inp("x", [S, D])
    inp("ident", [128, 128])
    inp("tri", [128, 128])
    inp("trin", [128, 128])
    inp("ones", [128, 128])
    inp("masks", [128, 2048])
    inp("smallp", [128, M_NP])
    inp("rgwa", [128, 4, 128])
    inp("rgwx", [128, 4, 128])
    inp("adaw", [4, 128, 8, 512])
    inp("win", [4, 128, 8, 896])
    io["mixT"] = nc.dram_tensor("mixT", [512, S], F32, kind="ExternalOutput").ap()
    if dbg:
        io["dbg_hT"] = nc.dram_tensor("dbg_hT", [128, 8, S], BF16, kind="ExternalOutput").ap()
        io["dbg_small"] = nc.dram_tensor("dbg_small", [128, 24], F32, kind="ExternalOutput").ap()
        for nm in ("dbg_q", "dbg_k", "dbg_v", "dbg_mixA", "dbg_sgb"):
            io[nm] = nc.dram_tensor(nm, [128, S], BF16, kind="ExternalOutput").ap()
    with contextlib.ExitStack() as st:
        k = K(nc, st)
        emit_mixer(k, nc, S, io)
        k.finish()
    return nc


def _pk(v):
    v = np.asarray(v, np.float32)
    return np.ascontiguousarray(v.reshape(-1, 128).T)


def _consts():
    p = np.arange(128)[:, None]
    c = np.arange(128)[None, :]
    tri = (p >= c).astype(np.float32)
    trin = (p < c).astype(np.float32)
    cc = np.arange(512)[None, :]
    masks = np.concatenate([((128 * i + p) < cc).astype(np.float32) for i in range(4)], axis=1)
    return dict(ident=np.eye(128, dtype=np.float32), tri=tri, trin=trin, ones=np.ones((128, 128), np.float32),
                masks=np.ascontiguousarray(masks))


def mixer_weights(l, b, hh, inp):
    if hh is None:
        ch, h0, ncc = slice(0, 1024), 0, 8
    else:
        ch, h0, ncc = slice(hh * 512, (hh + 1) * 512), hh * 4, 4
    d = {}
    ada_w, ada_b = inp["ada_w"][l], inp["ada_b"][l]
    smallp = np.zeros((128, M_NP), np.float32)
    smallp[:, M_CV:M_CV + 8] = _pk(inp["c"][b])
    smallp[:, M_BSH:M_BSH + 8] = _pk(ada_b[0:1024])
    smallp[:, M_BSC:M_BSC + 8] = _pk(ada_b[1024:2048])
    smallp[:, M_G1:M_G1 + 8] = _pk(inp["norm1_g"][l])
    cw = inp["conv_w"][l][:, ch]
    for cc in range(ncc):
        smallp[:, M_CW + cc * 4:M_CW + cc * 4 + 4] = cw[:, cc * 128:(cc + 1) * 128].T
    smallp[:, M_CB:M_CB + ncc] = _pk(inp["conv_b"][l][ch])
    smallp[:, M_BA:M_BA + ncc] = _pk(inp["rg_ba"][l][ch])
    smallp[:, M_BX:M_BX + ncc] = _pk(inp["rg_bx"][l][ch])
    smallp[:, M_LAM:M_LAM + ncc] = _pk(inp["rg_lambda"][l][ch])
    smallp[:, M_QG] = inp["q_norm_g"][l]
    smallp[:, M_KG] = inp["k_norm_g"][l]
    d["smallp"] = smallp
    d["rgwa"] = np.ascontiguousarray(inp["rg_wa"][l][h0:h0 + ncc].transpose(1, 0, 2))
    d["rgwx"] = np.ascontiguousarray(inp["rg_wx"][l][h0:h0 + ncc].transpose(1, 0, 2))
    aw = ada_w[:, 0:2048].reshape(8, 128, 4, 512).transpose(2, 1, 0, 3)
    d["adaw"] = np.ascontiguousarray(aw)
    w_in = inp["w_in"][l]
    w7 = w_in.reshape(8, 128, 7, 8, 128)[:, :, :, h0:h0 + ncc, :]
    d["win"] = np.ascontiguousarray(w7.transpose(3, 1, 0, 2, 4).reshape(ncc, 128, 8, 896))
    return d


def mixer_inputs(l, b, hh, xb, inp):
    d = dict(_consts())
    d["x"] = np.ascontiguousarray(xb, dtype=np.float32)
    d.update(mixer_weights(l, b, hh, inp))
    return d


_PROGS = {}


def _prog(kind, n):
    key = (kind, n)
    if key not in _PROGS:
        _PROGS[key] = build_mixer(n) if kind == "mixer" else build_ffn(n)
    return _PROGS[key]


FUSED = False


def kernel(**inputs):
    inp = {k_: np.asarray(v, dtype=np.float32) for k_, v in inputs.items()}
    x = inp["x"]
    B, S, _ = x.shape
    depth = inp["w_in"].shape[0]
    cores = list(range(8))
    if FUSED:
        key = ("fused", S, depth)
        if key not in _PROGS:
            _PROGS[key] = build_fused(S, depth)
        per_b = [fused_inputs(b, inp, depth) for b in range(B)]
        res = run_bass_kernel_spmd(_PROGS[key], [per_b[c // 2] for c in cores], core_ids=cores)
        half = S // 2
        return np.stack([np.concatenate([np.asarray(res.results[2 * b]["out"])[:half],
                                         np.asarray(res.results[2 * b + 1]["out"])[half:]], axis=0)
                         for b in range(B)], axis=0).astype(np.float32)
    for l in range(depth):
        nc = _prog("mixer", S)
        in_maps = [mixer_inputs(l, c // 2, c % 2, x[c // 2], inp) for c in cores]
        res = run_bass_kernel_spmd(nc, in_maps, core_ids=cores)
        mixT = [np.concatenate([np.asarray(res.results[2 * b]["mixT"]), np.asarray(res.results[2 * b + 1]["mixT"])],
                               axis=0) for b in range(B)]
        nc = _prog("ffn", S // 2)
        in_maps = [ffn_inputs(l, c // 2, c % 2, x[c // 2], mixT[c // 2], inp) for c in cores]
        res = run_bass_kernel_spmd(nc, in_maps, core_ids=cores)
        x = np.stack([np.concatenate([np.asarray(res.results[2 * b]["xout"]), np.asarray(res.results[2 * b + 1]["xout"])],
                                     axis=0) for b in range(B)], axis=0).astype(np.float32)
    return x


F_CV, F_BSH, F_BSC, F_G2, F_CW, F_CB, F_FLAG, F_NP = 0, 8, 16, 24, 32, 176, 224, 225


def emit_ffn(k, nc, S2, io, pfx="f", P=None, pre=True, mix_bf16=False):
    NT = S2 // 512
    x_d, mix_d, out_d = io["xin"], io["mixin"], io["xout"]
    if P is None:
        P = [k.ps(f"{pfx}ps{i}", [128, 512], F32) for i in range(8)]
    top = contextlib.ExitStack()
    bank_ctr = [0]

    def bank():
        b = P[4 + bank_ctr[0] % 4]
        bank_ctr[0] += 1
        return b

    ident = k.sb(f"{pfx}ident", [128, 128], BF16, stack=top)
    smp = k.sb(f"{pfx}smp", [128, F_NP], F32, stack=top)
    wo = k.sb(f"{pfx}wo", [128, 8, 1024], BF16, stack=top)
    wd = k.sb(f"{pfx}wd", [128, 24, 1024], BF16, stack=top)
    gtB = k.sb(f"{pfx}gtB", [128, 2048], F32, stack=top)
    modt = k.sb(f"{pfx}modt", [128, 16], F32, stack=top)
    A2 = k.sb(f"{pfx}A2", [128, 8], F32, stack=top)
    halo = k.sb(f"{pfx}halo", [128, 48, 2], F32, stack=top)
    s_c = k.slot(f"{pfx}const")
    s_c2 = k.slot(f"{pfx}const2")
    k.dma("sp", s_c2, smp[:], io["smallp"], w=[smp])
    k.dma("pool", s_c, ident[:], io["ident"], w=[ident])
    k.dma("pool", s_c, wo[:], io["wout"], w=[wo])
    for q in range(4):
        k.dma("pool", s_c, wd[:, q * 6:(q + 1) * 6, :], io["fdown"][:, q * 6:(q + 1) * 6, :], w=[wd])

    with contextlib.ExitStack() as ps:
        adaw_t = [k.sb(f"{pfx}adaw{i}", [128, 8, 512], F32, stack=ps) for i in range(2)]
        s_aw = [k.slot(f"{pfx}adaw{i}") for i in range(2)]
        cB = k.sb(f"{pfx}cB", [128, 8, 128], F32, stack=ps)
        onesf = k.sb(f"{pfx}onesf", [128, 128], F32, stack=ps)
        gtb_t = k.sb(f"{pfx}gtb_t", [128, 2048], F32, stack=ps)
        k.dma("sp", s_c2, gtb_t[:], io["gtb"].partition_broadcast(128), w=[gtb_t])
        k.op("dve", lambda e: e.memset(onesf[:], 1.0), w=[onesf])
        for kc in range(8):
            _ts(k, "dve", cB[:, kc, :], onesf[:], smp[:, F_CV + kc:F_CV + kc + 1], None, ALU.mult, None,
                r=[onesf, smp], w=[cB])
        pm = bank()
        for piece in range(4):
            at = adaw_t[piece % 2]
            k.dma("sp", s_aw[piece % 2], at[:], io["adaw"][piece], w=[at])
            for jj in range(4):
                j = piece * 4 + jj
                for kc in range(8):
                    _mm(k, pm[:, j:j + 1], at[:, kc, jj * 128:(jj + 1) * 128], smp[:, F_CV + kc:F_CV + kc + 1],
                        kc == 0, kc == 7, r=[at, smp], w=[pm])
        _tt(k, "dve", modt[:], pm[:, 0:16], smp[:, F_BSH:F_BSH + 16], ALU.add, r=[pm, smp], w=[modt])
        _ts(k, "dve", A2[:], modt[:, 8:16], 1.0, None, ALU.add, None, r=[modt], w=[A2])
        _tt(k, "dve", A2[:], A2[:], smp[:, F_G2:F_G2 + 8], ALU.mult, r=[A2, smp], w=[A2])
        for piece in range(4, 8):
            at = adaw_t[piece % 2]
            k.dma("sp", s_aw[piece % 2], at[:], io["adaw"][piece], w=[at])
            pb = bank()
            for kc in range(8):
                _mm(k, pb[:], cB[:, kc, :], at[:, kc, :], kc == 0, kc == 7, r=[cB, at], w=[pb])
            c0 = (piece - 4) * 512
            _tt(k, "dve", gtB[:, c0:c0 + 512], pb[:], gtb_t[:, c0:c0 + 512], ALU.add, r=[pb, gtb_t], w=[gtB])
        for kc in range(8):
            _tt(k, "dve" if kc % 2 == 0 else "pool", wo[:, kc, :], wo[:, kc, :], gtB[:, 0:1024], ALU.mult,
                r=[wo, gtB], w=[wo])
        for j in range(24):
            _tt(k, "dve" if j % 2 == 0 else "pool", wd[:, j, :], wd[:, j, :], gtB[:, 1024:2048], ALU.mult,
                r=[wd, gtB], w=[wd])
        k.barrier()

    with contextlib.ExitStack() as ps:
        mx = k.sb(f"{pfx}mx", [128, 8, 512], BF16, stack=ps)
        s_mx = k.slot(f"{pfx}mx0")
        xt = [k.sb(f"{pfx}xt{i}", [128, 1024], F32, stack=ps) for i in range(2)]
        s_xt = [k.slot(f"{pfx}xt{i}") for i in range(2)]
        x1t = [[k.sb(f"{pfx}x1t{p_}{i}", [128, 1024], F32, stack=ps) for i in range(4)] for p_ in range(2)]
        s_xo = [[k.slot(f"{pfx}xo{p_}{i}") for i in range(4)] for p_ in range(2)]
        xh = [k.sb(f"{pfx}xh{i}", [128, 1024], BF16, stack=ps) for i in range(2)]
        ssq = [k.sb(f"{pfx}ssq{i}", [128, 1], F32, stack=ps) for i in range(2)]
        h2T = [k.sb(f"{pfx}h2T{i}", [128, 8, 512], BF16, stack=ps) for i in range(2)]
        Wj = [k.sb(f"{pfx}Wj{i}", [128, 8, 256], BF16, stack=ps) for i in range(3)]
        s_wj = [k.slot(f"{pfx}wj{i}") for i in range(3)]
        raw = [k.sb(f"{pfx}raw{i}", [128, 514], F32, stack=ps) for i in range(4)]
        tcv = [k.sb(f"{pfx}tcv{i}", [128, 512], F32, stack=ps) for i in range(4)]
        gg = [k.sb(f"{pfx}gg{i}", [128, 512], F32, stack=ps) for i in range(2)]
        actT = k.sb(f"{pfx}actT", [128, 24, 512], BF16, stack=ps)
        tmp = [k.sb(f"{pfx}tmp{i}", [128, 512], F32, stack=ps) for i in range(2)]
        ctr = dict(x=0, raw=0, t=0, g=0, tmp=0, tmp1=0)
        mixv = mix_d.rearrange("(kc p) t -> p kc t", p=128)
        tiles = ([(0, 128, True, 0)] if pre else []) + \
            [((128 if pre else 0) + tt * 512, 512, False, tt * 512) for tt in range(NT)]
        wst = dict(issued=0, total=24 * len(tiles))
        tb = [P[i] for i in range(4)]
        tbv = [b_[:].bitcast(BF16) for b_ in tb]

        def issue_w(upto):
            while wst["issued"] <= upto and wst["issued"] < wst["total"]:
                g = wst["issued"]
                if "fup_bf" in io:
                    k.dma("sp", s_wj[g % 3], Wj[g % 3][:], io["fup_bf"][g % 24], w=[Wj[g % 3]])
                else:
                    k.dma("pool", s_wj[g % 3], Wj[g % 3][:], io["fup"][g % 24], w=[Wj[g % 3]])
                wst["issued"] += 1

        def st1_load(ti):
            row0, ntok = tiles[ti][0], tiles[ti][1]
            k.dma("sp" if mix_bf16 else "pool", s_mx, mx[:, :, 0:ntok], mixv[:, :, row0:row0 + ntok], w=[mx])

        st1x = {}

        def st1_a0(ti, sub):
            row0 = tiles[ti][0]
            xi = ctr["x"] % 2
            ctr["x"] += 1
            xx = xt[xi]
            k.dma("sp", s_xt[xi], xx[:], x_d[row0 + sub * 128:row0 + (sub + 1) * 128, :], w=[xx])
            tps = []
            for half in range(2):
                hs_ = slice(half * 512, (half + 1) * 512)
                po = bank()
                for kc in range(8):
                    _mm(k, po[:], mx[:, kc, sub * 128:(sub + 1) * 128], wo[:, kc, hs_], kc == 0, kc == 7,
                        r=[mx, wo], w=[po])
                _tt(k, "dve", x1t[ti % 2][sub][:, hs_], po[:], xx[:, hs_], ALU.add, r=[po, xx],
                    w=[x1t[ti % 2][sub]])

        def st1_a1(ti, sub):
            pass

        def st1_a2(ti, sub):
            x1 = x1t[ti % 2][sub]
            sq = ssq[sub % 2]
            xht = xh[sub % 2]
            _act(k, xht[:], x1[:], AF.Square, r=[x1], w=[xht, sq], accum_out=sq[:])
            _act(k, sq[:], sq[:], AF.Sqrt, r=[sq], w=[sq], scale=1.0 / D, bias=EPS)

        def st1_a3(ti, sub):
            x1 = x1t[ti % 2][sub]
            sq = ssq[sub % 2]
            xht = xh[sub % 2]
            k.op("dve", lambda e, sq=sq: e.reciprocal(out=sq[:], in_=sq[:]), r=[sq], w=[sq])
            _ts(k, "dve", xht[:, 0:512], x1[:, 0:512], sq[:, 0:1], None, ALU.mult, None, r=[x1, sq], w=[xht])
            _ts(k, "pool", xht[:, 512:1024], x1[:, 512:1024], sq[:, 0:1], None, ALU.mult, None, r=[x1, sq], w=[xht])

        def st1_a(ti, sub):
            st1_a0(ti, sub)
            st1_a1(ti, sub)
            st1_a2(ti, sub)
            st1_a3(ti, sub)

        def st1_b(ti, sub):
            xht = xh[sub % 2]
            for fc in range(8):
                bi = fc // 2
                c0 = (fc % 2) * 512 + sub * 128
                k.op("pe", lambda e, o=tbv[bi][:, c0:c0 + 128], i_=xht[:, fc * 128:(fc + 1) * 128]:
                     e.transpose(o, i_, ident[:]), r=[xht, ident], w=[tb[bi]])

        def st1_fin(ti, fcs=range(8)):
            ntok = tiles[ti][1]
            h2 = h2T[ti % 2]
            for fc in fcs:
                bi = fc // 2
                c0 = (fc % 2) * 512
                _act(k, h2[:, fc, 0:ntok], tbv[bi][:, c0:c0 + ntok], AF.Identity, r=[tb[bi], A2, modt], w=[h2],
                     scale=A2[:, fc:fc + 1], bias=modt[:, fc:fc + 1])

        pend = []

        def up_chunk(ti, j):
            ntok, is_pre = tiles[ti][1], tiles[ti][2]
            h2 = h2T[ti % 2]
            g = ti * 24 + j
            issue_w(g + 2)
            wj = Wj[g % 3]
            res = []
            for br in range(2):
                cidx = br * 24 + j
                pb = bank()
                for kc in range(8):
                    _mm(k, pb[:, 0:ntok], wj[:, kc, br * 128:(br + 1) * 128], h2[:, kc, 0:ntok], kc == 0, kc == 7,
                        r=[wj, h2], w=[pb])
                if is_pre:
                    _ts(k, "dve", halo[:, cidx, :], pb[:, ntok - 2:ntok], smp[:, F_FLAG:F_FLAG + 1], None,
                        ALU.mult, None, r=[pb, smp], w=[halo])
                    continue
                rw = raw[ctr["raw"] % 4]
                ctr["raw"] += 1
                _act(k, rw[:, 2:2 + ntok], pb[:, 0:ntok], AF.Copy, r=[pb], w=[rw])
                _copy(k, "pool", rw[:, 0:2], halo[:, cidx, :], r=[halo], w=[rw])
                tc = tcv[ctr["t"] % 4]
                ctr["t"] += 1
                cw = F_CW + cidx * 3
                _act(k, tc[:], rw[:, 0:512], AF.Identity, r=[rw, smp], w=[tc], scale=smp[:, cw:cw + 1],
                     bias=smp[:, F_CB + cidx:F_CB + cidx + 1])
                for tap in (1, 2):
                    _stt(k, tc[:], rw[:, tap:tap + 512], smp[:, cw + tap:cw + tap + 1], tc[:], ALU.mult, ALU.add,
                         r=[rw, tc, smp], w=[tc])
                _copy(k, "pool", halo[:, cidx, :], rw[:, 512:514], r=[rw], w=[halo])
                res.append(tc)
            if is_pre:
                return
            pend.append((j, res))

        def up_tail():
            if not pend:
                return
            j, res = pend.pop(0)
            g_ = gg[ctr["g"] % 2]
            ctr["g"] += 1
            _act(k, g_[:], res[0][:], AF.Gelu_apprx_tanh, r=[res[0]], w=[g_])
            _tt(k, "dve" if j % 2 == 1 else "pool", actT[:, j, :], g_[:], res[1][:], ALU.mult, r=[g_, res[1]], w=[actT])

        def down(ti):
            ntok, orow0 = tiles[ti][1], tiles[ti][3]
            for sub in range(ntok // 128):
                x1 = x1t[ti % 2][sub]
                for half in range(2):
                    hs_ = slice(half * 512, (half + 1) * 512)
                    pd = bank()
                    for j in range(24):
                        _mm(k, pd[:], actT[:, j, sub * 128:(sub + 1) * 128], wd[:, j, hs_], j == 0, j == 23,
                            r=[actT, wd], w=[pd])
                    _tt(k, "dve", x1[:, hs_], pd[:], x1[:, hs_], ALU.add, r=[pd, x1], w=[x1])
                r0 = orow0 + sub * 128
                k.dma("sp", s_xo[ti % 2][sub], out_d[r0:r0 + 128, :], x1[:], r=[x1], w=[k.dbuf((pfx + "out", r0))])

        if not pre:
            k.op("dve", lambda e: e.memset(halo[:], 0.0), w=[halo])
        nt_all = len(tiles)
        st1_load(0)
        for sub in range(tiles[0][1] // 128):
            st1_a(0, sub)
            st1_b(0, sub)
        st1_fin(0)
        for ti in range(nt_all):
            issue_w(ti * 24 + 1)
            sched = {}
            if ti + 1 < nt_all:
                st1_load(ti + 1)
                for sub in range(tiles[ti + 1][1] // 128):
                    j0 = 5 * sub
                    for dj, fn_ in ((0, st1_a0), (2, st1_a2), (3, st1_a3), (6, st1_b)):
                        sched.setdefault(j0 + dj, []).append(lambda ti=ti, sub=sub, fn_=fn_: fn_(ti + 1, sub))
                sched.setdefault(22, []).append(lambda ti=ti: st1_fin(ti + 1, range(0, 4)))
                sched.setdefault(23, []).append(lambda ti=ti: st1_fin(ti + 1, range(4, 8)))
            for j in range(24):
                up_chunk(ti, j)
                if j >= 1:
                    up_tail()
                for f_ in sched.get(j, []):
                    f_()
            up_tail()
            issue_w((ti + 1) * 24 + 1)
            if not tiles[ti][2]:
                down(ti)
        k.barrier()
    top.close()


def build_ffn(S2):
    nc = bass.Bass("TRN2", target_bir_lowering=False)
    io = {}

    def inp(name, shape, dt=F32):
        io[name] = nc.dram_tensor(name, list(shape), dt, kind="ExternalInput").ap()
    inp("xin", [128 + S2, D])
    inp("mixin", [D, 128 + S2])
    inp("ident", [128, 128])
    inp("smallp", [128, F_NP])
    inp("gtb", [2048])
    inp("adaw", [8, 128, 8, 512])
    inp("wout", [128, 8, 1024])
    inp("fup", [24, 128, 8, 256])
    inp("fdown", [128, 24, 1024])
    io["xout"] = nc.dram_tensor("xout", [S2, D], F32, kind="ExternalOutput").ap()
    with contextlib.ExitStack() as st:
        k = K(nc, st)
        emit_ffn(k, nc, S2, io)
        k.finish()
    return nc


def ffn_weights(l, b, flag, inp):
    d = {}
    ada_w, ada_b = inp["ada_w"][l], inp["ada_b"][l]
    smallp = np.zeros((128, F_NP), np.float32)
    smallp[:, F_CV:F_CV + 8] = _pk(inp["c"][b])
    smallp[:, F_BSH:F_BSH + 8] = _pk(ada_b[3072:4096])
    smallp[:, F_BSC:F_BSC + 8] = _pk(ada_b[4096:5120])
    smallp[:, F_G2:F_G2 + 8] = _pk(inp["norm2_g"][l])
    cw = inp["ffn_conv_w"][l]
    smallp[:, F_CW:F_CW + 144] = cw.reshape(3, 48, 128).transpose(2, 1, 0).reshape(128, 144)
    smallp[:, F_CB:F_CB + 48] = _pk(inp["ffn_conv_b"][l])
    smallp[:, F_FLAG] = flag
    d["smallp"] = smallp
    d["gtb"] = np.ascontiguousarray(np.concatenate([ada_b[2048:3072], ada_b[5120:6144]]))
    cols = np.concatenate([np.arange(3072, 5120), np.arange(2048, 3072), np.arange(5120, 6144)])
    aw = ada_w[:, cols].reshape(8, 128, 8, 512).transpose(2, 1, 0, 3)
    d["adaw"] = np.ascontiguousarray(aw)
    d["wout"] = np.ascontiguousarray(inp["w_out"][l].reshape(8, 128, 1024).transpose(1, 0, 2))
    fu = inp["ffn_up"][l].reshape(8, 128, 2, 24, 128)
    d["fup"] = np.ascontiguousarray(fu.transpose(3, 1, 0, 2, 4).reshape(24, 128, 8, 256))
    d["fdown"] = np.ascontiguousarray(inp["ffn_down"][l].reshape(24, 128, 1024).transpose(1, 0, 2))
    return d


def ffn_inputs(l, b, th, xb, mixTb, inp):
    S = xb.shape[0]
    S2 = S // 2
    t0 = th * S2
    p0 = t0 - 128 if th > 0 else 0
    d = dict(ident=np.eye(128, dtype=np.float32))
    d["xin"] = np.ascontiguousarray(np.concatenate([xb[p0:p0 + 128], xb[t0:t0 + S2]], axis=0), dtype=np.float32)
    d["mixin"] = np.ascontiguousarray(np.concatenate([mixTb[:, p0:p0 + 128], mixTb[:, t0:t0 + S2]], axis=1),
                                      dtype=np.float32)
    d.update(ffn_weights(l, b, 1.0 if th > 0 else 0.0, inp))
    return d


_MW = ("smallp", "rgwa", "rgwx", "adaw", "win")
_FW = ("smallp", "gtb", "adaw", "wout", "fup", "fdown")


def build_fused(S, depth=2, skip=()):
    nc = bass.Bass("TRN2", target_bir_lowering=False)
    ext = {}

    def inp(name, shape, dt=F32):
        ext[name] = nc.dram_tensor(name, list(shape), dt, kind="ExternalInput").ap()
    inp("x", [S, D])
    for n_ in ("ident", "tri", "trin", "ones"):
        inp(n_, [128, 128])
    inp("masks", [128, 2048])
    for l in range(depth):
        inp(f"m{l}_smallp", [128, M_NP])
        inp(f"m{l}_rgwa", [128, 8, 128])
        inp(f"m{l}_rgwx", [128, 8, 128])
        inp(f"m{l}_adaw", [4, 128, 8, 512])
        inp(f"m{l}_win", [8, 128, 8, 896])
        inp(f"f{l}_smallp", [128, F_NP])
        inp(f"f{l}_gtb", [2048])
        inp(f"f{l}_adaw", [8, 128, 8, 512])
        inp(f"f{l}_wout", [128, 8, 1024])
        inp(f"f{l}_fup", [24, 128, 8, 256])
        inp(f"f{l}_fdown", [128, 24, 1024])
    out = nc.dram_tensor("out", [S, D], F32, kind="ExternalOutput").ap()
    mixT_d = nc.dram_tensor("mixT_d", [D, S], BF16).ap()
    xmid = [nc.dram_tensor(f"xmid{l}", [S, D], F32).ap() for l in range(depth - 1)]
    fupb = [nc.dram_tensor(f"fupb{l}", [24, 128, 8, 256], BF16).ap() for l in range(depth)]
    with contextlib.ExitStack() as st:
        k = K(nc, st)
        P = [k.ps(f"ps{i}", [128, 512], F32) for i in range(8)]
        for l in range(depth):
            x_in = ext["x"] if l == 0 else xmid[l - 1]
            x_out = out if l == depth - 1 else xmid[l]
            io = dict(x=x_in, mixT=mixT_d, ident=ext["ident"], tri=ext["tri"], trin=ext["trin"], ones=ext["ones"],
                      masks=ext["masks"])
            for n_ in _MW:
                io[n_] = ext[f"m{l}_{n_}"]
            s_cast = k.slot("fcast")
            for j in range(24):
                k.dma("pool", s_cast, fupb[l][j], ext[f"f{l}_fup"][j], w=[k.dbuf(("fupb", l, j))])
            if f"m{l}" not in skip:
                emit_mixer(k, nc, S, io, pfx=f"m{l}", P=P, ncc=8, mix_bf16=True)
            io = dict(xin=x_in, mixin=mixT_d, xout=x_out, ident=ext["ident"], fup_bf=fupb[l])
            for n_ in _FW:
                io[n_] = ext[f"f{l}_{n_}"]
            if f"f{l}" not in skip:
                emit_ffn(k, nc, S, io, pfx=f"f{l}", P=P, pre=False, mix_bf16=True)
        k.finish()
    return nc


def fused_inputs(b, inp, depth):
    d = dict(_consts())
    d["x"] = np.ascontiguousarray(inp["x"][b], dtype=np.float32)
    for l in range(depth):
        for n_, v in mixer_weights(l, b, None, inp).items():
            d[f"m{l}_{n_}"] = v
        for n_, v in ffn_weights(l, b, 0.0, inp).items():
            d[f"f{l}_{n_}"] = v
    return d
```
